# Optimizing a Trainium2 kernel written in Bass

```python
import math
import jax, jax.numpy as jnp
from jax import lax
import numpy as np

D_MODEL = 2048
BATCH = 2
SEQ = 16384
DEPTH = 2

HEAD_DIM = 128
ROPE_THETA = 10000.0
NORM_EPS = 1e-6
BLOCK_Q = 128
D_FF = 1536

A_HEADS = 6
A_PATTERNS = ((128, 1), (512, 4), (2048, 16))
B_HEADS = 4
B_Q_LORA = 384
B_KV_LORA = 256
B_NOPE = 64
B_ROPE = 32
B_V = 128
D_MIX_EVEN = A_HEADS * HEAD_DIM + B_HEADS * B_V
EVEN_SPLITS = (A_HEADS * HEAD_DIM, A_HEADS * HEAD_DIM, A_HEADS * HEAD_DIM, B_Q_LORA, B_KV_LORA, B_ROPE)
IN_EVEN = sum(EVEN_SPLITS)

C_WINDOWS = (2, 4, 8, 16)
C_GROUPS = len(C_WINDOWS)
C_GROUP_DIM = 128
C_WIDTH = C_GROUPS * C_GROUP_DIM
D_HEADS = 4
D_MIX_ODD = C_WIDTH + D_HEADS * HEAD_DIM
ODD_SPLITS = (C_WIDTH, D_HEADS * HEAD_DIM, D_HEADS * HEAD_DIM, D_HEADS * HEAD_DIM)
IN_ODD = sum(ODD_SPLITS)

N_EVEN = (DEPTH + 1) // 2
N_ODD = DEPTH // 2

kernel_name = "hybrid_dilated_mla_pool_stickbreak_macaron"


def rms_norm(x, g):
    x32 = x.astype(jnp.float32)
    y = x32 * lax.rsqrt(jnp.mean(x32 * x32, axis=-1, keepdims=True) + NORM_EPS)
    return (y * g.astype(jnp.float32)).astype(x.dtype)


def rope(x, pos):
    d = x.shape[-1]
    inv = ROPE_THETA ** (-jnp.arange(0, d, 2, dtype=jnp.float32) / d)
    ang = pos.astype(jnp.float32)[:, None, :, None] * inv
    cos, sin = jnp.cos(ang), jnp.sin(ang)
    x32 = x.astype(jnp.float32)
    x1, x2 = x32[..., : d // 2], x32[..., d // 2:]
    return jnp.concatenate([x1 * cos - x2 * sin, x2 * cos + x1 * sin], axis=-1).astype(x.dtype)


def swiglu(x, w_gate, w_up, w_down):
    return (jax.nn.silu(x @ w_gate) * (x @ w_up)) @ w_down


def split_heads(t, n_heads):
    b, s, _ = t.shape
    return t.reshape(b, s, n_heads, -1).transpose(0, 2, 1, 3)


def merge_heads(t):
    b, h, s, d = t.shape
    return t.transpose(0, 2, 1, 3).reshape(b, s, h * d)


def dilated_attention(q, k, v):
    b, h, s, d = q.shape
    dt = q.dtype
    q32 = q.astype(jnp.float32) * (d ** -0.5)
    k32, v32 = k.astype(jnp.float32), v.astype(jnp.float32)
    qi = jnp.arange(BLOCK_Q)
    ki = jnp.arange(2 * BLOCK_Q)
    outs, lses = [], []
    for w, dil in A_PATTERNS:
        n_back = w // dil
        assert n_back <= BLOCK_Q
        length = s // dil
        nb = -(-length // BLOCK_Q)
        lp = nb * BLOCK_Q

        def to_phase(t):
            t = t.reshape(b, h, length, dil, d).transpose(0, 1, 3, 2, 4)
            t = jnp.pad(t, ((0, 0), (0, 0), (0, 0), (0, lp - length), (0, 0)))
            return t.reshape(b, h, dil, nb, BLOCK_Q, d)

        def band(t):
            prev = jnp.pad(t, ((0, 0), (0, 0), (0, 0), (1, 0), (0, 0), (0, 0)))[:, :, :, :-1]
            return jnp.concatenate([prev, t], axis=-2)

        qp = to_phase(q32)
        kb, vb = band(to_phase(k32)), band(to_phase(v32))
        logits = jnp.einsum('bhrnqd,bhrnkd->bhrnqk', qp, kb)
        dist = BLOCK_Q + qi[:, None] - ki[None, :]
        valid = ((dist >= 0) & (dist <= n_back))[None]
        first = (jnp.arange(nb) == 0)[:, None, None] & (ki < BLOCK_Q)[None, None, :]
        valid = valid & ~first
        logits = jnp.where(valid, logits, -jnp.inf)
        m = jnp.max(logits, axis=-1, keepdims=True)
        p = jnp.exp(logits - m)
        den = jnp.sum(p, axis=-1)
        o = jnp.einsum('bhrnqk,bhrnkd->bhrnqd', p, vb) / den[..., None]
        lse = m[..., 0] + jnp.log(den)
        o = o.reshape(b, h, dil, lp, d)[:, :, :, :length].transpose(0, 1, 3, 2, 4).reshape(b, h, s, d)
        lse = lse.reshape(b, h, dil, lp)[:, :, :, :length].transpose(0, 1, 3, 2).reshape(b, h, s)
        outs.append(o)
        lses.append(lse)
    wts = jax.nn.softmax(jnp.stack(lses, axis=0), axis=0)
    return jnp.sum(wts[..., None] * jnp.stack(outs, axis=0), axis=0).astype(dt)


def causal_softmax_attention(q, k, v):
    b, h, s, dk = q.shape
    dt = v.dtype
    q32 = q.astype(jnp.float32) * (dk ** -0.5)
    k32, v32 = k.astype(jnp.float32), v.astype(jnp.float32)
    qi = jnp.arange(BLOCK_Q)
    outs = []
    for bi in range(s // BLOCK_Q):
        s0 = bi * BLOCK_Q
        n_keys = s0 + BLOCK_Q
        logits = jnp.einsum('bhqd,bhkd->bhqk', q32[:, :, s0:n_keys], k32[:, :, :n_keys])
        mask = jnp.arange(n_keys)[None, :] <= (s0 + qi)[:, None]
        p = jax.nn.softmax(jnp.where(mask, logits, -jnp.inf), axis=-1)
        outs.append(jnp.einsum('bhqk,bhkd->bhqd', p, v32[:, :, :n_keys]))
    return jnp.concatenate(outs, axis=2).astype(dt)


def stick_breaking_attention(q, k, v):
    b, h, s, d = q.shape
    dt = v.dtype
    q32 = q.astype(jnp.float32) * (d ** -0.5)
    k32, v32 = k.astype(jnp.float32), v.astype(jnp.float32)
    qi = jnp.arange(BLOCK_Q)
    kk = jnp.arange(BLOCK_Q)
    strict_after = (kk[None, :] > kk[:, None]).astype(jnp.float32)
    outs = []
    for bi in range(s // BLOCK_Q):
        s0 = bi * BLOCK_Q
        n_keys = s0 + BLOCK_Q
        nk = bi + 1
        z = jnp.einsum('bhqd,bhkd->bhqk', q32[:, :, s0:n_keys], k32[:, :, :n_keys])
        mask = jnp.arange(n_keys)[None, :] < (s0 + qi)[:, None]
        ls_neg = jax.nn.log_sigmoid(-z)
        log_keep = jnp.where(mask, ls_neg, 0.0).reshape(b, h, BLOCK_Q, nk, BLOCK_Q)
        within = jnp.einsum('bhqnk,jk->bhqnj', log_keep, strict_after)
        totals = jnp.sum(log_keep, axis=-1)
        later_blocks = lax.cumsum(totals, axis=3, reverse=True) - totals
        after = (within + later_blocks[..., None]).reshape(b, h, BLOCK_Q, n_keys)
        a = jnp.where(mask, jnp.exp(z + ls_neg + after), 0.0)
        outs.append(jnp.einsum('bhqk,bhkd->bhqd', a, v32[:, :, :n_keys]))
    return jnp.concatenate(outs, axis=2).astype(dt)


def pool_mixer(u, w_group, scale):
    b, s, c = u.shape
    u32 = u.astype(jnp.float32)
    cs = jnp.concatenate([jnp.zeros((b, 1, c), jnp.float32), lax.cumsum(u32, axis=1)], axis=1)
    t = jnp.arange(s)
    outs = []
    for g, w in enumerate(C_WINDOWS):
        sl = slice(g * C_GROUP_DIM, (g + 1) * C_GROUP_DIM)
        csg = cs[..., sl]
        lo = jnp.maximum(t + 1 - w, 0)
        total = csg[:, 1:] - csg[:, lo]
        count = jnp.minimum(t + 1, w).astype(jnp.float32)
        outs.append(total / count[None, :, None] - u32[..., sl])
    pooled = jnp.stack(outs, axis=2)
    mixed = jnp.einsum('bsgc,gce->bsge', pooled, w_group.astype(jnp.float32)).reshape(b, s, c)
    return (mixed * scale.astype(jnp.float32)).astype(u.dtype)


def even_mixer(h, positions, w_in, q_norm, w_q_up, kv_norm, w_kv_up, w_out):
    b, s, _ = h.shape
    proj = h @ w_in
    offs = [int(o) for o in np.cumsum(EVEN_SPLITS)[:-1]]
    qa, ka, va, c_q, c_kv, k_rope = jnp.split(proj, offs, axis=-1)
    out_a = dilated_attention(rope(split_heads(qa, A_HEADS), positions),
                              rope(split_heads(ka, A_HEADS), positions),
                              split_heads(va, A_HEADS))
    qb = split_heads(rms_norm(c_q, q_norm) @ w_q_up, B_HEADS)
    q_nope, q_pe = qb[..., :B_NOPE], qb[..., B_NOPE:]
    kv = split_heads(rms_norm(c_kv, kv_norm) @ w_kv_up, B_HEADS)
    k_nope, v_b = kv[..., :B_NOPE], kv[..., B_NOPE:]
    k_pe = rope(k_rope[:, None], positions)
    q_full = jnp.concatenate([q_nope, rope(q_pe, positions)], axis=-1)
    k_full = jnp.concatenate([k_nope, jnp.broadcast_to(k_pe, (b, B_HEADS, s, B_ROPE))], axis=-1)
    out_b = causal_softmax_attention(q_full, k_full, v_b)
    merged = merge_heads(jnp.concatenate([out_a, out_b], axis=1))
    return merged @ w_out


def odd_mixer(h, w_in, pool_w, pool_scale, w_out):
    proj = h @ w_in
    offs = [int(o) for o in np.cumsum(ODD_SPLITS)[:-1]]
    u, qd, kd, vd = jnp.split(proj, offs, axis=-1)
    out_c = pool_mixer(u, pool_w, pool_scale)
    out_d = merge_heads(stick_breaking_attention(split_heads(qd, D_HEADS), split_heads(kd, D_HEADS),
                                                 split_heads(vd, D_HEADS)))
    return jnp.concatenate([out_c, out_d], axis=-1) @ w_out


def setup_inputs(seed: int = 0) -> dict:
    key = jax.random.key(seed)
    ks = jax.random.split(key, 20)
    f32 = jnp.float32

    def nrm(k, shape, fan_in):
        return jax.random.normal(k, shape, f32) * (fan_in ** -0.5)

    def gain(k, shape):
        return 1.0 + 0.02 * jax.random.normal(k, shape, f32)

    x = jax.random.normal(ks[0], (BATCH, SEQ, D_MODEL), f32)
    offsets = jax.random.randint(ks[1], (BATCH, 1), 0, 4096, dtype=jnp.int32)
    positions = (jnp.arange(SEQ, dtype=jnp.int32)[None, :] + offsets).astype(jnp.int32)
    return {
        "x": x,
        "positions": positions,
        "norm_g": gain(ks[2], (DEPTH, 3, D_MODEL)),
        "ffn_w_gate": nrm(ks[3], (DEPTH, 2, D_MODEL, D_FF), D_MODEL),
        "ffn_w_up": nrm(ks[4], (DEPTH, 2, D_MODEL, D_FF), D_MODEL),
        "ffn_w_down": nrm(ks[5], (DEPTH, 2, D_FF, D_MODEL), D_FF),
        "even_w_in": nrm(ks[6], (N_EVEN, D_MODEL, IN_EVEN), D_MODEL),
        "even_q_norm": gain(ks[7], (N_EVEN, B_Q_LORA)),
        "even_w_q_up": nrm(ks[8], (N_EVEN, B_Q_LORA, B_HEADS * (B_NOPE + B_ROPE)), B_Q_LORA),
        "even_kv_norm": gain(ks[9], (N_EVEN, B_KV_LORA)),
        "even_w_kv_up": nrm(ks[10], (N_EVEN, B_KV_LORA, B_HEADS * (B_NOPE + B_V)), B_KV_LORA),
        "even_w_out": nrm(ks[11], (N_EVEN, D_MIX_EVEN, D_MODEL), D_MIX_EVEN),
        "odd_w_in": nrm(ks[12], (N_ODD, D_MODEL, IN_ODD), D_MODEL),
        "odd_pool_w": nrm(ks[13], (N_ODD, C_GROUPS, C_GROUP_DIM, C_GROUP_DIM), C_GROUP_DIM),
        "odd_pool_scale": gain(ks[14], (N_ODD, C_WIDTH)),
        "odd_w_out": nrm(ks[15], (N_ODD, D_MIX_ODD, D_MODEL), D_MIX_ODD),
        "final_norm": gain(ks[16], (D_MODEL,)),
    }


def reference(x, positions, norm_g, ffn_w_gate, ffn_w_up, ffn_w_down, even_w_in, even_q_norm, even_w_q_up,
              even_kv_norm, even_w_kv_up, even_w_out, odd_w_in, odd_pool_w, odd_pool_scale, odd_w_out,
              final_norm):
    h = x
    for i in range(DEPTH):
        h = h + 0.5 * swiglu(rms_norm(h, norm_g[i, 0]), ffn_w_gate[i, 0], ffn_w_up[i, 0], ffn_w_down[i, 0])
        hn = rms_norm(h, norm_g[i, 1])
        if i % 2 == 0:
            e = i // 2
            mix = even_mixer(hn, positions, even_w_in[e], even_q_norm[e], even_w_q_up[e], even_kv_norm[e],
                             even_w_kv_up[e], even_w_out[e])
        else:
            o = i // 2
            mix = odd_mixer(hn, odd_w_in[o], odd_pool_w[o], odd_pool_scale[o], odd_w_out[o])
        h = h + mix
        h = h + 0.5 * swiglu(rms_norm(h, norm_g[i, 2]), ffn_w_gate[i, 1], ffn_w_up[i, 1], ffn_w_down[i, 1])
    return rms_norm(h, final_norm)
```

```python
import numpy as np
import concourse.bass as bass
import concourse.mybir as mybir
from concourse.bass_utils import run_bass_kernel_spmd

F32 = mybir.dt.float32
BF16 = mybir.dt.bfloat16
I32 = mybir.dt.int32
AF = mybir.ActivationFunctionType
ALU = mybir.AluOpType
AX = mybir.AxisListType

SAME_ENGINE_SYNC = {"pe": False, "act": False, "dve": False, "pool": False, "sp": False}
EPOCH = 30000


class Dep:
    __slots__ = ("w", "r")

    def __init__(self):
        self.w = None
        self.r = {}


def deps(n):
    return [Dep() for _ in range(n)]


class K:
    def __init__(self, nc, stack, n_dma_sems=6):
        self.nc = nc
        self.stack = stack
        self.engs = {"pe": nc.tensor, "act": nc.scalar, "dve": nc.vector, "pool": nc.gpsimd, "sp": nc.sync}
        self.prog = {e: [] for e in self.engs}
        self.cnt = {e: 0 for e in self.engs}
        self.sem = {e: self._newsem(e) for e in self.engs}
        self.known = {e: {} for e in self.engs}
        self.dq = {}
        for q in ("sp", "pool", "act"):
            self.dq[q] = {"sems": [self._newsem("d%s%d" % (q, i)) for i in range(n_dma_sems)], "n": 0}
        self.allsems = {}
        self.ninst = 0
        self.pending = {}
        self.cuts = []

    def _newsem(self, name):
        self._semn = getattr(self, "_semn", 0) + 1
        return self.stack.enter_context(self.nc.semaphore("s_%s_%d" % (name, self._semn)))

    def _collect(self, eng, reads, writes, extra=(), same=None):
        waits = {}
        own = id(self.sem[eng])
        kn = self.known[eng]

        def need(ev):
            if ev is None:
                return
            sem, val = ev
            sid = id(sem)
            if sid == own and not (SAME_ENGINE_SYNC[eng] if same is None else same):
                return
            if kn.get(sid, 0) >= val:
                return
            if sid not in waits or waits[sid][1] < val:
                waits[sid] = (sem, val)

        for d in reads:
            need(d.w)
        for d in writes:
            need(d.w)
            for ev in d.r.values():
                need(ev)
        for ev in extra:
            need(ev)
        for sid, (sem, val) in waits.items():
            kn[sid] = val
        return list(waits.values())

    def op(self, eng, fn, reads=(), writes=(), inc=True):
        wl = self._collect(eng, reads, writes)
        if inc and self.cnt[eng] >= EPOCH and not self.pending.get(eng):
            self.sem[eng] = self._newsem(eng)
            self.cnt[eng] = 0
        if inc:
            self.cnt[eng] += 1
            my = (self.sem[eng], self.cnt[eng])
        else:
            my = (self.sem[eng], self.cnt[eng] + 1)
        self.allsems[id(my[0])] = (my[0], max(my[1], self.allsems.get(id(my[0]), (None, 0))[1])) if inc else self.allsems.get(id(my[0]), (my[0], 0))

        def emit(e, fn=fn, wl=wl, my=my, inc=inc):
            for sem, val in wl:
                e.wait_ge(sem, val)
            if inc:
                fn(e).then_inc(my[0], 1)
            else:
                fn(e)

        self.pending[eng] = not inc
        self.prog[eng].append(emit)
        self.ninst += 1
        sid = id(my[0])
        for d in reads:
            d.r[sid] = my
        for d in writes:
            d.w = my
            d.r = {}
        return my

    def dma(self, q, out_ap, in_ap, reads=(), writes=(), **kw):
        st = self.dq[q]
        n = st["n"]
        st["n"] += 1
        P = len(st["sems"])
        sem = st["sems"][n % P]
        val = 16 * (n // P + 1)
        extra = [(sem, val - 16)] if n >= P else []
        wl = self._collect(q, reads, writes, extra, same=True)
        my = (sem, val)
        self.allsems[id(sem)] = my

        def emit(e, wl=wl, my=my, out_ap=out_ap, in_ap=in_ap, kw=kw):
            for s, v in wl:
                e.wait_ge(s, v)
            e.dma_start(out=out_ap, in_=in_ap, **kw).then_inc(my[0], 16)

        self.prog[q].append(emit)
        self.ninst += 1
        sid = id(sem)
        for d in reads:
            d.r[sid] = my
        for d in writes:
            d.w = my
            d.r = {}
        return my

    def coll(self, kind, in_ap, out_ap, groups, reads=(), writes=()):
        q = "pool"
        st = self.dq[q]
        n = st["n"]
        st["n"] += 1
        P = len(st["sems"])
        sem = st["sems"][n % P]
        val = 16 * (n // P + 1)
        extra = [(sem, val - 16)] if n >= P else []
        wl = self._collect(q, reads, writes, extra, same=True)
        my = (sem, val)
        self.allsems[id(sem)] = my

        def emit(e, wl=wl, my=my):
            for s_, v in wl:
                e.wait_ge(s_, v)
            e.collective_compute(kind, ALU.bypass, replica_groups=groups, ins=[in_ap], outs=[out_ap]).then_inc(my[0], 16)

        self.prog[q].append(emit)
        self.ninst += 1
        sid = id(sem)
        for d in reads:
            d.r[sid] = my
        for d in writes:
            d.w = my
            d.r = {}
        return my

    def barrier(self):
        finals = [v for v in self.allsems.values() if v[1] > 0]
        for eng in self.engs:
            def emit(e, finals=finals):
                for sem, val in finals:
                    e.wait_ge(sem, val)
            self.prog[eng].append(emit)
            for sem, val in finals:
                if self.known[eng].get(id(sem), 0) < val:
                    self.known[eng][id(sem)] = val

    def cut(self):
        self.barrier()
        self.cuts.append({e: len(self.prog[e]) for e in self.engs})

    def finish(self):
        finals = [v for v in self.allsems.values() if v[1] > 0]

        def emit_final(e):
            for sem, val in finals:
                e.wait_ge(sem, val)

        self.prog["sp"].append(emit_final)
        nc = self.nc
        bounds = self.cuts + [{e: len(self.prog[e]) for e in self.engs}]
        prev = {e: 0 for e in self.engs}
        for bd in bounds:
            seg = {e: self.prog[e][prev[e]:bd[e]] for e in self.engs}
            prev = bd
            if not any(seg.values()):
                continue
            with nc.Block() as block:
                @block.tensor
                def _(e, seg=seg):
                    for f in seg["pe"]:
                        f(e)

                @block.scalar
                def _(e, seg=seg):
                    for f in seg["act"]:
                        f(e)

                @block.vector
                def _(e, seg=seg):
                    for f in seg["dve"]:
                        f(e)

                @block.gpsimd
                def _(e, seg=seg):
                    for f in seg["pool"]:
                        f(e)

                @block.sync
                def _(e, seg=seg):
                    for f in seg["sp"]:
                        f(e)


import math
from contextlib import ExitStack
import numpy as np

D = 2048
DFF = 1536
NT = 256
EPS = 1e-6
TWO_PI = 2 * math.pi
CW1 = 6.28125
CW2 = float(np.float32(TWO_PI - CW1))
CW3 = float(TWO_PI - CW1 - CW2)
MAGIC = 12582912.0


class Cx:
    def __init__(self, nc, st):
        self.nc = nc
        self.st = st
        self.k = K(nc, st)
        self._n = 0
        self.stk = [st]
        self.banks = []
        for i in range(8):
            t = st.enter_context(nc.psum_tensor("bank%d" % i, [128, 512], F32))
            self.banks.append((t, Dep()))
        self._b = 0
        self.ones = self.sb([128, 128], BF16)
        self.dones = Dep()
        self.k.op("dve", lambda e: e.memset(self.ones[:], 1.0), writes=[self.dones])
        self.one32 = self.sb([128, 1], F32)
        self.done32 = Dep()
        self.k.op("dve", lambda e: e.memset(self.one32[:], 1.0), writes=[self.done32])

    def push(self, st):
        self.stk.append(st)

    def pop(self):
        self.stk.pop()

    def sb(self, shape, dt):
        self._n += 1
        return self.stk[-1].enter_context(self.nc.sbuf_tensor("t%d" % self._n, list(shape), dt))

    def rot(self, shape, dt, n):
        return Rot([(self.sb(shape, dt), Dep()) for _ in range(n)])

    def din(self, name, shape, dt):
        return self.nc.dram_tensor(name, list(shape), dt, kind="ExternalInput").ap()

    def dout(self, name, shape, dt):
        return self.nc.dram_tensor(name, list(shape), dt, kind="ExternalOutput").ap()

    def dscr(self, name, shape, dt):
        return self.nc.dram_tensor(name, list(shape), dt, kind="Internal").ap()

    def bank(self):
        b = self.banks[self._b % 8]
        self._b += 1
        return b

    def chain(self, M, N, pairs, reads, skip=False):
        bank, dep = self.bank()
        out = bank[0:M, 0:N]
        n = len(pairs)
        for i, (l, r) in enumerate(pairs):
            self.k.op("pe", lambda e, l=l, r=r, i=i: e.matmul(out, l, r, start=(i == 0), stop=(i == n - 1)),
                      reads=reads, writes=[dep], inc=(i == n - 1))
        return out, dep


class Rot:
    def __init__(self, items):
        self.items = items
        self.i = 0

    def next(self):
        it = self.items[self.i % len(self.items)]
        self.i += 1
        return it


def load_w(cx, w_ap, kc_n, F, q="pool"):
    W = cx.sb([128, kc_n, F], BF16)
    dW = deps(kc_n)
    wv = w_ap.rearrange("(kc p) f -> p kc f", p=128)
    for kc in range(kc_n):
        cx.k.dma(q, W[:, kc, :], wv[:, kc, :], writes=[dW[kc]])
    return W, dW


def fold_g(cx, W, dW, gcol, dg, kc_n, eng="pool"):
    for kc in range(kc_n):
        cx.k.op(eng, lambda e, kc=kc: e.tensor_scalar(W[:, kc, :], W[:, kc, :], gcol[:, kc:kc + 1], None, ALU.mult),
                reads=[dg], writes=[dW[kc]])


def load_small(cx, ap, shape, dt=F32, q="sp"):
    t = cx.sb(shape, dt)
    d = Dep()
    cx.k.dma(q, t[:], ap, writes=[d])
    return t, d


def rstd_from_ss(cx, ss_ap, dss, out, dout, n_feat, tmp, dtmp):
    k = cx.k
    k.op("dve", lambda e: e.tensor_scalar(tmp, ss_ap, 1.0 / n_feat, EPS, ALU.mult, ALU.add), reads=[dss], writes=[dtmp])
    k.op("act", lambda e: e.activation(tmp, tmp, AF.Sqrt), reads=[dtmp], writes=[dtmp])
    k.op("dve", lambda e: e.reciprocal(out, tmp), reads=[dtmp], writes=[dout])


def load_x_stats(cx, srcv, t0, nt, X, dX, SQ, dSQ, kc_n, rstd, drstd, tmp, dtmp, n_feat, dsrc, srcb=None):
    k = cx.k
    if srcb is not None:
        k.dma("sp", X[:, :, 0:nt], srcb.rearrange("(kc p) t -> p kc t", p=128)[:, :, t0:t0 + nt], reads=[dsrc], writes=[dX])
    else:
        k.dma("pool", X[:, :, 0:nt], srcv[:, :, t0:t0 + nt], reads=[dsrc], writes=[dX])
    k.op("pool", lambda e: e.tensor_tensor(SQ[:, :, 0:nt], X[:, :, 0:nt], X[:, :, 0:nt], ALU.mult), reads=[dX], writes=[dSQ])
    ss, dss = cx.chain(128, nt, [(cx.ones[:], SQ[:, kc, 0:nt]) for kc in range(kc_n)], [dSQ, cx.dones])
    rstd_from_ss(cx, ss, dss, rstd, drstd, n_feat, tmp, dtmp)


def emit_ffn(cx, T, src, gcol_ap, wg, wu, wd, dst, dsrc=None, ddst=None, dst2=None, srcb=None, dstb=None):
    k = cx.k
    dsrc = dsrc or Dep()
    ddst = ddst or Dep()
    with ExitStack() as ps:
        cx.push(ps)
        gcol, dg = load_small(cx, gcol_ap, [128, 16])
        Wg, dWg = load_w(cx, wg, 16, DFF)
        Wu, dWu = load_w(cx, wu, 16, DFF)
        Wd, dWd = load_w(cx, wd, 12, D)
        fold_g(cx, Wg, dWg, gcol, dg, 16)
        fold_g(cx, Wu, dWu, gcol, dg, 16)
        srcv = src.rearrange("(kc p) t -> p kc t", p=128)
        Xr = cx.rot([128, 16, NT], BF16, 2)
        SQr = cx.rot([128, 16, NT], BF16, 1)
        Hr = cx.rot([128, 12, NT], BF16, 2)
        rs_r = cx.rot([128, NT], F32, 2)
        tmp_r = cx.rot([128, NT], F32, 2)
        g1r = cx.rot([128, NT], F32, 3)
        u1r = cx.rot([128, NT], F32, 3)
        xr_r = cx.rot([128, NT], F32, 4)
        o_r = cx.rot([128, NT], F32, 4)
        ob_r = cx.rot([128, NT], BF16, 4)
        for tt in range(T // NT):
            t0 = tt * NT
            X, dX = Xr.next()
            SQ, dSQ = SQr.next()
            H, dH = Hr.next()
            rstd, drs = rs_r.next()
            tmp, dtmp = tmp_r.next()
            load_x_stats(cx, srcv, t0, NT, X, dX, SQ, dSQ, 16, rstd[:], drs, tmp[:], dtmp, D, dsrc, srcb)
            for fc in range(12):
                fs = slice(fc * 128, (fc + 1) * 128)
                pg, dpg = cx.chain(128, NT, [(Wg[:, kc, fs], X[:, kc, :]) for kc in range(16)], dWg + [dX])
                pu, dpu = cx.chain(128, NT, [(Wu[:, kc, fs], X[:, kc, :]) for kc in range(16)], dWu + [dX])
                g1, dg1 = g1r.next()
                u1, du1 = u1r.next()
                k.op("dve", lambda e, g1=g1, pg=pg, rstd=rstd: e.tensor_tensor(g1[:], pg, rstd[:], ALU.mult), reads=[dpg, drs], writes=[dg1])
                k.op("act", lambda e, g1=g1: e.activation(g1[:], g1[:], AF.Silu), reads=[dg1], writes=[dg1])
                k.op("dve", lambda e, u1=u1, pu=pu, rstd=rstd: e.tensor_tensor(u1[:], pu, rstd[:], ALU.mult), reads=[dpu, drs], writes=[du1])
                k.op("pool", lambda e, H=H, fc=fc, g1=g1, u1=u1: e.tensor_tensor(H[:, fc, :], g1[:], u1[:], ALU.mult), reads=[dg1, du1], writes=[dH])
            for dc in range(16):
                ds_ = slice(dc * 128, (dc + 1) * 128)
                py, dpy = cx.chain(128, NT, [(Wd[:, fc, ds_], H[:, fc, :]) for fc in range(12)], dWd + [dH])
                xr, dxr = xr_r.next()
                o, do = o_r.next()
                k.dma("sp", xr[:], src[ds_, t0:t0 + NT], reads=[dsrc], writes=[dxr])
                k.op("dve", lambda e, o=o, py=py, xr=xr: e.scalar_tensor_tensor(o[:], py, 0.5, xr[:], ALU.mult, ALU.add), reads=[dpy, dxr], writes=[do])
                k.dma("act", dst[ds_, t0:t0 + NT], o[:], reads=[do], writes=[ddst])
                if dst2 is not None:
                    k.dma("act", dst2[ds_, t0:t0 + NT], o[:], reads=[do])
                if dstb is not None:
                    ob16, dob16 = ob_r.next()
                    k.op("act", lambda e, ob16=ob16, o=o: e.activation(ob16[:], o[:], AF.Copy), reads=[do], writes=[dob16])
                    k.dma("act", dstb[ds_, t0:t0 + NT], ob16[:], reads=[dob16], writes=[ddst])
        cx.pop()
    k.cut()


def rope_tables(cx, posf, dposf, inv, dinv, P, nt, rk, sin_t, dsin, cos_t, dcos, ang, dang, kk, dkk):
    k = cx.k
    k.op("dve", lambda e: e.tensor_scalar(ang, posf, inv, None, ALU.mult), reads=[dposf, dinv], writes=[dang])
    k.op("dve", lambda e: e.tensor_scalar(kk, ang, 1.0 / TWO_PI, MAGIC, ALU.mult, ALU.add), reads=[dang], writes=[dkk])
    k.op("dve", lambda e: e.tensor_scalar(kk, kk, -MAGIC, None, ALU.add), reads=[dkk], writes=[dkk])
    for cc in (CW1, CW2, CW3):
        k.op("dve", lambda e, cc=cc: e.scalar_tensor_tensor(ang, kk, -cc, ang, ALU.mult, ALU.add), reads=[dkk, dang], writes=[dang])
    k.op("dve", lambda e: e.tensor_scalar(ang, ang, math.pi, -math.pi, ALU.min, ALU.max), reads=[dang], writes=[dang])
    k.op("act", lambda e: e.activation(sin_t, ang, AF.Sin), reads=[dang], writes=[dsin])
    k.op("act", lambda e: e.activation(kk, ang, AF.Abs), reads=[dang], writes=[dkk])
    k.op("dve", lambda e: e.tensor_scalar(kk, kk, -1.0, math.pi / 2, ALU.mult, ALU.add), reads=[dkk], writes=[dkk])
    k.op("act", lambda e: e.activation(cos_t, kk, AF.Sin), reads=[dkk], writes=[dcos])


class Long:
    def __init__(self, cx, names):
        self.t = {n: (cx.sb([128, NT], F32), Dep()) for n in names}

    def __getitem__(self, n):
        return self.t[n]


def pos_tables(cx, A, t0, posr, L, invc, dinv, P, sname, cname):
    k = cx.k
    pi_, dpi = posr.next()
    posf, dposf = L["posf"]
    k.dma("sp", pi_[:], A["posrep"][:, t0:t0 + NT], writes=[dpi])
    k.op("dve", lambda e: e.tensor_copy(posf[:], pi_[:]), reads=[dpi], writes=[dposf])
    sn, dsn = L[sname]; cs, dcs = L[cname]; ang, dang = L["ang"]; kk, dkk = L["kk"]
    rope_tables(cx, posf[0:P, :], dposf, invc[0:P, 0:1], dinv, P, NT, None, sn[0:P, :], dsn, cs[0:P, :], dcs, ang[0:P, :], dang, kk[0:P, :], dkk)


def emit_inproj_even_A(cx, T, src, dsrc, A, srcb=None):
    k = cx.k
    with ExitStack() as ps:
        cx.push(ps)
        gcol, dg = load_small(cx, A["g1col"], [128, 16])
        invA, dinvA = load_small(cx, A["invA"], [128, 1])
        w_in = A["w_in"]
        Win, dWin = load_w(cx, w_in[:, 0:1536], 16, 1536)
        fold_g(cx, Win, dWin, gcol, dg, 16)
        Wrot = cx.sb([128, 16, 1536], BF16)
        dWrot = deps(16)
        wv5 = w_in[:, 0:1536].rearrange("(kc p) (h two i) -> p kc h two i", p=128, two=2, i=64)
        Wr5 = Wrot[:].rearrange("p kc (h two i) -> p kc h two i", two=2, i=64)
        for kc in range(16):
            k.dma("pool", Wr5[:, kc, :, 0, :], wv5[:, kc, :, 1, :], writes=[dWrot[kc]])
            k.dma("pool", Wr5[:, kc, :, 1, :], wv5[:, kc, :, 0, :], writes=[dWrot[kc]])
            k.op("pool", lambda e, kc=kc: e.tensor_scalar(Wr5[:, kc, :, 0, :], Wr5[:, kc, :, 0, :], gcol[:, kc:kc + 1], -1.0, ALU.mult, ALU.mult), reads=[dg], writes=[dWrot[kc]])
            k.op("pool", lambda e, kc=kc: e.tensor_scalar(Wr5[:, kc, :, 1, :], Wr5[:, kc, :, 1, :], gcol[:, kc:kc + 1], None, ALU.mult), reads=[dg], writes=[dWrot[kc]])
        srcv = src.rearrange("(kc p) t -> p kc t", p=128)
        Xr = cx.rot([128, 16, NT], BF16, 2)
        SQr = cx.rot([128, 16, NT], BF16, 1)
        L = Long(cx, ["rstd", "tmp", "posf", "sinA", "cosA", "ang", "kk", "cq", "sq"])
        f32r = cx.rot([128, NT], F32, 6)
        obr = cx.rot([128, NT], BF16, 6)
        posr = cx.rot([128, NT], I32, 2)
        SA = 128 ** -0.5
        for tt in range(T // NT):
            t0 = tt * NT
            X, dX = Xr.next()
            SQ, dSQ = SQr.next()
            rstd, drs = L["rstd"]; tmp, dtmp = L["tmp"]
            load_x_stats(cx, srcv, t0, NT, X, dX, SQ, dSQ, 16, rstd[:], drs, tmp[:], dtmp, D, dsrc, srcb)
            pos_tables(cx, A, t0, posr, L, invA, dinvA, 128, "sinA", "cosA")
            sinA, dsinA = L["sinA"]; cosA, dcosA = L["cosA"]; cq, dcq = L["cq"]; sq_, dsq_ = L["sq"]
            k.op("dve", lambda e: e.scalar_tensor_tensor(cq[:], cosA[:], SA, rstd[:], ALU.mult, ALU.mult), reads=[dcosA, drs], writes=[dcq])
            k.op("dve", lambda e: e.scalar_tensor_tensor(sq_[:], sinA[:], SA, rstd[:], ALU.mult, ALU.mult), reads=[dsinA, drs], writes=[dsq_])
            k.op("dve", lambda e: e.tensor_tensor(cosA[:], cosA[:], rstd[:], ALU.mult), reads=[drs], writes=[dcosA])
            k.op("dve", lambda e: e.tensor_tensor(sinA[:], sinA[:], rstd[:], ALU.mult), reads=[drs], writes=[dsinA])
            for hh in range(12):
                cs = slice(hh * 128, (hh + 1) * 128)
                p1, dp1 = cx.chain(128, NT, [(Win[:, kc, cs], X[:, kc, :]) for kc in range(16)], dWin + [dX])
                p2, dp2 = cx.chain(128, NT, [(Wrot[:, kc, cs], X[:, kc, :]) for kc in range(16)], dWrot + [dX])
                ct, dct, stb, dst_ = (cq, dcq, sq_, dsq_) if hh < 6 else (cosA, dcosA, sinA, dsinA)
                t1, dt1 = f32r.next(); t2, dt2 = f32r.next()
                k.op("dve", lambda e, t1=t1, p1=p1, ct=ct: e.tensor_tensor(t1[:], p1, ct[:], ALU.mult), reads=[dp1, dct], writes=[dt1])
                k.op("dve", lambda e, t2=t2, p2=p2, stb=stb: e.tensor_tensor(t2[:], p2, stb[:], ALU.mult), reads=[dp2, dst_], writes=[dt2])
                ob, dob = obr.next()
                k.op("pool", lambda e, ob=ob, t1=t1, t2=t2: e.tensor_tensor(ob[:], t1[:], t2[:], ALU.add), reads=[dt1, dt2], writes=[dob])
                dstT = A["qaT"] if hh < 6 else A["kaT"]
                k.dma("act", dstT[hh % 6, :, t0:t0 + NT], ob[:], reads=[dob])
        cx.pop()
    k.cut()


def rstd_cols(cx, rstd, drs, colr, n_sub):
    out = []
    for s in range(n_sub):
        pc, dpc = cx.chain(128, 1, [(rstd[0:1, s * 128:(s + 1) * 128], cx.one32[0:1, 0:1])], [drs, cx.done32])
        rc, drc = colr.next()
        cx.k.op("dve", lambda e, rc=rc, pc=pc: e.tensor_copy(rc[:], pc), reads=[dpc], writes=[drc])
        out.append((rc, drc))
    return out


def emit_inproj_even_B(cx, T, src, dsrc, A, srcb=None):
    k = cx.k
    with ExitStack() as ps:
        cx.push(ps)
        gcol, dg = load_small(cx, A["g1col"], [128, 16])
        qn, dqn = load_small(cx, A["qncol"], [128, 3])
        kvn, dkvn = load_small(cx, A["kvncol"], [128, 2])
        invB, dinvB = load_small(cx, A["invB"], [128, 1])
        w_in = A["w_in"]
        OFF = 1536
        Win, dWin = load_w(cx, w_in[:, OFF:2976], 16, 2976 - OFF)
        fold_g(cx, Win, dWin, gcol, dg, 16)
        Wkr = cx.sb([128, 16, 96], BF16)
        Wkrr = cx.sb([128, 16, 96], BF16)
        dWkr = Dep()
        k.op("dve", lambda e: e.memset(Wkr[:], 0.0), writes=[dWkr])
        k.op("dve", lambda e: e.memset(Wkrr[:], 0.0), writes=[dWkr])
        wkv = w_in[:, 2944:2976].rearrange("(kc p) f -> p kc f", p=128)
        k.dma("pool", Wkr[:, :, 64:96], wkv, writes=[dWkr])
        k.dma("pool", Wkrr[:, :, 64:80], wkv[:, :, 16:32], writes=[dWkr])
        k.dma("pool", Wkrr[:, :, 80:96], wkv[:, :, 0:16], writes=[dWkr])
        for kc in range(16):
            k.op("pool", lambda e, kc=kc: e.tensor_scalar(Wkr[:, kc, 64:96], Wkr[:, kc, 64:96], gcol[:, kc:kc + 1], None, ALU.mult), reads=[dg], writes=[dWkr])
            k.op("pool", lambda e, kc=kc: e.tensor_scalar(Wkrr[:, kc, 64:80], Wkrr[:, kc, 64:80], gcol[:, kc:kc + 1], -1.0, ALU.mult, ALU.mult), reads=[dg], writes=[dWkr])
            k.op("pool", lambda e, kc=kc: e.tensor_scalar(Wkrr[:, kc, 80:96], Wkrr[:, kc, 80:96], gcol[:, kc:kc + 1], None, ALU.mult), reads=[dg], writes=[dWkr])
        Wq, dWq = load_w(cx, A["w_q_up"], 3, 384)
        fold_g(cx, Wq, dWq, qn, dqn, 3)
        Wqr = cx.sb([128, 3, 384], BF16)
        dWqr = Dep()
        k.op("dve", lambda e: e.memset(Wqr[:], 0.0), writes=[dWqr])
        wq4 = A["w_q_up"].rearrange("(kc p) (h c) -> p kc h c", p=128, c=96)
        Wqr4 = Wqr[:].rearrange("p kc (h c) -> p kc h c", c=96)
        for kc in range(3):
            k.dma("pool", Wqr4[:, kc, :, 64:80], wq4[:, kc, :, 80:96], writes=[dWqr])
            k.dma("pool", Wqr4[:, kc, :, 80:96], wq4[:, kc, :, 64:80], writes=[dWqr])
            k.op("pool", lambda e, kc=kc: e.tensor_scalar(Wqr4[:, kc, :, 64:80], Wqr4[:, kc, :, 64:80], qn[:, kc:kc + 1], -1.0, ALU.mult, ALU.mult), reads=[dqn], writes=[dWqr])
            k.op("pool", lambda e, kc=kc: e.tensor_scalar(Wqr4[:, kc, :, 80:96], Wqr4[:, kc, :, 80:96], qn[:, kc:kc + 1], None, ALU.mult), reads=[dqn], writes=[dWqr])
        Wkv, dWkv = load_w(cx, A["w_kv_up"], 2, 768)
        fold_g(cx, Wkv, dWkv, kvn, dkvn, 2)
        Wkv4 = Wkv[:].rearrange("p kc (h c) -> p kc h c", c=192)

        srcv = src.rearrange("(kc p) t -> p kc t", p=128)
        Xr = cx.rot([128, 16, NT], BF16, 2)
        SQr = cx.rot([128, 16, NT], BF16, 1)
        L = Long(cx, ["rstd", "tmp", "posf", "sinB", "cosB", "ang", "kk", "rstd2", "tmp2", "cB2", "sB2", "rstd3", "tmp3", "cB1", "sB1"])
        f32r = cx.rot([128, NT], F32, 6)
        obr = cx.rot([128, NT], BF16, 6)
        otr = cx.rot([128, 512], BF16, 3)
        colr = cx.rot([128, 1], F32, 8)
        cqr = cx.rot([128, 3, NT], BF16, 2)
        ckr = cx.rot([128, 2, NT], BF16, 2)
        sq2r = cx.rot([128, 3, NT], BF16, 2)
        posr = cx.rot([128, NT], I32, 2)
        SB_ = 96 ** -0.5
        CV, CQ, CKV = 0, 768, 1152
        for tt in range(T // NT):
            t0 = tt * NT
            X, dX = Xr.next()
            SQ, dSQ = SQr.next()
            rstd, drs = L["rstd"]; tmp, dtmp = L["tmp"]
            load_x_stats(cx, srcv, t0, NT, X, dX, SQ, dSQ, 16, rstd[:], drs, tmp[:], dtmp, D, dsrc, srcb)
            pos_tables(cx, A, t0, posr, L, invB, dinvB, 96, "sinB", "cosB")
            sinB, dsinB = L["sinB"]; cosB, dcosB = L["cosB"]
            rcols = rstd_cols(cx, rstd, drs, colr, NT // 128)
            for s in range(NT // 128):
                ts_ = slice(s * 128, (s + 1) * 128)
                rc, drc = rcols[s]
                for (f0, f1) in ((0, 512), (512, 768)):
                    pv, dpv = cx.chain(128, f1 - f0, [(X[:, kc, ts_], Win[:, kc, CV + f0:CV + f1]) for kc in range(16)], dWin + [dX])
                    ot, dot = otr.next()
                    k.op("act", lambda e, ot=ot, pv=pv, rc=rc, w=f1 - f0: e.activation(ot[:, 0:w], pv, AF.Copy, scale=rc[:, 0:1]), reads=[dpv, drc], writes=[dot])
                    k.dma("act", A["va"][t0 + s * 128:t0 + (s + 1) * 128, f0:f1], ot[:, 0:f1 - f0], reads=[dot])
            cqb, dcqb = cqr.next()
            sq2, dsq2 = sq2r.next()
            for c in range(3):
                pcq, dpcq = cx.chain(128, NT, [(Win[:, kc, CQ + c * 128:CQ + (c + 1) * 128], X[:, kc, :]) for kc in range(16)], dWin + [dX])
                k.op("dve", lambda e, c=c, cqb=cqb, pcq=pcq: e.tensor_tensor(cqb[:, c, :], pcq, rstd[:], ALU.mult), reads=[dpcq, drs], writes=[dcqb])
            k.op("pool", lambda e, sq2=sq2, cqb=cqb: e.tensor_tensor(sq2[:], cqb[:], cqb[:], ALU.mult), reads=[dcqb], writes=[dsq2])
            ss2, dss2 = cx.chain(128, NT, [(cx.ones[:], sq2[:, c, :]) for c in range(3)], [dsq2, cx.dones])
            rstd2, drs2 = L["rstd2"]; tmp2, dtmp2 = L["tmp2"]
            rstd_from_ss(cx, ss2, dss2, rstd2[:], drs2, 384, tmp2[:], dtmp2)
            cB2, dcB2 = L["cB2"]; sB2, dsB2 = L["sB2"]
            k.op("dve", lambda e: e.scalar_tensor_tensor(cB2[0:96, :], cosB[0:96, :], SB_, rstd2[0:96, :], ALU.mult, ALU.mult), reads=[dcosB, drs2], writes=[dcB2])
            k.op("dve", lambda e: e.scalar_tensor_tensor(sB2[0:96, :], sinB[0:96, :], SB_, rstd2[0:96, :], ALU.mult, ALU.mult), reads=[dsinB, drs2], writes=[dsB2])
            for h in range(4):
                cs = slice(h * 96, (h + 1) * 96)
                p1, dp1 = cx.chain(96, NT, [(Wq[:, c, cs], cqb[:, c, :]) for c in range(3)], dWq + [dcqb])
                p2, dp2 = cx.chain(96, NT, [(Wqr[:, c, cs], cqb[:, c, :]) for c in range(3)], [dWqr, dcqb])
                t1, dt1 = f32r.next(); t2, dt2 = f32r.next()
                k.op("dve", lambda e, t1=t1, p1=p1: e.tensor_tensor(t1[0:96, :], p1, cB2[0:96, :], ALU.mult), reads=[dp1, dcB2], writes=[dt1])
                k.op("dve", lambda e, t2=t2, p2=p2: e.tensor_tensor(t2[0:96, :], p2, sB2[0:96, :], ALU.mult), reads=[dp2, dsB2], writes=[dt2])
                ob, dob = obr.next()
                k.op("pool", lambda e, ob=ob, t1=t1, t2=t2: e.tensor_tensor(ob[0:96, :], t1[0:96, :], t2[0:96, :], ALU.add), reads=[dt1, dt2], writes=[dob])
                k.dma("act", A["qbT"][h, :, t0:t0 + NT], ob[0:96, :], reads=[dob])
            ckb, dckb = ckr.next()
            sq3, dsq3 = sq2r.next()
            for c in range(2):
                pck, dpck = cx.chain(128, NT, [(Win[:, kc, CKV + c * 128:CKV + (c + 1) * 128], X[:, kc, :]) for kc in range(16)], dWin + [dX])
                k.op("dve", lambda e, c=c, ckb=ckb, pck=pck: e.tensor_tensor(ckb[:, c, :], pck, rstd[:], ALU.mult), reads=[dpck, drs], writes=[dckb])
            k.op("pool", lambda e, sq3=sq3, ckb=ckb: e.tensor_tensor(sq3[:, 0:2, :], ckb[:], ckb[:], ALU.mult), reads=[dckb], writes=[dsq3])
            ss3, dss3 = cx.chain(128, NT, [(cx.ones[:], sq3[:, c, :]) for c in range(2)], [dsq3, cx.dones])
            rstd3, drs3 = L["rstd3"]; tmp3, dtmp3 = L["tmp3"]
            rstd_from_ss(cx, ss3, dss3, rstd3[:], drs3, 256, tmp3[:], dtmp3)
            for h in range(4):
                pk, dpk = cx.chain(64, NT, [(Wkv4[:, c, h, 0:64], ckb[:, c, :]) for c in range(2)], dWkv + [dckb])
                ob, dob = obr.next()
                k.op("dve", lambda e, ob=ob, pk=pk: e.tensor_tensor(ob[0:64, :], pk, rstd3[0:64, :], ALU.mult), reads=[dpk, drs3], writes=[dob])
                k.dma("act", A["kbT"][h, 0:64, t0:t0 + NT], ob[0:64, :], reads=[dob])
            rcols3 = rstd_cols(cx, rstd3, drs3, colr, NT // 128)
            for s in range(NT // 128):
                ts_ = slice(s * 128, (s + 1) * 128)
                rc, drc = rcols3[s]
                pv, dpv = cx.chain(128, 512, [(ckb[:, c, ts_], Wkv4[:, c, :, 64:192]) for c in range(2)], dWkv + [dckb])
                ot, dot = otr.next()
                k.op("act", lambda e, ot=ot, pv=pv, rc=rc: e.activation(ot[:], pv, AF.Copy, scale=rc[:, 0:1]), reads=[dpv, drc], writes=[dot])
                k.dma("act", A["vb"][t0 + s * 128:t0 + (s + 1) * 128, :], ot[:], reads=[dot])
            p1, dp1 = cx.chain(96, NT, [(Wkr[:, kc, :], X[:, kc, :]) for kc in range(16)], [dWkr, dX])
            p2, dp2 = cx.chain(96, NT, [(Wkrr[:, kc, :], X[:, kc, :]) for kc in range(16)], [dWkr, dX])
            cB1, dcB1 = L["cB1"]; sB1, dsB1 = L["sB1"]
            k.op("dve", lambda e: e.tensor_tensor(cB1[64:96, :], cosB[64:96, :], rstd[64:96, :], ALU.mult), reads=[dcosB, drs], writes=[dcB1])
            k.op("dve", lambda e: e.tensor_tensor(sB1[64:96, :], sinB[64:96, :], rstd[64:96, :], ALU.mult), reads=[dsinB, drs], writes=[dsB1])
            t1, dt1 = f32r.next(); t2, dt2 = f32r.next()
            k.op("dve", lambda e, t1=t1, p1=p1: e.tensor_tensor(t1[64:96, :], p1[64:96, :], cB1[64:96, :], ALU.mult), reads=[dp1, dcB1], writes=[dt1])
            k.op("dve", lambda e, t2=t2, p2=p2: e.tensor_tensor(t2[64:96, :], p2[64:96, :], sB1[64:96, :], ALU.mult), reads=[dp2, dsB1], writes=[dt2])
            ob, dob = obr.next()
            k.op("pool", lambda e, ob=ob, t1=t1, t2=t2: e.tensor_tensor(ob[64:96, :], t1[64:96, :], t2[64:96, :], ALU.add), reads=[dt1, dt2], writes=[dob])
            for h in range(4):
                k.dma("act", A["kbT"][h, 64:96, t0:t0 + NT], ob[64:96, :], reads=[dob])
        cx.pop()
    k.cut()


def norm2_bcast(cx, XT, dXT, P, c0, n, sqr):
    sq, dsq = sqr.next()
    cx.k.op("pool", lambda e: e.tensor_tensor(sq[0:P, 0:n], XT[0:P, c0:c0 + n], XT[0:P, c0:c0 + n], ALU.mult), reads=[dXT], writes=[dsq])
    return cx.chain(128, n, [(cx.ones[0:P, :], sq[0:P, 0:n])], [dsq, cx.dones])


def kmax2(cx, KT, dKT, P, Lk, sqr, f32r, km, dkm):
    k = cx.k
    first = True
    for c0 in range(0, Lk, 512):
        n = min(512, Lk - c0)
        pn, dpn = norm2_bcast(cx, KT, dKT, P, c0, n, sqr)
        if first:
            k.op("dve", lambda e, pn=pn: e.reduce_max(km[:], pn, AX.X), reads=[dpn], writes=[dkm])
            first = False
        else:
            t, dt = f32r.next()
            k.op("dve", lambda e, t=t, pn=pn: e.reduce_max(t[:, 0:1], pn, AX.X), reads=[dpn], writes=[dt])
            k.op("dve", lambda e, t=t: e.tensor_tensor(km[:], km[:], t[:, 0:1], ALU.max), reads=[dt], writes=[dkm])


def negm_rows(cx, QT, dQT, P, Lq, km, dkm, sqr, f32r, out_row, dout):
    k = cx.k
    for c0 in range(0, Lq, 512):
        n = min(512, Lq - c0)
        pn, dpn = norm2_bcast(cx, QT, dQT, P, c0, n, sqr)
        t, dt = f32r.next()
        k.op("act", lambda e, t=t, pn=pn, n=n: e.activation(t[0:1, 0:n], pn[0:1, :], AF.Sqrt, scale=km[0:1, 0:1]), reads=[dpn, dkm], writes=[dt])
        k.op("dve", lambda e, t=t, c0=c0, n=n: e.tensor_scalar(out_row[0:1, c0:c0 + n], t[0:1, 0:n], -1.0, None, ALU.mult), reads=[dt], writes=[dout])


def emit_mla_attn(cx, S, A):
    k = cx.k
    NB = S // 128
    with ExitStack() as ps:
        cx.push(ps)
        QT = cx.sb([128, S], BF16); dQT = Dep()
        KT = cx.sb([128, S], BF16); dKT = Dep()
        V = cx.sb([128, NB, 128], BF16); dV = Dep()
        MK = cx.sb([128, 4, 512], BF16); dMK = Dep()
        k.dma("sp", QT[0:96, :], A["qT"], writes=[dQT])
        k.dma("sp", KT[0:96, :], A["kT"], writes=[dKT])
        k.dma("sp", KT[96:97, :], A["onesrow"][0:1, 0:S], writes=[dKT])
        k.dma("sp", V[:], A["v"].rearrange("(nb p) d -> p nb d", p=128), writes=[dV])
        k.dma("sp", MK[:], A["masks"].rearrange("a p q -> p a q"), writes=[dMK])
        sqr = cx.rot([128, 512], BF16, 2)
        f32r = cx.rot([128, 512], F32, 3)
        km = cx.sb([128, 1], F32); dkm = Dep()
        negm = cx.sb([1, S], BF16); dnegm = Dep()
        kmax2(cx, KT, dKT, 96, S, sqr, f32r, km, dkm)
        negm_rows(cx, QT, dQT, 96, S, km, dkm, sqr, f32r, negm, dnegm)
        k.dma("sp", QT[96:97, :], negm[0:1, :], reads=[dnegm], writes=[dQT])
        ptr = cx.rot([128, 512], BF16, 4)
        recr = cx.rot([128, 512], F32, 2)
        outr = cx.rot([128, 512], BF16, 2)
        blocks = [(i, kb) for i in range(S // 512) for kb in range(4 * i + 4)]
        sb_i = 0
        pend = []

        def qk(i, kb, n):
            bank, dep = cx.banks[n % 4]
            out = bank[:, 0:512]
            k.op("pe", lambda e: e.matmul(out, KT[0:97, kb * 128:(kb + 1) * 128], QT[0:97, i * 512:(i + 1) * 512], start=True, stop=True),
                 reads=[dKT, dQT], writes=[dep])
            return out, dep

        def av(i, kb, sT, dsT):
            pt, dpt = ptr.next()
            k.op("act", lambda e: e.activation(pt[:], sT, AF.Exp), reads=[dsT], writes=[dpt])
            if kb >= 4 * i:
                a = kb - 4 * i
                k.op("pool", lambda e: e.tensor_tensor(pt[:], pt[:], MK[:, a, :], ALU.mult), reads=[dMK], writes=[dpt])
            ob, dob = cx.banks[4 + (i % 2)]
            db, ddb = cx.banks[6 + (i % 2)]
            first, last = (kb == 0), (kb == 4 * i + 3)
            k.op("pe", lambda e: e.matmul(ob[:, 0:512], V[:, kb, :], pt[:], start=first, stop=last), reads=[dV, dpt], writes=[dob], inc=False)
            k.op("pe", lambda e: e.matmul(db[:, 0:512], cx.ones[:], pt[:], start=first, stop=last), reads=[cx.dones, dpt], writes=[ddb])
            if last:
                rec, drec = recr.next()
                o, do = outr.next()
                k.op("dve", lambda e: e.reciprocal(rec[:], db[:, 0:512]), reads=[ddb], writes=[drec])
                k.op("dve", lambda e: e.tensor_tensor(o[:], ob[:, 0:512], rec[:], ALU.mult), reads=[dob, drec], writes=[do])
                k.dma("sp", A["oT"][:, i * 512:(i + 1) * 512], o[:], reads=[do])

        LOOK = 2
        for n, (i, kb) in enumerate(blocks):
            sT, dsT = qk(i, kb, n)
            pend.append((i, kb, sT, dsT))
            if len(pend) > LOOK:
                av(*pend.pop(0))
        while pend:
            av(*pend.pop(0))
        cx.pop()
    k.cut()


def emit_dil_attn(cx, T, A):
    k = cx.k
    HALO = 2048
    Lk = T + HALO
    NBK = Lk // 128
    with ExitStack() as ps:
        cx.push(ps)
        MK = cx.sb([128, 256], BF16); dMK = Dep()
        k.dma("sp", MK[:], A["dmask"], writes=[dMK])
        BK = cx.sb([2, Lk], BF16); dBK = Dep()
        k.dma("sp", BK[:], A["kbias2"], writes=[dBK])
        QB = cx.sb([2, T], BF16); dQB = Dep()
        KTr = cx.rot([128, Lk], BF16, 2)
        QTr = cx.rot([128, T], BF16, 2)
        Vr = cx.rot([128, 3, NBK, 128], BF16, 2)
        ACCr = cx.rot([128, 2, T], F32, 1)
        sqr = cx.rot([128, 512], BF16, 2)
        f32r = cx.rot([128, 512], F32, 3)
        kmr = cx.rot([128, 1], F32, 2)
        ptr = cx.rot([128, 256], BF16, 4)
        recr = cx.rot([128, T], F32, 1)
        outr = cx.rot([128, T], BF16, 2)
        def head(h):
            KT, dKT = KTr.next(); QT, dQT = QTr.next(); V, dV = Vr.next(); ACC, dACC = ACCr.next()
            k.dma("sp", KT[:], A["kaT"][h], writes=[dKT])
            k.dma("sp", QT[:], A["qaT"][h], writes=[dQT])
            for p in range(3):
                k.dma("sp", V[:, p, :, :], A["vreg"][p, h], writes=[dV])
            km, dkm = kmr.next()
            kmax2(cx, KT, dKT, 128, Lk, sqr, f32r, km, dkm)
            k.dma("sp", QB[1:2, :], A["onesrow"][0:1, 0:T], writes=[dQB])
            negm_rows(cx, QT, dQT, 128, T, km, dkm, sqr, f32r, QB, dQB)
            units = []
            for p, dil in enumerate((1, 4, 16)):
                nb = Lk // (128 * dil)
                nbh = HALO // (128 * dil)
                for r in range(dil):
                    for n in range(nbh, nb):
                        units.append((p, dil, nb, nbh, r, n))
            pend = []
            cnt = [0]

            def qk(u):
                p, dil, nb, nbh, r, n = u
                bank, dep = cx.banks[cnt[0] % 4]
                cnt[0] += 1
                q0 = (n - nbh) * 128 * dil + r
                qs = slice(q0, q0 + 127 * dil + 1, dil)
                for half, nn in enumerate((n - 1, n)):
                    k0 = nn * 128 * dil + r
                    ks = slice(k0, k0 + 127 * dil + 1, dil)
                    out = bank[:, half * 128:(half + 1) * 128]
                    k.op("pe", lambda e, out=out, ks=ks: e.matmul(out, KT[:, ks], QT[:, qs], start=True, stop=False), reads=[dKT, dQT], writes=[dep], inc=False)
                    k.op("pe", lambda e, out=out, ks=ks: e.matmul(out, BK[0:2, ks], QB[0:2, qs], start=False, stop=True), reads=[dBK, dQB], writes=[dep], inc=(half == 1))
                return (u, bank, dep, qs)

            def av(u, bank, dep, qs):
                p, dil, nb, nbh, r, n = u
                pt, dpt = ptr.next()
                k.op("act", lambda e: e.activation(pt[:], bank[:, 0:256], AF.Exp), reads=[dep], writes=[dpt])
                k.op("pool", lambda e: e.tensor_tensor(pt[:], pt[:], MK[:], ALU.mult), reads=[dMK], writes=[dpt])
                ob, dob = cx.banks[4 + (cnt[0] % 4)]
                for half, nn in enumerate((n - 1, n)):
                    k.op("pe", lambda e, half=half, nn=nn: e.matmul(ob[:, 0:128], V[:, p, r * nb + nn, :], pt[:, half * 128:(half + 1) * 128], start=(half == 0), stop=(half == 1)),
                         reads=[dV, dpt], writes=[dob], inc=False)
                for half in range(2):
                    k.op("pe", lambda e, half=half: e.matmul(ob[:, 128:256], cx.ones[:], pt[:, half * 128:(half + 1) * 128], start=(half == 0), stop=(half == 1)),
                         reads=[cx.dones, dpt], writes=[dob], inc=(half == 1))
                src3 = ob[:, 0:256].rearrange("p (a q) -> p a q", a=2)
                if p == 0:
                    k.op("dve", lambda e: e.tensor_copy(ACC[:, :, qs], src3), reads=[dob], writes=[dACC])
                else:
                    k.op("dve", lambda e: e.tensor_tensor(ACC[:, :, qs], src3, ACC[:, :, qs], ALU.add), reads=[dob], writes=[dACC])

            for u in units:
                pend.append(qk(u))
                if len(pend) > 2:
                    av(*pend.pop(0))
            while pend:
                av(*pend.pop(0))
            rec, drec = recr.next()
            o, do = outr.next()
            k.op("dve", lambda e, rec=rec, ACC=ACC: e.reciprocal(rec[:], ACC[:, 1, :]), reads=[dACC], writes=[drec])
            k.op("dve", lambda e, o=o, rec=rec, ACC=ACC: e.tensor_tensor(o[:], ACC[:, 0, :], rec[:], ALU.mult), reads=[dACC, drec], writes=[do])
            k.dma("sp", A["oaT"][h], o[:], reads=[do])

        for h in range(6):
            head(h)
        cx.pop()
    k.cut()


def outproj_tail(cx, T, t0, KC, W, dW, M, dM, hin, dhin, hout, dhout, xr_r, o_r, houtb=None, ob_r=None):
    k = cx.k
    for dc in range(16):
        ds_ = slice(dc * 128, (dc + 1) * 128)
        py, dpy = cx.chain(128, NT, [(W[:, c, ds_], M[:, c, :]) for c in range(KC)], dW + [dM])
        xr, dxr = xr_r.next()
        o, do = o_r.next()
        k.dma("sp", xr[:], hin[ds_, t0:t0 + NT], reads=[dhin], writes=[dxr])
        k.op("dve", lambda e, o=o, py=py, xr=xr: e.tensor_tensor(o[:], py, xr[:], ALU.add), reads=[dpy, dxr], writes=[do])
        k.dma("act", hout[ds_, t0:t0 + NT], o[:], reads=[do], writes=[dhout])
        if houtb is not None:
            ob16, dob16 = ob_r.next()
            k.op("act", lambda e, ob16=ob16, o=o: e.activation(ob16[:], o[:], AF.Copy), reads=[do], writes=[dob16])
            k.dma("act", houtb[ds_, t0:t0 + NT], ob16[:], reads=[dob16], writes=[dhout])


def emit_outproj_even(cx, T, mT, w_out, hin, dhin, hout, dhout, houtb=None):
    k = cx.k
    KC = 10
    with ExitStack() as ps:
        cx.push(ps)
        W, dW = load_w(cx, w_out, KC, D)
        Mr = cx.rot([128, KC, NT], BF16, 2)
        xr_r = cx.rot([128, NT], F32, 4)
        o_r = cx.rot([128, NT], F32, 4)
        ob_r = cx.rot([128, NT], BF16, 4)
        mv = mT.rearrange("(c p) t -> p c t", p=128)
        for tt in range(T // NT):
            t0 = tt * NT
            M, dM = Mr.next()
            k.dma("sp", M[:], mv[:, :, t0:t0 + NT], writes=[dM])
            outproj_tail(cx, T, t0, KC, W, dW, M, dM, hin, dhin, hout, dhout, xr_r, o_r, houtb, ob_r)
        cx.pop()
    k.cut()


def emit_inproj_odd(cx, T, src, dsrc, A, srcb=None):
    k = cx.k
    with ExitStack() as ps:
        cx.push(ps)
        gcol, dg = load_small(cx, A["g1col"], [128, 16])
        Win, dWin = load_w(cx, A["w_in"], 16, 2048)
        fold_g(cx, Win, dWin, gcol, dg, 16)
        srcv = src.rearrange("(kc p) t -> p kc t", p=128)
        Xr = cx.rot([128, 16, NT], BF16, 2)
        SQr = cx.rot([128, 16, NT], BF16, 1)
        L = Long(cx, ["rstd", "tmp"])
        f32r = cx.rot([128, NT], F32, 4)
        obr = cx.rot([128, NT], BF16, 6)
        otr = cx.rot([128, 512], BF16, 3)
        colr = cx.rot([128, 1], F32, 8)
        SA = 128 ** -0.5
        for tt in range(T // NT):
            t0 = tt * NT
            X, dX = Xr.next()
            SQ, dSQ = SQr.next()
            rstd, drs = L["rstd"]; tmp, dtmp = L["tmp"]
            load_x_stats(cx, srcv, t0, NT, X, dX, SQ, dSQ, 16, rstd[:], drs, tmp[:], dtmp, D, dsrc, srcb)
            for c in range(12):
                cs = slice(c * 128, (c + 1) * 128)
                p1, dp1 = cx.chain(128, NT, [(Win[:, kc, cs], X[:, kc, :]) for kc in range(16)], dWin + [dX])
                if c < 4:
                    t1, dt1 = f32r.next()
                    k.op("dve", lambda e, t1=t1, p1=p1: e.tensor_tensor(t1[:], p1, rstd[:], ALU.mult), reads=[dp1, drs], writes=[dt1])
                    k.dma("act", A["uT"][cs, t0:t0 + NT], t1[:], reads=[dt1])
                else:
                    ob, dob = obr.next()
                    sc = SA if c < 8 else 1.0
                    k.op("dve", lambda e, ob=ob, p1=p1, sc=sc: e.scalar_tensor_tensor(ob[:], p1, sc, rstd[:], ALU.mult, ALU.mult), reads=[dp1, drs], writes=[dob])
                    dstT = A["qdT"] if c < 8 else A["kdT"]
                    k.dma("act", dstT[c % 4, :, t0:t0 + NT], ob[:], reads=[dob])
            rcols = rstd_cols(cx, rstd, drs, colr, NT // 128)
            for s in range(NT // 128):
                ts_ = slice(s * 128, (s + 1) * 128)
                rc, drc = rcols[s]
                pv, dpv = cx.chain(128, 512, [(X[:, kc, ts_], Win[:, kc, 1536:2048]) for kc in range(16)], dWin + [dX])
                ot, dot = otr.next()
                k.op("act", lambda e, ot=ot, pv=pv, rc=rc: e.activation(ot[:], pv, AF.Copy, scale=rc[:, 0:1]), reads=[dpv, drc], writes=[dot])
                k.dma("act", A["vd"][t0 + s * 128:t0 + (s + 1) * 128, :], ot[:], reads=[dot])
        cx.pop()
    k.cut()


def emit_outproj_odd(cx, T, A, hin, dhin, hout, dhout, houtb=None):
    k = cx.k
    KC = 8
    HL = 16
    with ExitStack() as ps:
        cx.push(ps)
        W, dW = load_w(cx, A["w_out"], KC, D)
        PW, dPW = load_w(cx, A["pool_w"], 1, 512)
        psc, dpsc = load_small(cx, A["pscol"], [128, 4])
        Mr = cx.rot([128, KC, NT], BF16, 2)
        Ur = cx.rot([128, 4, NT + HL], F32, 2)
        S1r = cx.rot([128, 4, NT + HL], F32, 1)
        S2r = cx.rot([128, 4, NT + HL], F32, 1)
        ICr = cx.rot([128, 4, NT], F32, 2)
        PBr = cx.rot([128, 4, NT], BF16, 2)
        xr_r = cx.rot([128, NT], F32, 4)
        o_r = cx.rot([128, NT], F32, 4)
        ob_r = cx.rot([128, NT], BF16, 4)
        uv = A["uTh"].rearrange("(g p) t -> p g t", p=128)
        ov = A["odT"].rearrange("(c p) t -> p c t", p=128)
        for tt in range(T // NT):
            t0 = tt * NT
            M, dM = Mr.next()
            U, dU = Ur.next(); S1, dS1 = S1r.next(); S2, dS2 = S2r.next(); IC, dIC = ICr.next(); PB, dPB = PBr.next()
            k.dma("sp", M[:, 4:8, :], ov[:, :, t0:t0 + NT], writes=[dM])
            k.dma("sp", U[:], uv[:, :, t0:t0 + NT + HL], writes=[dU])
            k.dma("sp", IC[:], A["invcnt"][:, :, t0:t0 + NT], writes=[dIC])
            W_ = NT + HL
            src_t, dsrc_t = U, dU
            cur, dcur = None, None
            bufs = [(S1, dS1), (S2, dS2)]
            sh = 1
            for lvl in range(4):
                dstt, ddst = bufs[lvl % 2]
                g0 = lvl
                a, da = (U, dU) if lvl == 0 else bufs[(lvl - 1) % 2]
                k.op("dve", lambda e, dstt=dstt, a=a, g0=g0, sh=sh: e.tensor_tensor(dstt[:, g0:4, sh:W_], a[:, g0:4, sh:W_], a[:, g0:4, 0:W_ - sh], ALU.add),
                     reads=[da], writes=[ddst])
                t1 = dstt
                k.op("pool", lambda e, t1=t1, lvl=lvl, IC=IC: e.tensor_tensor(t1[:, lvl, HL:W_], t1[:, lvl, HL:W_], IC[:, lvl, :], ALU.mult), reads=[dIC], writes=[ddst])
                k.op("pool", lambda e, t1=t1, lvl=lvl, PB=PB, U=U: e.tensor_tensor(PB[:, lvl, :], t1[:, lvl, HL:W_], U[:, lvl, HL:W_], ALU.subtract), reads=[ddst, dU], writes=[dPB])
                sh *= 2
            for g in range(4):
                pm, dpm = cx.chain(128, NT, [(PW[:, 0, g * 128:(g + 1) * 128], PB[:, g, :])], dPW + [dPB])
                k.op("act", lambda e, M=M, g=g, pm=pm: e.activation(M[:, g, :], pm, AF.Copy, scale=psc[:, g:g + 1]), reads=[dpm, dpsc], writes=[dM])
            outproj_tail(cx, T, t0, KC, W, dW, M, dM, hin, dhin, hout, dhout, xr_r, o_r, houtb, ob_r)
        cx.pop()
    k.cut()


def emit_final_norm(cx, T, hin, dhin, gcol_ap, out):
    k = cx.k
    with ExitStack() as ps:
        cx.push(ps)
        gcol, dg = load_small(cx, gcol_ap, [128, 16])
        hv = hin.rearrange("(kc p) t -> p kc t", p=128)
        ov = out.rearrange("(kc p) t -> p kc t", p=128)
        Xr = cx.rot([128, 16, NT], F32, 2)
        SQr = cx.rot([128, 16, NT], BF16, 1)
        Or = cx.rot([128, 16, NT], F32, 2)
        L = Long(cx, ["rstd", "tmp"])
        for tt in range(T // NT):
            t0 = tt * NT
            X, dX = Xr.next(); SQ, dSQ = SQr.next(); O, dO = Or.next()
            rstd, drs = L["rstd"]; tmp, dtmp = L["tmp"]
            k.dma("sp", X[:], hv[:, :, t0:t0 + NT], reads=[dhin], writes=[dX])
            k.op("pool", lambda e, SQ=SQ, X=X: e.tensor_tensor(SQ[:], X[:], X[:], ALU.mult), reads=[dX], writes=[dSQ])
            ss, dss = cx.chain(128, NT, [(cx.ones[:], SQ[:, kc, :]) for kc in range(16)], [dSQ, cx.dones])
            rstd_from_ss(cx, ss, dss, rstd[:], drs, D, tmp[:], dtmp)
            for kc in range(16):
                k.op("dve", lambda e, O=O, X=X, kc=kc: e.scalar_tensor_tensor(O[:, kc, :], X[:, kc, :], gcol[:, kc:kc + 1], rstd[:], ALU.mult, ALU.mult), reads=[dX, dg, drs], writes=[dO])
            k.dma("act", ov[:, :, t0:t0 + NT], O[:], reads=[dO])
        cx.pop()
    k.cut()


def emit_sb_attn(cx, S, A):
    k = cx.k
    NB = S // 128
    with ExitStack() as ps:
        cx.push(ps)
        QT = cx.sb([128, S], BF16); dQT = Dep()
        KT = cx.sb([128, S], BF16); dKT = Dep()
        V = cx.sb([128, NB, 128], BF16); dV = Dep()
        MK = cx.sb([128, 4, 512], BF16); dMK = Dep()
        TRI = cx.sb([128, 2, 128], BF16); dTRI = Dep()
        k.dma("sp", QT[:], A["qT"], writes=[dQT])
        k.dma("sp", KT[:], A["kT"], writes=[dKT])
        k.dma("sp", V[:], A["v"].rearrange("(nb p) d -> p nb d", p=128), writes=[dV])
        k.dma("sp", MK[:], A["masks"].rearrange("a p q -> p a q"), writes=[dMK])
        k.dma("sp", TRI[:], A["tri"].rearrange("a p q -> p a q"), writes=[dTRI])
        exr = cx.rot([128, 512], F32, 2)
        spr = cx.rot([128, 512], BF16, 5)
        zsr = cx.rot([128, 512], F32, 4)
        ebr = cx.rot([128, 512], F32, 2)
        ar = cx.rot([128, 512], BF16, 5)
        outr = cx.rot([128, 512], BF16, 2)
        blocks = [(i, kb) for i in range(S // 512) for kb in range(4 * i + 3, -1, -1)]
        N = len(blocks)
        st = [dict() for _ in range(N)]

        def sZ(n):
            i, kb = blocks[n]
            bank, dep = cx.banks[n % 3]
            z = bank[:, 0:512]
            k.op("pe", lambda e: e.matmul(z, KT[:, kb * 128:(kb + 1) * 128], QT[:, i * 512:(i + 1) * 512], start=True, stop=True), reads=[dKT, dQT], writes=[dep])
            st[n]["z"] = (z, dep)

        def sSP(n):
            i, kb = blocks[n]
            z, dz = st[n]["z"]
            ex, dex = exr.next(); sp, dsp = spr.next(); zs, dzs = zsr.next()
            k.op("dve", lambda e: e.tensor_copy(zs[:], z), reads=[dz], writes=[dzs])
            k.op("act", lambda e: e.activation(ex[:], zs[:], AF.Exp), reads=[dzs], writes=[dex])
            k.op("act", lambda e: e.activation(sp[:], ex[:], AF.Ln, bias=1.0), reads=[dex], writes=[dsp])
            if kb >= 4 * i:
                a = kb - 4 * i
                k.op("pool", lambda e: e.tensor_tensor(sp[:], sp[:], MK[:, a, :], ALU.mult), reads=[dMK], writes=[dsp])
            st[n]["sp"] = (sp, dsp); st[n]["zs"] = (zs, dzs)

        def sTI(n):
            i, kb = blocks[n]
            sp, dsp = st[n]["sp"]
            cb, dcb = cx.banks[3 + (i % 2)]
            k.op("pe", lambda e: e.matmul(cb[:, 0:512], TRI[:, 0, :], sp[:], start=(kb == 4 * i + 3), stop=False, skip_group_check=True), reads=[dTRI, dsp], writes=[dcb])

        def sE(n):
            i, kb = blocks[n]
            zs, dzs = st[n]["zs"]
            cb, dcb = cx.banks[3 + (i % 2)]
            eb, deb = ebr.next(); a_, da_ = ar.next()
            k.op("dve", lambda e: e.scalar_tensor_tensor(eb[:], cb[:, 0:512], -1.0, zs[:], ALU.mult, ALU.add), reads=[dcb, dzs], writes=[deb])
            k.op("act", lambda e: e.activation(a_[:], eb[:], AF.Exp), reads=[deb], writes=[da_])
            if kb >= 4 * i:
                a = kb - 4 * i
                k.op("pool", lambda e: e.tensor_tensor(a_[:], a_[:], MK[:, a, :], ALU.mult), reads=[dMK], writes=[da_])
            st[n]["a"] = (a_, da_)

        def sTR(n):
            i, kb = blocks[n]
            sp, dsp = st[n]["sp"]
            cb, dcb = cx.banks[3 + (i % 2)]
            k.op("pe", lambda e: e.matmul(cb[:, 0:512], TRI[:, 1, :], sp[:], start=False, stop=(kb == 0), skip_group_check=True), reads=[dTRI, dsp], writes=[dcb])

        def sAV(n):
            i, kb = blocks[n]
            a_, da_ = st[n]["a"]
            ob, dob = cx.banks[5 + (i % 2)]
            k.op("pe", lambda e: e.matmul(ob[:, 0:512], V[:, kb, :], a_[:], start=(kb == 4 * i + 3), stop=(kb == 0)), reads=[dV, da_], writes=[dob])
            if kb == 0:
                o, do = outr.next()
                k.op("dve", lambda e: e.tensor_copy(o[:], ob[:, 0:512]), reads=[dob], writes=[do])
                k.dma("sp", A["oT"][:, i * 512:(i + 1) * 512], o[:], reads=[do])
            st[n].clear()

        for t in range(N + 4):
            if t < N:
                sZ(t)
            if 0 <= t - 1 < N:
                sSP(t - 1)
            if 0 <= t - 3 < N:
                sTR(t - 3)
            if 0 <= t - 2 < N:
                sTI(t - 2)
                sE(t - 2)
            if 0 <= t - 4 < N:
                sAV(t - 4)
        cx.pop()
    k.cut()


import numpy as np
import ml_dtypes
BF = ml_dtypes.bfloat16

def mla_masks():
    kk = np.arange(128)[:, None]; q = np.arange(512)[None, :]
    return np.stack([((a * 128 + kk) <= q) for a in range(4)]).astype(BF)

def sb_masks():
    kk = np.arange(128)[:, None]; q = np.arange(512)[None, :]
    return np.stack([((a * 128 + kk) < q) for a in range(4)]).astype(BF)

def dil_mask():
    kk = np.arange(128)[:, None]; q = np.arange(128)[None, :]
    return np.concatenate([(kk >= q), (kk <= q)], axis=1).astype(BF)

def vreg_layout(v_h, Lk):
    out = []
    for dil in (1, 4, 16):
        nb = Lk // (128 * dil)
        t = v_h.reshape(6, nb, 128, dil, 128)
        t = t.transpose(0, 2, 3, 1, 4).reshape(6, 128, dil * nb, 128)
        out.append(t)
    return np.ascontiguousarray(np.stack(out))

def tri_mats():
    j = np.arange(128)[:, None]; s = np.arange(128)[None, :]
    return np.stack([(j >= s), (j < s)]).astype(BF)


def _colT(v, n):
    return np.ascontiguousarray(np.asarray(v, np.float32).reshape(n, 128).T)


def _inv_cols():
    invA = (10000.0 ** (-np.arange(0, 128, 2, dtype=np.float32) / 128)).astype(np.float32)
    invA = np.concatenate([invA, invA]).reshape(128, 1)
    inv16 = (10000.0 ** (-np.arange(0, 32, 2, dtype=np.float32) / 32)).astype(np.float32)
    invB = np.zeros((128, 1), np.float32)
    invB[64:80, 0] = inv16
    invB[80:96, 0] = inv16
    return invA, invB


def _ffn_ins(cx, tag):
    return (cx.din("g_" + tag, [128, 16], F32), cx.din("wg_" + tag, [D, DFF], F32), cx.din("wu_" + tag, [D, DFF], F32), cx.din("wd_" + tag, [DFF, D], F32))


def _ffn_vals(tag, g, wg, wu, wd):
    return {"g_" + tag: _colT(g, 16), "wg_" + tag: np.ascontiguousarray(wg), "wu_" + tag: np.ascontiguousarray(wu), "wd_" + tag: np.ascontiguousarray(wd)}


def build_L1(T):
    nc = bass.Bass("TRN2", target_bir_lowering=False)
    with ExitStack() as st:
        cx = Cx(nc, st)
        xT = cx.din("xT", [D, T], F32)
        f = _ffn_ins(cx, "a")
        h1o = cx.dout("h1T", [D, T], F32)
        h1T = cx.dscr("h1s", [D, T], F32)
        dh = Dep()
        h1b = cx.dscr("h1b", [D, T], BF16)
        emit_ffn(cx, T, xT, f[0], f[1], f[2], f[3], h1T, ddst=dh, dst2=h1o, dstb=h1b)
        A = {"g1col": cx.din("g1col", [128, 16], F32), "qncol": cx.din("qncol", [128, 3], F32), "kvncol": cx.din("kvncol", [128, 2], F32),
             "invA": cx.din("invA", [128, 1], F32), "invB": cx.din("invB", [128, 1], F32), "w_in": cx.din("w_in", [D, 2976], F32),
             "w_q_up": cx.din("w_q_up", [384, 384], F32), "w_kv_up": cx.din("w_kv_up", [256, 768], F32), "posrep": cx.din("posrep", [128, T], I32),
             "qaT": cx.dout("qaT", [6, 128, T], BF16), "kaT": cx.dout("kaT", [6, 128, T], BF16), "va": cx.dout("va", [T, 768], BF16),
             "qbT": cx.dout("qbT", [4, 96, T], BF16), "kbT": cx.dout("kbT", [4, 96, T], BF16), "vb": cx.dout("vb", [T, 512], BF16)}
        emit_inproj_even_A(cx, T, h1T, dh, A, srcb=h1b)
        emit_inproj_even_B(cx, T, h1T, dh, A, srcb=h1b)
        cx.k.finish()
    return nc


def build_L2(S, T):
    nc = bass.Bass("TRN2", target_bir_lowering=False)
    Lk = T + 2048
    with ExitStack() as st:
        cx = Cx(nc, st)
        A = {"qT": cx.din("qT", [96, S], BF16), "kT": cx.din("kT", [96, S], BF16), "v": cx.din("v", [S, 128], BF16),
             "onesrow": cx.din("onesrow", [1, S], BF16), "masks": cx.din("masks", [4, 128, 512], BF16), "oT": cx.dout("oT", [128, S], BF16)}
        emit_mla_attn(cx, S, A)
        B = {"qaT": cx.din("qaT", [6, 128, T], BF16), "kaT": cx.din("kaT", [6, 128, Lk], BF16), "vreg": cx.din("vreg", [3, 6, 128, Lk // 128, 128], BF16),
             "dmask": cx.din("dmask", [128, 256], BF16), "kbias2": cx.din("kbias2", [2, Lk], BF16), "onesrow": A["onesrow"], "oaT": cx.dout("oaT", [6, 128, T], BF16)}
        emit_dil_attn(cx, T, B)
        cx.k.finish()
    return nc


def build_L3(T):
    nc = bass.Bass("TRN2", target_bir_lowering=False)
    with ExitStack() as st:
        cx = Cx(nc, st)
        mT = cx.din("mT", [1280, T], BF16)
        w_out = cx.din("w_out", [1280, D], F32)
        h1T = cx.din("h1T", [D, T], F32)
        h2T = cx.dscr("h2T", [D, T], F32); d2 = Dep()
        h3T = cx.dscr("h3T", [D, T], F32); d3 = Dep()
        h4o = cx.dout("h4T", [D, T], F32)
        h4T = cx.dscr("h4s", [D, T], F32); d4 = Dep()
        h2b = cx.dscr("h2b", [D, T], BF16); h3b = cx.dscr("h3b", [D, T], BF16); h4b = cx.dscr("h4b", [D, T], BF16)
        emit_outproj_even(cx, T, mT, w_out, h1T, Dep(), h2T, d2, houtb=h2b)
        f = _ffn_ins(cx, "b")
        emit_ffn(cx, T, h2T, f[0], f[1], f[2], f[3], h3T, dsrc=d2, ddst=d3, srcb=h2b, dstb=h3b)
        f = _ffn_ins(cx, "c")
        emit_ffn(cx, T, h3T, f[0], f[1], f[2], f[3], h4T, dsrc=d3, ddst=d4, dst2=h4o, srcb=h3b, dstb=h4b)
        A = {"g1col": cx.din("g1col", [128, 16], F32), "w_in": cx.din("w_in", [D, 2048], F32),
             "uT": cx.dout("uT", [512, T], F32), "qdT": cx.dout("qdT", [4, 128, T], BF16), "kdT": cx.dout("kdT", [4, 128, T], BF16), "vd": cx.dout("vd", [T, 512], BF16)}
        emit_inproj_odd(cx, T, h4T, d4, A, srcb=h4b)
        cx.k.finish()
    return nc


def build_L4(S):
    nc = bass.Bass("TRN2", target_bir_lowering=False)
    with ExitStack() as st:
        cx = Cx(nc, st)
        A = {"qT": cx.din("qT", [128, S], BF16), "kT": cx.din("kT", [128, S], BF16), "v": cx.din("v", [S, 128], BF16),
             "masks": cx.din("masks", [4, 128, 512], BF16), "tri": cx.din("tri", [2, 128, 128], BF16), "oT": cx.dout("oT", [128, S], BF16)}
        emit_sb_attn(cx, S, A)
        cx.k.finish()
    return nc


def build_L5(T):
    nc = bass.Bass("TRN2", target_bir_lowering=False)
    with ExitStack() as st:
        cx = Cx(nc, st)
        A = {"w_out": cx.din("w_out", [1024, D], F32), "pool_w": cx.din("pool_w", [128, 512], F32), "pscol": cx.din("pscol", [128, 4], F32),
             "uTh": cx.din("uTh", [512, T + 16], F32), "odT": cx.din("odT", [512, T], BF16), "invcnt": cx.din("invcnt", [128, 4, T], F32)}
        h4T = cx.din("h4T", [D, T], F32)
        h5T = cx.dscr("h5T", [D, T], F32); d5 = Dep()
        h6T = cx.dscr("h6T", [D, T], F32); d6 = Dep()
        outT = cx.dout("outT", [D, T], F32)
        h5b = cx.dscr("h5b", [D, T], BF16)
        emit_outproj_odd(cx, T, A, h4T, Dep(), h5T, d5, houtb=h5b)
        f = _ffn_ins(cx, "d")
        emit_ffn(cx, T, h5T, f[0], f[1], f[2], f[3], h6T, dsrc=d5, ddst=d6, srcb=h5b)
        emit_final_norm(cx, T, h6T, d6, cx.din("gfin", [128, 16], F32), outT)
        cx.k.finish()
    return nc


def _run(nc, ims):
    res = run_bass_kernel_spmd(nc, ims, core_ids=list(range(8)))
    return res.results


TH = 4096


def kernel(x, positions, norm_g, ffn_w_gate, ffn_w_up, ffn_w_down, even_w_in, even_q_norm, even_w_q_up,
           even_kv_norm, even_w_kv_up, even_w_out, odd_w_in, odd_pool_w, odd_pool_scale, odd_w_out, final_norm):
    x = np.asarray(x)
    Bn, S, _ = x.shape
    T = S // 4
    Th = min(TH, T)
    NS = S // Th
    shards = [(b, jj) for b in range(Bn) for jj in range(NS)]
    rounds = [shards[i:i + 8] for i in range(0, len(shards), 8)]
    Lk = T + 2048
    positions = np.asarray(positions)
    norm_g = np.asarray(norm_g); wg = np.asarray(ffn_w_gate); wu = np.asarray(ffn_w_up); wd = np.asarray(ffn_w_down)
    invA, invB = _inv_cols()
    cores = [(c // 4, c % 4) for c in range(8)]

    def tok(jj):
        return slice(jj * Th, (jj + 1) * Th)

    nc1 = build_L1(Th)
    H1T = np.empty((Bn, D, S), np.float32)
    QAT = np.empty((Bn, 6, 128, S), BF); KAT = np.empty((Bn, 6, 128, S), BF); VA = np.empty((Bn, S, 768), BF)
    QBT = np.empty((Bn, 4, 96, S), BF); KBT = np.empty((Bn, 4, 96, S), BF); VB = np.empty((Bn, S, 512), BF)
    common1 = {}
    common1.update(_ffn_vals("a", norm_g[0, 0], wg[0, 0], wu[0, 0], wd[0, 0]))
    common1.update({"g1col": _colT(norm_g[0, 1], 16), "qncol": _colT(np.asarray(even_q_norm)[0], 3), "kvncol": _colT(np.asarray(even_kv_norm)[0], 2),
                    "invA": invA, "invB": invB, "w_in": np.ascontiguousarray(np.asarray(even_w_in)[0]), "w_q_up": np.ascontiguousarray(np.asarray(even_w_q_up)[0]),
                    "w_kv_up": np.ascontiguousarray(np.asarray(even_w_kv_up)[0])})
    for rd in rounds:
        ims = []
        for b, jj in rd:
            im = dict(common1)
            im["xT"] = np.ascontiguousarray(x[b, tok(jj)].T)
            im["posrep"] = np.ascontiguousarray(np.broadcast_to(positions[b, tok(jj)][None, :], (128, Th))).astype(np.int32)
            ims.append(im)
        r = _run(nc1, ims)
        for (b, jj), rr in zip(rd, r):
            H1T[b][:, tok(jj)] = np.asarray(rr["h1T"])
            QAT[b][:, :, tok(jj)] = np.asarray(rr["qaT"]); KAT[b][:, :, tok(jj)] = np.asarray(rr["kaT"]); VA[b][tok(jj)] = np.asarray(rr["va"])
            QBT[b][:, :, tok(jj)] = np.asarray(rr["qbT"]); KBT[b][:, :, tok(jj)] = np.asarray(rr["kbT"]); VB[b][tok(jj)] = np.asarray(rr["vb"])
        del r
    ones_row = np.ones((1, S), BF)
    mm = mla_masks(); dm = dil_mask()
    ims = []
    for c, (b, j) in enumerate(cores):
        h = j
        q0 = j * T
        lo = q0 - 2048
        a = max(lo, 0)
        kaT = np.zeros((6, 128, Lk), BF)
        kaT[:, :, a - lo:] = KAT[b][:, :, a:q0 + T]
        vh = np.zeros((6, Lk, 128), BF)
        vh[:, a - lo:] = VA[b][a:q0 + T].reshape(-1, 6, 128).transpose(1, 0, 2)
        kb2 = np.zeros((2, Lk), np.float32); kb2[0] = 1.0; kb2[1, :a - lo] = -30000.0
        ims.append({"qT": np.ascontiguousarray(QBT[b][h]), "kT": np.ascontiguousarray(KBT[b][h]),
                    "v": np.ascontiguousarray(VB[b][:, h * 128:(h + 1) * 128]), "onesrow": ones_row, "masks": mm,
                    "qaT": np.ascontiguousarray(QAT[b][:, :, q0:q0 + T]), "kaT": kaT, "vreg": vreg_layout(vh, Lk), "dmask": dm, "kbias2": kb2.astype(BF)})
    r2 = _run(build_L2(S, T), ims)
    MT = np.empty((Bn, 1280, S), BF)
    for c, (b, j) in enumerate(cores):
        MT[b][0:768, j * T:(j + 1) * T] = np.asarray(r2[c]["oaT"]).reshape(768, T)
        MT[b][768 + j * 128:768 + (j + 1) * 128, :] = np.asarray(r2[c]["oT"])
    del r2, ims, QAT, KAT, VA, QBT, KBT, VB
    nc3 = build_L3(Th)
    H4T = np.empty((Bn, D, S), np.float32); UT = np.empty((Bn, 512, S), np.float32)
    QDT = np.empty((Bn, 4, 128, S), BF); KDT = np.empty((Bn, 4, 128, S), BF); VD = np.empty((Bn, S, 512), BF)
    common3 = {"w_out": np.ascontiguousarray(np.asarray(even_w_out)[0])}
    common3.update(_ffn_vals("b", norm_g[0, 2], wg[0, 1], wu[0, 1], wd[0, 1]))
    common3.update(_ffn_vals("c", norm_g[1, 0], wg[1, 0], wu[1, 0], wd[1, 0]))
    common3.update({"g1col": _colT(norm_g[1, 1], 16), "w_in": np.ascontiguousarray(np.asarray(odd_w_in)[0])})
    for rd in rounds:
        ims = []
        for b, jj in rd:
            im = dict(common3)
            im["mT"] = np.ascontiguousarray(MT[b][:, tok(jj)])
            im["h1T"] = np.ascontiguousarray(H1T[b][:, tok(jj)])
            ims.append(im)
        r = _run(nc3, ims)
        for (b, jj), rr in zip(rd, r):
            H4T[b][:, tok(jj)] = np.asarray(rr["h4T"]); UT[b][:, tok(jj)] = np.asarray(rr["uT"])
            QDT[b][:, :, tok(jj)] = np.asarray(rr["qdT"]); KDT[b][:, :, tok(jj)] = np.asarray(rr["kdT"]); VD[b][tok(jj)] = np.asarray(rr["vd"])
        del r
    del H1T, MT
    sm = sb_masks(); tm = tri_mats()
    ims = []
    for c, (b, j) in enumerate(cores):
        h = j
        ims.append({"qT": np.ascontiguousarray(QDT[b][h]), "kT": np.ascontiguousarray(KDT[b][h]),
                    "v": np.ascontiguousarray(VD[b][:, h * 128:(h + 1) * 128]), "masks": sm, "tri": tm})
    r4 = _run(build_L4(S), ims)
    ODT = np.empty((Bn, 512, S), BF)
    for c, (b, j) in enumerate(cores):
        ODT[b][j * 128:(j + 1) * 128, :] = np.asarray(r4[c]["oT"])
    del r4, ims
    nc5 = build_L5(Th)
    pw = np.ascontiguousarray(np.asarray(odd_pool_w)[0].transpose(1, 0, 2).reshape(128, 512))
    psc = _colT(np.asarray(odd_pool_scale)[0], 4)
    common5 = {"w_out": np.ascontiguousarray(np.asarray(odd_w_out)[0]), "pool_w": pw, "pscol": psc, "gfin": _colT(final_norm, 16)}
    common5.update(_ffn_vals("d", norm_g[1, 2], wg[1, 1], wu[1, 1], wd[1, 1]))
    out = np.empty((Bn, S, D), np.float32)
    for rd in rounds:
        ims = []
        for b, jj in rd:
            im = dict(common5)
            uTh = np.zeros((512, Th + 16), np.float32)
            uTh[:, 16:] = UT[b][:, tok(jj)]
            if jj > 0:
                uTh[:, :16] = UT[b][:, jj * Th - 16:jj * Th]
            tg = np.arange(jj * Th, (jj + 1) * Th)
            ic = np.stack([1.0 / np.minimum(tg + 1, w) for w in (2, 4, 8, 16)]).astype(np.float32)
            im.update({"uTh": uTh, "odT": np.ascontiguousarray(ODT[b][:, tok(jj)]),
                       "invcnt": np.ascontiguousarray(np.broadcast_to(ic[None], (128, 4, Th))), "h4T": np.ascontiguousarray(H4T[b][:, tok(jj)])})
            ims.append(im)
        r = _run(nc5, ims)
        for (b, jj), rr in zip(rd, r):
            out[b, tok(jj)] = np.asarray(rr["outT"]).T
        del r
    return out
```

```python
import numpy as np
import concourse.bass as bass
import concourse.mybir as mybir
from concourse.bass_utils import run_bass_kernel_spmd

F32 = mybir.dt.float32
BF16 = mybir.dt.bfloat16
I32 = mybir.dt.int32
AF = mybir.ActivationFunctionType
ALU = mybir.AluOpType
AX = mybir.AxisListType

SAME_ENGINE_SYNC = {"pe": False, "act": False, "dve": False, "pool": False, "sp": False}
EPOCH = 30000


class Dep:
    __slots__ = ("w", "r")

    def __init__(self):
        self.w = None
        self.r = {}


def deps(n):
    return [Dep() for _ in range(n)]


class K:
    def __init__(self, nc, stack, n_dma_sems=16):
        self.nc = nc
        self.stack = stack
        self.engs = {"pe": nc.tensor, "act": nc.scalar, "dve": nc.vector, "pool": nc.gpsimd, "sp": nc.sync}
        self.prog = {e: [] for e in self.engs}
        self.cnt = {e: 0 for e in self.engs}
        self.sem = {e: self._newsem(e) for e in self.engs}
        self.known = {e: {} for e in self.engs}
        self.dq = {}
        for q in ("sp", "pool", "act"):
            self.dq[q] = {"sems": [self._newsem("d%s%d" % (q, i)) for i in range(n_dma_sems)], "n": 0}
        self.allsems = {}
        self.ninst = 0
        self.pending = {}
        self.cuts = []

    def _newsem(self, name):
        self._semn = getattr(self, "_semn", 0) + 1
        return self.stack.enter_context(self.nc.semaphore("s_%s_%d" % (name, self._semn)))

    def _collect(self, eng, reads, writes, extra=(), same=None):
        waits = {}
        own = id(self.sem[eng])
        kn = self.known[eng]

        def need(ev):
            if ev is None:
                return
            sem, val = ev
            sid = id(sem)
            if sid == own and not (SAME_ENGINE_SYNC[eng] if same is None else same):
                return
            if kn.get(sid, 0) >= val:
                return
            if sid not in waits or waits[sid][1] < val:
                waits[sid] = (sem, val)

        for d in reads:
            need(d.w)
        for d in writes:
            need(d.w)
            for ev in d.r.values():
                need(ev)
        for ev in extra:
            need(ev)
        for sid, (sem, val) in waits.items():
            kn[sid] = val
        return list(waits.values())

    def op(self, eng, fn, reads=(), writes=(), inc=True):
        wl = self._collect(eng, reads, writes)
        if inc and self.cnt[eng] >= EPOCH and not self.pending.get(eng):
            self.sem[eng] = self._newsem(eng)
            self.cnt[eng] = 0
        if inc:
            self.cnt[eng] += 1
            my = (self.sem[eng], self.cnt[eng])
        else:
            my = (self.sem[eng], self.cnt[eng] + 1)
        self.allsems[id(my[0])] = (my[0], max(my[1], self.allsems.get(id(my[0]), (None, 0))[1])) if inc else self.allsems.get(id(my[0]), (my[0], 0))

        def emit(e, fn=fn, wl=wl, my=my, inc=inc):
            for sem, val in wl:
                e.wait_ge(sem, val)
            if inc:
                fn(e).then_inc(my[0], 1)
            else:
                fn(e)

        self.pending[eng] = not inc
        self.prog[eng].append(emit)
        self.ninst += 1
        sid = id(my[0])
        for d in reads:
            d.r[sid] = my
        for d in writes:
            d.w = my
            d.r = {}
        return my

    def dma(self, q, out_ap, in_ap, reads=(), writes=(), **kw):
        st = self.dq[q]
        n = st["n"]
        st["n"] += 1
        P = len(st["sems"])
        sem = st["sems"][n % P]
        val = 16 * (n // P + 1)
        extra = [(sem, val - 16)] if n >= P else []
        wl = self._collect(q, reads, writes, extra, same=True)
        my = (sem, val)
        self.allsems[id(sem)] = my

        def emit(e, wl=wl, my=my, out_ap=out_ap, in_ap=in_ap, kw=kw):
            for s, v in wl:
                e.wait_ge(s, v)
            e.dma_start(out=out_ap, in_=in_ap, **kw).then_inc(my[0], 16)

        self.prog[q].append(emit)
        self.ninst += 1
        sid = id(sem)
        for d in reads:
            d.r[sid] = my
        for d in writes:
            d.w = my
            d.r = {}
        return my

    def coll(self, kind, in_ap, out_ap, groups, reads=(), writes=()):
        q = "pool"
        st = self.dq[q]
        n = st["n"]
        st["n"] += 1
        P = len(st["sems"])
        sem = st["sems"][n % P]
        val = 16 * (n // P + 1)
        extra = [(sem, val - 16)] if n >= P else []
        wl = self._collect(q, reads, writes, extra, same=True)
        my = (sem, val)
        self.allsems[id(sem)] = my

        def emit(e, wl=wl, my=my):
            for s_, v in wl:
                e.wait_ge(s_, v)
            e.collective_compute(kind, ALU.bypass, replica_groups=groups, ins=[in_ap], outs=[out_ap]).then_inc(my[0], 16)

        self.prog[q].append(emit)
        self.ninst += 1
        sid = id(sem)
        for d in reads:
            d.r[sid] = my
        for d in writes:
            d.w = my
            d.r = {}
        return my

    def barrier(self):
        finals = [v for v in self.allsems.values() if v[1] > 0]
        for eng in self.engs:
            def emit(e, finals=finals):
                for sem, val in finals:
                    e.wait_ge(sem, val)
            self.prog[eng].append(emit)
            for sem, val in finals:
                if self.known[eng].get(id(sem), 0) < val:
                    self.known[eng][id(sem)] = val

    def cut(self):
        self.barrier()
        self.cuts.append({e: len(self.prog[e]) for e in self.engs})

    def finish(self):
        finals = [v for v in self.allsems.values() if v[1] > 0]

        def emit_final(e):
            for sem, val in finals:
                e.wait_ge(sem, val)

        self.prog["sp"].append(emit_final)
        nc = self.nc
        bounds = self.cuts + [{e: len(self.prog[e]) for e in self.engs}]
        prev = {e: 0 for e in self.engs}
        for bd in bounds:
            seg = {e: self.prog[e][prev[e]:bd[e]] for e in self.engs}
            prev = bd
            if not any(seg.values()):
                continue
            with nc.Block() as block:
                @block.tensor
                def _(e, seg=seg):
                    for f in seg["pe"]:
                        f(e)

                @block.scalar
                def _(e, seg=seg):
                    for f in seg["act"]:
                        f(e)

                @block.vector
                def _(e, seg=seg):
                    for f in seg["dve"]:
                        f(e)

                @block.gpsimd
                def _(e, seg=seg):
                    for f in seg["pool"]:
                        f(e)

                @block.sync
                def _(e, seg=seg):
                    for f in seg["sp"]:
                        f(e)


import math
from contextlib import ExitStack
import numpy as np

D = 2048
DFF = 1536
NT = 256
EPS = 1e-6
TWO_PI = 2 * math.pi
CW1 = 6.28125
CW2 = float(np.float32(TWO_PI - CW1))
CW3 = float(TWO_PI - CW1 - CW2)
MAGIC = 12582912.0


class Cx:
    def __init__(self, nc, st):
        self.nc = nc
        self.st = st
        self.k = K(nc, st)
        self._n = 0
        self.stk = [st]
        self.banks = []
        for i in range(8):
            t = st.enter_context(nc.psum_tensor("bank%d" % i, [128, 512], F32))
            self.banks.append((t, Dep()))
        self._b = 0
        self.ones = self.sb([128, 128], BF16)
        self.dones = Dep()
        self.k.op("dve", lambda e: e.memset(self.ones[:], 1.0), writes=[self.dones])
        self.one32 = self.sb([128, 1], F32)
        self.done32 = Dep()
        self.k.op("dve", lambda e: e.memset(self.one32[:], 1.0), writes=[self.done32])

    def push(self, st):
        self.stk.append(st)

    def pop(self):
        self.stk.pop()
        self._stage_key = None

    def sb(self, shape, dt):
        self._n += 1
        return self.stk[-1].enter_context(self.nc.sbuf_tensor("t%d" % self._n, list(shape), dt))

    def rot(self, shape, dt, n):
        return Rot([(self.sb(shape, dt), Dep()) for _ in range(n)])

    def din(self, name, shape, dt):
        return self.nc.dram_tensor(name, list(shape), dt, kind="ExternalInput").ap()

    def dout(self, name, shape, dt):
        return self.nc.dram_tensor(name, list(shape), dt, kind="ExternalOutput").ap()

    def dscr(self, name, shape, dt):
        return self.nc.dram_tensor(name, list(shape), dt, kind="Internal").ap()

    def bank(self):
        b = self.banks[self._b % 8]
        self._b += 1
        return b

    def chain(self, M, N, pairs, reads, skip=False):
        bank, dep = self.bank()
        out = bank[0:M, 0:N]
        n = len(pairs)
        for i, (l, r) in enumerate(pairs):
            self.k.op("pe", lambda e, l=l, r=r, i=i: e.matmul(out, l, r, start=(i == 0), stop=(i == n - 1)),
                      reads=reads, writes=[dep], inc=(i == n - 1))
        return out, dep


class Rot:
    def __init__(self, items):
        self.items = items
        self.i = 0

    def next(self):
        it = self.items[self.i % len(self.items)]
        self.i += 1
        return it


def load_w(cx, w_ap, kc_n, F, gcol=None, dg=None, extra=None):
    k = cx.k
    W = cx.sb([128, kc_n, F], BF16)
    dW = deps(kc_n)
    wv = w_ap.rearrange("(kc p) f -> p kc f", p=128)
    key = id(cx.stk[-1])
    if getattr(cx, "_stage_key", None) != key:
        cx._stage_key = key
        cx._stage = cx.rot([128, 512], F32, 4)
        cx._stage_n = 0
    for kc in range(kc_n):
        for c0 in range(0, F, 512):
            w = min(512, F - c0)
            st, dst_ = cx._stage.next()
            n = cx._stage_n
            cx._stage_n += 1
            k.dma(("sp", "act")[n % 2], st[:, 0:w], wv[:, kc, c0:c0 + w], writes=[dst_])
            eng = ("dve", "act", "pool")[n % 3]
            out = W[:, kc, c0:c0 + w]
            rd = [dst_] + ([dg] if gcol is not None else [])
            if eng == "act":
                if gcol is not None:
                    k.op("act", lambda e, out=out, st=st, w=w, kc=kc: e.activation(out, st[:, 0:w], AF.Copy, scale=gcol[:, kc:kc + 1]), reads=rd, writes=[dW[kc]])
                else:
                    k.op("act", lambda e, out=out, st=st, w=w: e.activation(out, st[:, 0:w], AF.Copy), reads=rd, writes=[dW[kc]])
            else:
                if gcol is not None:
                    k.op(eng, lambda e, out=out, st=st, w=w, kc=kc: e.tensor_scalar(out, st[:, 0:w], gcol[:, kc:kc + 1], None, ALU.mult), reads=rd, writes=[dW[kc]])
                else:
                    k.op(eng, lambda e, out=out, st=st, w=w: e.tensor_copy(out, st[:, 0:w]), reads=rd, writes=[dW[kc]])
            if extra is not None:
                extra(kc, c0, w, st, dst_)
    return W, dW


def fold_g(cx, W, dW, gcol, dg, kc_n, eng="pool"):
    return


def load_small(cx, ap, shape, dt=F32, q="sp"):
    t = cx.sb(shape, dt)
    d = Dep()
    cx.k.dma(q, t[:], ap, writes=[d])
    return t, d


def rstd_from_ss(cx, ss_ap, dss, out, dout, n_feat, tmp, dtmp):
    k = cx.k
    k.op("dve", lambda e: e.tensor_scalar(tmp, ss_ap, 1.0 / n_feat, EPS, ALU.mult, ALU.add), reads=[dss], writes=[dtmp])
    k.op("act", lambda e: e.activation(tmp, tmp, AF.Sqrt), reads=[dtmp], writes=[dtmp])
    k.op("dve", lambda e: e.reciprocal(out, tmp), reads=[dtmp], writes=[dout])


def load_x_stats(cx, srcv, t0, nt, X, dX, SQ, dSQ, kc_n, rstd, drstd, tmp, dtmp, n_feat, dsrc, srcb=None):
    k = cx.k
    if srcb is not None:
        k.dma("sp", X[:, :, 0:nt], srcb.rearrange("(kc p) t -> p kc t", p=128)[:, :, t0:t0 + nt], reads=[dsrc], writes=[dX])
    else:
        k.dma("pool", X[:, :, 0:nt], srcv[:, :, t0:t0 + nt], reads=[dsrc], writes=[dX])
    k.op("pool", lambda e: e.tensor_tensor(SQ[:, :, 0:nt], X[:, :, 0:nt], X[:, :, 0:nt], ALU.mult), reads=[dX], writes=[dSQ])
    ss, dss = cx.chain(128, nt, [(cx.ones[:], SQ[:, kc, 0:nt]) for kc in range(kc_n)], [dSQ, cx.dones])
    rstd_from_ss(cx, ss, dss, rstd, drstd, n_feat, tmp, dtmp)


def emit_ffn(cx, T, src, gcol_ap, wg, wu, wd, dst, dsrc=None, ddst=None, dst2=None, srcb=None, dstb=None):
    k = cx.k
    dsrc = dsrc or Dep()
    ddst = ddst or Dep()
    with ExitStack() as ps:
        cx.push(ps)
        gcol, dg = load_small(cx, gcol_ap, [128, 16])
        Wg, dWg = load_w(cx, wg, 16, DFF, gcol, dg)
        Wu, dWu = load_w(cx, wu, 16, DFF, gcol, dg)
        Wd, dWd = load_w(cx, wd, 12, D)
        srcv = src.rearrange("(kc p) t -> p kc t", p=128)
        Xr = cx.rot([128, 16, NT], BF16, 2)
        SQr = cx.rot([128, 16, NT], BF16, 1)
        Hr = cx.rot([128, 12, NT], BF16, 2)
        rs_r = cx.rot([128, NT], F32, 2)
        tmp_r = cx.rot([128, NT], F32, 2)
        g1r = cx.rot([128, NT], F32, 2)
        u1r = cx.rot([128, NT], F32, 2)
        xr_r = cx.rot([128, NT], F32, 3)
        o_r = cx.rot([128, NT], F32, 3)
        ob_r = cx.rot([128, NT], BF16, 2)
        for tt in range(T // NT):
            t0 = tt * NT
            X, dX = Xr.next()
            SQ, dSQ = SQr.next()
            H, dH = Hr.next()
            rstd, drs = rs_r.next()
            tmp, dtmp = tmp_r.next()
            load_x_stats(cx, srcv, t0, NT, X, dX, SQ, dSQ, 16, rstd[:], drs, tmp[:], dtmp, D, dsrc, srcb)
            for fc in range(12):
                fs = slice(fc * 128, (fc + 1) * 128)
                pg, dpg = cx.chain(128, NT, [(Wg[:, kc, fs], X[:, kc, :]) for kc in range(16)], dWg + [dX])
                pu, dpu = cx.chain(128, NT, [(Wu[:, kc, fs], X[:, kc, :]) for kc in range(16)], dWu + [dX])
                g1, dg1 = g1r.next()
                u1, du1 = u1r.next()
                k.op("dve", lambda e, g1=g1, pg=pg, rstd=rstd: e.tensor_tensor(g1[:], pg, rstd[:], ALU.mult), reads=[dpg, drs], writes=[dg1])
                k.op("act", lambda e, g1=g1: e.activation(g1[:], g1[:], AF.Silu), reads=[dg1], writes=[dg1])
                k.op("dve", lambda e, u1=u1, pu=pu, rstd=rstd: e.tensor_tensor(u1[:], pu, rstd[:], ALU.mult), reads=[dpu, drs], writes=[du1])
                k.op("pool", lambda e, H=H, fc=fc, g1=g1, u1=u1: e.tensor_tensor(H[:, fc, :], g1[:], u1[:], ALU.mult), reads=[dg1, du1], writes=[dH])
            for dc in range(16):
                ds_ = slice(dc * 128, (dc + 1) * 128)
                py, dpy = cx.chain(128, NT, [(Wd[:, fc, ds_], H[:, fc, :]) for fc in range(12)], dWd + [dH])
                xr, dxr = xr_r.next()
                o, do = o_r.next()
                k.dma("sp", xr[:], src[ds_, t0:t0 + NT], reads=[dsrc], writes=[dxr])
                k.op("dve", lambda e, o=o, py=py, xr=xr: e.scalar_tensor_tensor(o[:], py, 0.5, xr[:], ALU.mult, ALU.add), reads=[dpy, dxr], writes=[do])
                k.dma("act", dst[ds_, t0:t0 + NT], o[:], reads=[do], writes=[ddst])
                if dst2 is not None:
                    k.dma("act", dst2[ds_, t0:t0 + NT], o[:], reads=[do])
                if dstb is not None:
                    ob16, dob16 = ob_r.next()
                    k.op("act", lambda e, ob16=ob16, o=o: e.activation(ob16[:], o[:], AF.Copy), reads=[do], writes=[dob16])
                    k.dma("act", dstb[ds_, t0:t0 + NT], ob16[:], reads=[dob16], writes=[ddst])
        cx.pop()
    k.cut()


def rope_tables(cx, posf, dposf, inv, dinv, P, nt, rk, sin_t, dsin, cos_t, dcos, ang, dang, kk, dkk):
    k = cx.k
    k.op("dve", lambda e: e.tensor_scalar(ang, posf, inv, None, ALU.mult), reads=[dposf, dinv], writes=[dang])
    k.op("dve", lambda e: e.tensor_scalar(kk, ang, 1.0 / TWO_PI, MAGIC, ALU.mult, ALU.add), reads=[dang], writes=[dkk])
    k.op("dve", lambda e: e.tensor_scalar(kk, kk, -MAGIC, None, ALU.add), reads=[dkk], writes=[dkk])
    for cc in (CW1, CW2, CW3):
        k.op("dve", lambda e, cc=cc: e.scalar_tensor_tensor(ang, kk, -cc, ang, ALU.mult, ALU.add), reads=[dkk, dang], writes=[dang])
    k.op("dve", lambda e: e.tensor_scalar(ang, ang, math.pi, -math.pi, ALU.min, ALU.max), reads=[dang], writes=[dang])
    k.op("act", lambda e: e.activation(sin_t, ang, AF.Sin), reads=[dang], writes=[dsin])
    k.op("act", lambda e: e.activation(kk, ang, AF.Abs), reads=[dang], writes=[dkk])
    k.op("dve", lambda e: e.tensor_scalar(kk, kk, -1.0, math.pi / 2, ALU.mult, ALU.add), reads=[dkk], writes=[dkk])
    k.op("act", lambda e: e.activation(cos_t, kk, AF.Sin), reads=[dkk], writes=[dcos])


class Long:
    def __init__(self, cx, names):
        self.t = {n: (cx.sb([128, NT], F32), Dep()) for n in names}

    def __getitem__(self, n):
        return self.t[n]


def pos_tables(cx, A, t0, posr, L, invc, dinv, P, sname, cname):
    k = cx.k
    pi_, dpi = posr.next()
    posf, dposf = L["posf"]
    k.dma("sp", pi_[:], A["posrep"][:, t0:t0 + NT], writes=[dpi])
    k.op("dve", lambda e: e.tensor_copy(posf[:], pi_[:]), reads=[dpi], writes=[dposf])
    sn, dsn = L[sname]; cs, dcs = L[cname]; ang, dang = L["ang"]; kk, dkk = L["kk"]
    rope_tables(cx, posf[0:P, :], dposf, invc[0:P, 0:1], dinv, P, NT, None, sn[0:P, :], dsn, cs[0:P, :], dcs, ang[0:P, :], dang, kk[0:P, :], dkk)


def emit_inproj_even_A(cx, T, src, dsrc, A, srcb=None):
    k = cx.k
    with ExitStack() as ps:
        cx.push(ps)
        gcol, dg = load_small(cx, A["g1col"], [128, 16])
        invA, dinvA = load_small(cx, A["invA"], [128, 1])
        w_in = A["w_in"]
        Wrot = cx.sb([128, 16, 1536], BF16)
        dWrot = deps(16)
        Wr5 = Wrot[:].rearrange("p kc (h two i) -> p kc h two i", two=2, i=64)

        def rot_extra(kc, c0, w, st, dst_):
            h0 = c0 // 128
            nh = w // 128
            sv = st[:, 0:w].rearrange("p (h two i) -> p h two i", two=2, i=64)
            k.op("dve", lambda e: e.tensor_scalar(Wr5[:, kc, h0:h0 + nh, 0, :], sv[:, :, 1, :], gcol[:, kc:kc + 1], -1.0, ALU.mult, ALU.mult), reads=[dst_, dg], writes=[dWrot[kc]])
            k.op("pool", lambda e: e.tensor_scalar(Wr5[:, kc, h0:h0 + nh, 1, :], sv[:, :, 0, :], gcol[:, kc:kc + 1], None, ALU.mult), reads=[dst_, dg], writes=[dWrot[kc]])

        Win, dWin = load_w(cx, w_in[:, 0:1536], 16, 1536, gcol, dg, extra=rot_extra)
        srcv = src.rearrange("(kc p) t -> p kc t", p=128)
        Xr = cx.rot([128, 16, NT], BF16, 2)
        SQr = cx.rot([128, 16, NT], BF16, 1)
        L = Long(cx, ["rstd", "tmp", "posf", "sinA", "cosA", "ang", "kk", "cq", "sq"])
        f32r = cx.rot([128, NT], F32, 6)
        obr = cx.rot([128, NT], BF16, 6)
        posr = cx.rot([128, NT], I32, 2)
        SA = 128 ** -0.5
        for tt in range(T // NT):
            t0 = tt * NT
            X, dX = Xr.next()
            SQ, dSQ = SQr.next()
            rstd, drs = L["rstd"]; tmp, dtmp = L["tmp"]
            load_x_stats(cx, srcv, t0, NT, X, dX, SQ, dSQ, 16, rstd[:], drs, tmp[:], dtmp, D, dsrc, srcb)
            pos_tables(cx, A, t0, posr, L, invA, dinvA, 128, "sinA", "cosA")
            sinA, dsinA = L["sinA"]; cosA, dcosA = L["cosA"]; cq, dcq = L["cq"]; sq_, dsq_ = L["sq"]
            k.op("dve", lambda e: e.scalar_tensor_tensor(cq[:], cosA[:], SA, rstd[:], ALU.mult, ALU.mult), reads=[dcosA, drs], writes=[dcq])
            k.op("dve", lambda e: e.scalar_tensor_tensor(sq_[:], sinA[:], SA, rstd[:], ALU.mult, ALU.mult), reads=[dsinA, drs], writes=[dsq_])
            k.op("dve", lambda e: e.tensor_tensor(cosA[:], cosA[:], rstd[:], ALU.mult), reads=[drs], writes=[dcosA])
            k.op("dve", lambda e: e.tensor_tensor(sinA[:], sinA[:], rstd[:], ALU.mult), reads=[drs], writes=[dsinA])
            for hh in range(12):
                cs = slice(hh * 128, (hh + 1) * 128)
                p1, dp1 = cx.chain(128, NT, [(Win[:, kc, cs], X[:, kc, :]) for kc in range(16)], dWin + [dX])
                p2, dp2 = cx.chain(128, NT, [(Wrot[:, kc, cs], X[:, kc, :]) for kc in range(16)], dWrot + [dX])
                ct, dct, stb, dst_ = (cq, dcq, sq_, dsq_) if hh < 6 else (cosA, dcosA, sinA, dsinA)
                t1, dt1 = f32r.next(); t2, dt2 = f32r.next()
                k.op("dve", lambda e, t1=t1, p1=p1, ct=ct: e.tensor_tensor(t1[:], p1, ct[:], ALU.mult), reads=[dp1, dct], writes=[dt1])
                k.op("dve", lambda e, t2=t2, p2=p2, stb=stb: e.tensor_tensor(t2[:], p2, stb[:], ALU.mult), reads=[dp2, dst_], writes=[dt2])
                ob, dob = obr.next()
                k.op("pool", lambda e, ob=ob, t1=t1, t2=t2: e.tensor_tensor(ob[:], t1[:], t2[:], ALU.add), reads=[dt1, dt2], writes=[dob])
                dstT = A["qaT"] if hh < 6 else A["kaT"]
                k.dma("act", dstT[hh % 6, :, t0:t0 + NT], ob[:], reads=[dob])
        cx.pop()
    k.cut()


def rstd_cols(cx, rstd, drs, colr, n_sub):
    out = []
    for s in range(n_sub):
        pc, dpc = cx.chain(128, 1, [(rstd[0:1, s * 128:(s + 1) * 128], cx.one32[0:1, 0:1])], [drs, cx.done32])
        rc, drc = colr.next()
        cx.k.op("dve", lambda e, rc=rc, pc=pc: e.tensor_copy(rc[:], pc), reads=[dpc], writes=[drc])
        out.append((rc, drc))
    return out


def emit_inproj_even_B(cx, T, src, dsrc, A, srcb=None):
    k = cx.k
    with ExitStack() as ps:
        cx.push(ps)
        gcol, dg = load_small(cx, A["g1col"], [128, 16])
        qn, dqn = load_small(cx, A["qncol"], [128, 3])
        kvn, dkvn = load_small(cx, A["kvncol"], [128, 2])
        invB, dinvB = load_small(cx, A["invB"], [128, 1])
        w_in = A["w_in"]
        OFF = 1536
        Win, dWin = load_w(cx, w_in[:, OFF:2976], 16, 2976 - OFF, gcol, dg)
        Wkr = cx.sb([128, 16, 96], BF16)
        Wkrr = cx.sb([128, 16, 96], BF16)
        dWkr = Dep()
        k.op("dve", lambda e: e.memset(Wkr[:], 0.0), writes=[dWkr])
        k.op("dve", lambda e: e.memset(Wkrr[:], 0.0), writes=[dWkr])
        wkv = w_in[:, 2944:2976].rearrange("(kc p) f -> p kc f", p=128)
        k.dma("pool", Wkr[:, :, 64:96], wkv, writes=[dWkr])
        k.dma("pool", Wkrr[:, :, 64:80], wkv[:, :, 16:32], writes=[dWkr])
        k.dma("pool", Wkrr[:, :, 80:96], wkv[:, :, 0:16], writes=[dWkr])
        for kc in range(16):
            k.op("pool", lambda e, kc=kc: e.tensor_scalar(Wkr[:, kc, 64:96], Wkr[:, kc, 64:96], gcol[:, kc:kc + 1], None, ALU.mult), reads=[dg], writes=[dWkr])
            k.op("pool", lambda e, kc=kc: e.tensor_scalar(Wkrr[:, kc, 64:80], Wkrr[:, kc, 64:80], gcol[:, kc:kc + 1], -1.0, ALU.mult, ALU.mult), reads=[dg], writes=[dWkr])
            k.op("pool", lambda e, kc=kc: e.tensor_scalar(Wkrr[:, kc, 80:96], Wkrr[:, kc, 80:96], gcol[:, kc:kc + 1], None, ALU.mult), reads=[dg], writes=[dWkr])
        Wq, dWq = load_w(cx, A["w_q_up"], 3, 384, qn, dqn)
        Wqr = cx.sb([128, 3, 384], BF16)
        dWqr = Dep()
        k.op("dve", lambda e: e.memset(Wqr[:], 0.0), writes=[dWqr])
        wq4 = A["w_q_up"].rearrange("(kc p) (h c) -> p kc h c", p=128, c=96)
        Wqr4 = Wqr[:].rearrange("p kc (h c) -> p kc h c", c=96)
        for kc in range(3):
            k.dma("pool", Wqr4[:, kc, :, 64:80], wq4[:, kc, :, 80:96], writes=[dWqr])
            k.dma("pool", Wqr4[:, kc, :, 80:96], wq4[:, kc, :, 64:80], writes=[dWqr])
            k.op("pool", lambda e, kc=kc: e.tensor_scalar(Wqr4[:, kc, :, 64:80], Wqr4[:, kc, :, 64:80], qn[:, kc:kc + 1], -1.0, ALU.mult, ALU.mult), reads=[dqn], writes=[dWqr])
            k.op("pool", lambda e, kc=kc: e.tensor_scalar(Wqr4[:, kc, :, 80:96], Wqr4[:, kc, :, 80:96], qn[:, kc:kc + 1], None, ALU.mult), reads=[dqn], writes=[dWqr])
        Wkv, dWkv = load_w(cx, A["w_kv_up"], 2, 768, kvn, dkvn)
        Wkv4 = Wkv[:].rearrange("p kc (h c) -> p kc h c", c=192)

        srcv = src.rearrange("(kc p) t -> p kc t", p=128)
        Xr = cx.rot([128, 16, NT], BF16, 2)
        SQr = cx.rot([128, 16, NT], BF16, 1)
        L = Long(cx, ["rstd", "tmp", "posf", "sinB", "cosB", "ang", "kk", "rstd2", "tmp2", "cB2", "sB2", "rstd3", "tmp3", "cB1", "sB1"])
        f32r = cx.rot([128, NT], F32, 6)
        obr = cx.rot([128, NT], BF16, 6)
        otr = cx.rot([128, 512], BF16, 3)
        colr = cx.rot([128, 1], F32, 8)
        cqr = cx.rot([128, 3, NT], BF16, 2)
        ckr = cx.rot([128, 2, NT], BF16, 2)
        sq2r = cx.rot([128, 3, NT], BF16, 2)
        posr = cx.rot([128, NT], I32, 2)
        SB_ = 96 ** -0.5
        CV, CQ, CKV = 0, 768, 1152
        for tt in range(T // NT):
            t0 = tt * NT
            X, dX = Xr.next()
            SQ, dSQ = SQr.next()
            rstd, drs = L["rstd"]; tmp, dtmp = L["tmp"]
            load_x_stats(cx, srcv, t0, NT, X, dX, SQ, dSQ, 16, rstd[:], drs, tmp[:], dtmp, D, dsrc, srcb)
            pos_tables(cx, A, t0, posr, L, invB, dinvB, 96, "sinB", "cosB")
            sinB, dsinB = L["sinB"]; cosB, dcosB = L["cosB"]
            rcols = rstd_cols(cx, rstd, drs, colr, NT // 128)
            for s in range(NT // 128):
                ts_ = slice(s * 128, (s + 1) * 128)
                rc, drc = rcols[s]
                for (f0, f1) in ((0, 512), (512, 768)):
                    pv, dpv = cx.chain(128, f1 - f0, [(X[:, kc, ts_], Win[:, kc, CV + f0:CV + f1]) for kc in range(16)], dWin + [dX])
                    ot, dot = otr.next()
                    k.op("act", lambda e, ot=ot, pv=pv, rc=rc, w=f1 - f0: e.activation(ot[:, 0:w], pv, AF.Copy, scale=rc[:, 0:1]), reads=[dpv, drc], writes=[dot])
                    k.dma("act", A["va"][t0 + s * 128:t0 + (s + 1) * 128, f0:f1], ot[:, 0:f1 - f0], reads=[dot])
            cqb, dcqb = cqr.next()
            sq2, dsq2 = sq2r.next()
            for c in range(3):
                pcq, dpcq = cx.chain(128, NT, [(Win[:, kc, CQ + c * 128:CQ + (c + 1) * 128], X[:, kc, :]) for kc in range(16)], dWin + [dX])
                k.op("dve", lambda e, c=c, cqb=cqb, pcq=pcq: e.tensor_tensor(cqb[:, c, :], pcq, rstd[:], ALU.mult), reads=[dpcq, drs], writes=[dcqb])
            k.op("pool", lambda e, sq2=sq2, cqb=cqb: e.tensor_tensor(sq2[:], cqb[:], cqb[:], ALU.mult), reads=[dcqb], writes=[dsq2])
            ss2, dss2 = cx.chain(128, NT, [(cx.ones[:], sq2[:, c, :]) for c in range(3)], [dsq2, cx.dones])
            rstd2, drs2 = L["rstd2"]; tmp2, dtmp2 = L["tmp2"]
            rstd_from_ss(cx, ss2, dss2, rstd2[:], drs2, 384, tmp2[:], dtmp2)
            cB2, dcB2 = L["cB2"]; sB2, dsB2 = L["sB2"]
            k.op("dve", lambda e: e.scalar_tensor_tensor(cB2[0:96, :], cosB[0:96, :], SB_, rstd2[0:96, :], ALU.mult, ALU.mult), reads=[dcosB, drs2], writes=[dcB2])
            k.op("dve", lambda e: e.scalar_tensor_tensor(sB2[0:96, :], sinB[0:96, :], SB_, rstd2[0:96, :], ALU.mult, ALU.mult), reads=[dsinB, drs2], writes=[dsB2])
            for h in range(4):
                cs = slice(h * 96, (h + 1) * 96)
                p1, dp1 = cx.chain(96, NT, [(Wq[:, c, cs], cqb[:, c, :]) for c in range(3)], dWq + [dcqb])
                p2, dp2 = cx.chain(96, NT, [(Wqr[:, c, cs], cqb[:, c, :]) for c in range(3)], [dWqr, dcqb])
                t1, dt1 = f32r.next(); t2, dt2 = f32r.next()
                k.op("dve", lambda e, t1=t1, p1=p1: e.tensor_tensor(t1[0:96, :], p1, cB2[0:96, :], ALU.mult), reads=[dp1, dcB2], writes=[dt1])
                k.op("dve", lambda e, t2=t2, p2=p2: e.tensor_tensor(t2[0:96, :], p2, sB2[0:96, :], ALU.mult), reads=[dp2, dsB2], writes=[dt2])
                ob, dob = obr.next()
                k.op("pool", lambda e, ob=ob, t1=t1, t2=t2: e.tensor_tensor(ob[0:96, :], t1[0:96, :], t2[0:96, :], ALU.add), reads=[dt1, dt2], writes=[dob])
                k.dma("act", A["qbT"][h, :, t0:t0 + NT], ob[0:96, :], reads=[dob])
            ckb, dckb = ckr.next()
            sq3, dsq3 = sq2r.next()
            for c in range(2):
                pck, dpck = cx.chain(128, NT, [(Win[:, kc, CKV + c * 128:CKV + (c + 1) * 128], X[:, kc, :]) for kc in range(16)], dWin + [dX])
                k.op("dve", lambda e, c=c, ckb=ckb, pck=pck: e.tensor_tensor(ckb[:, c, :], pck, rstd[:], ALU.mult), reads=[dpck, drs], writes=[dckb])
            k.op("pool", lambda e, sq3=sq3, ckb=ckb: e.tensor_tensor(sq3[:, 0:2, :], ckb[:], ckb[:], ALU.mult), reads=[dckb], writes=[dsq3])
            ss3, dss3 = cx.chain(128, NT, [(cx.ones[:], sq3[:, c, :]) for c in range(2)], [dsq3, cx.dones])
            rstd3, drs3 = L["rstd3"]; tmp3, dtmp3 = L["tmp3"]
            rstd_from_ss(cx, ss3, dss3, rstd3[:], drs3, 256, tmp3[:], dtmp3)
            for h in range(4):
                pk, dpk = cx.chain(64, NT, [(Wkv4[:, c, h, 0:64], ckb[:, c, :]) for c in range(2)], dWkv + [dckb])
                ob, dob = obr.next()
                k.op("dve", lambda e, ob=ob, pk=pk: e.tensor_tensor(ob[0:64, :], pk, rstd3[0:64, :], ALU.mult), reads=[dpk, drs3], writes=[dob])
                k.dma("act", A["kbT"][h, 0:64, t0:t0 + NT], ob[0:64, :], reads=[dob])
            rcols3 = rstd_cols(cx, rstd3, drs3, colr, NT // 128)
            for s in range(NT // 128):
                ts_ = slice(s * 128, (s + 1) * 128)
                rc, drc = rcols3[s]
                pv, dpv = cx.chain(128, 512, [(ckb[:, c, ts_], Wkv4[:, c, :, 64:192]) for c in range(2)], dWkv + [dckb])
                ot, dot = otr.next()
                k.op("act", lambda e, ot=ot, pv=pv, rc=rc: e.activation(ot[:], pv, AF.Copy, scale=rc[:, 0:1]), reads=[dpv, drc], writes=[dot])
                k.dma("act", A["vb"][t0 + s * 128:t0 + (s + 1) * 128, :], ot[:], reads=[dot])
            p1, dp1 = cx.chain(96, NT, [(Wkr[:, kc, :], X[:, kc, :]) for kc in range(16)], [dWkr, dX])
            p2, dp2 = cx.chain(96, NT, [(Wkrr[:, kc, :], X[:, kc, :]) for kc in range(16)], [dWkr, dX])
            cB1, dcB1 = L["cB1"]; sB1, dsB1 = L["sB1"]
            k.op("dve", lambda e: e.tensor_tensor(cB1[64:96, :], cosB[64:96, :], rstd[64:96, :], ALU.mult), reads=[dcosB, drs], writes=[dcB1])
            k.op("dve", lambda e: e.tensor_tensor(sB1[64:96, :], sinB[64:96, :], rstd[64:96, :], ALU.mult), reads=[dsinB, drs], writes=[dsB1])
            t1, dt1 = f32r.next(); t2, dt2 = f32r.next()
            k.op("dve", lambda e, t1=t1, p1=p1: e.tensor_tensor(t1[64:96, :], p1[64:96, :], cB1[64:96, :], ALU.mult), reads=[dp1, dcB1], writes=[dt1])
            k.op("dve", lambda e, t2=t2, p2=p2: e.tensor_tensor(t2[64:96, :], p2[64:96, :], sB1[64:96, :], ALU.mult), reads=[dp2, dsB1], writes=[dt2])
            ob, dob = obr.next()
            k.op("pool", lambda e, ob=ob, t1=t1, t2=t2: e.tensor_tensor(ob[64:96, :], t1[64:96, :], t2[64:96, :], ALU.add), reads=[dt1, dt2], writes=[dob])
            for h in range(4):
                k.dma("act", A["kbT"][h, 64:96, t0:t0 + NT], ob[64:96, :], reads=[dob])
        cx.pop()
    k.cut()


def norm2_bcast(cx, XT, dXT, P, c0, n, sqr):
    sq, dsq = sqr.next()
    cx.k.op("pool", lambda e: e.tensor_tensor(sq[0:P, 0:n], XT[0:P, c0:c0 + n], XT[0:P, c0:c0 + n], ALU.mult), reads=[dXT], writes=[dsq])
    return cx.chain(128, n, [(cx.ones[0:P, :], sq[0:P, 0:n])], [dsq, cx.dones])


def kmax2(cx, KT, dKT, P, Lk, sqr, f32r, km, dkm):
    k = cx.k
    first = True
    for c0 in range(0, Lk, 512):
        n = min(512, Lk - c0)
        pn, dpn = norm2_bcast(cx, KT, dKT, P, c0, n, sqr)
        if first:
            k.op("dve", lambda e, pn=pn: e.reduce_max(km[:], pn, AX.X), reads=[dpn], writes=[dkm])
            first = False
        else:
            t, dt = f32r.next()
            k.op("dve", lambda e, t=t, pn=pn: e.reduce_max(t[:, 0:1], pn, AX.X), reads=[dpn], writes=[dt])
            k.op("dve", lambda e, t=t: e.tensor_tensor(km[:], km[:], t[:, 0:1], ALU.max), reads=[dt], writes=[dkm])


def negm_rows(cx, QT, dQT, P, Lq, km, dkm, sqr, f32r, out_row, dout):
    k = cx.k
    for c0 in range(0, Lq, 512):
        n = min(512, Lq - c0)
        pn, dpn = norm2_bcast(cx, QT, dQT, P, c0, n, sqr)
        t, dt = f32r.next()
        k.op("act", lambda e, t=t, pn=pn, n=n: e.activation(t[0:1, 0:n], pn[0:1, :], AF.Sqrt, scale=km[0:1, 0:1]), reads=[dpn, dkm], writes=[dt])
        k.op("dve", lambda e, t=t, c0=c0, n=n: e.tensor_scalar(out_row[0:1, c0:c0 + n], t[0:1, 0:n], -1.0, None, ALU.mult), reads=[dt], writes=[dout])


def emit_mla_attn(cx, S, A):
    k = cx.k
    NB = S // 128
    with ExitStack() as ps:
        cx.push(ps)
        QT = cx.sb([128, S], BF16); dQT = Dep()
        KT = cx.sb([128, S], BF16); dKT = Dep()
        V = cx.sb([128, NB, 128], BF16); dV = Dep()
        MK = cx.sb([128, 4, 512], BF16); dMK = Dep()
        k.dma("sp", QT[0:96, :], A["qT"], writes=[dQT])
        k.dma("sp", KT[0:96, :], A["kT"], writes=[dKT])
        k.dma("sp", KT[96:97, :], A["onesrow"][0:1, 0:S], writes=[dKT])
        k.dma("sp", V[:], A["v"].rearrange("(nb p) d -> p nb d", p=128), writes=[dV])
        k.dma("sp", MK[:], A["masks"].rearrange("a p q -> p a q"), writes=[dMK])
        sqr = cx.rot([128, 512], BF16, 2)
        f32r = cx.rot([128, 512], F32, 3)
        km = cx.sb([128, 1], F32); dkm = Dep()
        negm = cx.sb([1, S], BF16); dnegm = Dep()
        kmax2(cx, KT, dKT, 96, S, sqr, f32r, km, dkm)
        negm_rows(cx, QT, dQT, 96, S, km, dkm, sqr, f32r, negm, dnegm)
        k.dma("sp", QT[96:97, :], negm[0:1, :], reads=[dnegm], writes=[dQT])
        ptr = cx.rot([128, 512], BF16, 4)
        recr = cx.rot([128, 512], F32, 2)
        outr = cx.rot([128, 512], BF16, 2)
        blocks = [(i, kb) for i in range(S // 512) for kb in range(4 * i + 4)]
        sb_i = 0
        pend = []

        def qk(i, kb, n):
            bank, dep = cx.banks[n % 4]
            out = bank[:, 0:512]
            k.op("pe", lambda e: e.matmul(out, KT[0:97, kb * 128:(kb + 1) * 128], QT[0:97, i * 512:(i + 1) * 512], start=True, stop=True),
                 reads=[dKT, dQT], writes=[dep])
            return out, dep

        def av(i, kb, sT, dsT):
            pt, dpt = ptr.next()
            k.op("act", lambda e: e.activation(pt[:], sT, AF.Exp), reads=[dsT], writes=[dpt])
            if kb >= 4 * i:
                a = kb - 4 * i
                k.op("pool", lambda e: e.tensor_tensor(pt[:], pt[:], MK[:, a, :], ALU.mult), reads=[dMK], writes=[dpt])
            ob, dob = cx.banks[4 + (i % 2)]
            db, ddb = cx.banks[6 + (i % 2)]
            first, last = (kb == 0), (kb == 4 * i + 3)
            k.op("pe", lambda e: e.matmul(ob[:, 0:512], V[:, kb, :], pt[:], start=first, stop=last), reads=[dV, dpt], writes=[dob], inc=False)
            k.op("pe", lambda e: e.matmul(db[:, 0:512], cx.ones[:], pt[:], start=first, stop=last), reads=[cx.dones, dpt], writes=[ddb])
            if last:
                rec, drec = recr.next()
                o, do = outr.next()
                k.op("dve", lambda e: e.reciprocal(rec[:], db[:, 0:512]), reads=[ddb], writes=[drec])
                k.op("dve", lambda e: e.tensor_tensor(o[:], ob[:, 0:512], rec[:], ALU.mult), reads=[dob, drec], writes=[do])
                k.dma("sp", A["oT"][:, i * 512:(i + 1) * 512], o[:], reads=[do])

        LOOK = 2
        for n, (i, kb) in enumerate(blocks):
            sT, dsT = qk(i, kb, n)
            pend.append((i, kb, sT, dsT))
            if len(pend) > LOOK:
                av(*pend.pop(0))
        while pend:
            av(*pend.pop(0))
        cx.pop()
    k.cut()


def emit_dil_attn(cx, T, A):
    k = cx.k
    HALO = 2048
    Lk = T + HALO
    NBK = Lk // 128
    with ExitStack() as ps:
        cx.push(ps)
        MK = cx.sb([128, 256], BF16); dMK = Dep()
        k.dma("sp", MK[:], A["dmask"], writes=[dMK])
        BK = cx.sb([2, Lk], BF16); dBK = Dep()
        k.dma("sp", BK[:], A["kbias2"], writes=[dBK])
        QB = cx.sb([2, T], BF16); dQB = Dep()
        KTr = cx.rot([128, Lk], BF16, 2)
        QTr = cx.rot([128, T], BF16, 2)
        Vr = cx.rot([128, 3, NBK, 128], BF16, 2)
        ACCr = cx.rot([128, 2, T], F32, 1)
        sqr = cx.rot([128, 512], BF16, 2)
        f32r = cx.rot([128, 512], F32, 3)
        kmr = cx.rot([128, 1], F32, 2)
        ptr = cx.rot([128, 256], BF16, 4)
        recr = cx.rot([128, T], F32, 1)
        outr = cx.rot([128, T], BF16, 2)
        def head(h):
            KT, dKT = KTr.next(); QT, dQT = QTr.next(); V, dV = Vr.next(); ACC, dACC = ACCr.next()
            k.dma("sp", KT[:], A["kaT"][h], writes=[dKT])
            k.dma("sp", QT[:], A["qaT"][h], writes=[dQT])
            for p in range(3):
                k.dma("sp", V[:, p, :, :], A["vreg"][p, h], writes=[dV])
            km, dkm = kmr.next()
            kmax2(cx, KT, dKT, 128, Lk, sqr, f32r, km, dkm)
            k.dma("sp", QB[1:2, :], A["onesrow"][0:1, 0:T], writes=[dQB])
            negm_rows(cx, QT, dQT, 128, T, km, dkm, sqr, f32r, QB, dQB)
            units = []
            for p, dil in enumerate((1, 4, 16)):
                nb = Lk // (128 * dil)
                nbh = HALO // (128 * dil)
                for r in range(dil):
                    for n in range(nbh, nb):
                        units.append((p, dil, nb, nbh, r, n))
            pend = []
            cnt = [0]

            def qk(u):
                p, dil, nb, nbh, r, n = u
                bank, dep = cx.banks[cnt[0] % 4]
                cnt[0] += 1
                q0 = (n - nbh) * 128 * dil + r
                qs = slice(q0, q0 + 127 * dil + 1, dil)
                for half, nn in enumerate((n - 1, n)):
                    k0 = nn * 128 * dil + r
                    ks = slice(k0, k0 + 127 * dil + 1, dil)
                    out = bank[:, half * 128:(half + 1) * 128]
                    k.op("pe", lambda e, out=out, ks=ks: e.matmul(out, KT[:, ks], QT[:, qs], start=True, stop=False), reads=[dKT, dQT], writes=[dep], inc=False)
                    k.op("pe", lambda e, out=out, ks=ks: e.matmul(out, BK[0:2, ks], QB[0:2, qs], start=False, stop=True), reads=[dBK, dQB], writes=[dep], inc=(half == 1))
                return (u, bank, dep, qs)

            def av(u, bank, dep, qs):
                p, dil, nb, nbh, r, n = u
                pt, dpt = ptr.next()
                k.op("act", lambda e: e.activation(pt[:], bank[:, 0:256], AF.Exp), reads=[dep], writes=[dpt])
                k.op("pool", lambda e: e.tensor_tensor(pt[:], pt[:], MK[:], ALU.mult), reads=[dMK], writes=[dpt])
                ob, dob = cx.banks[4 + (cnt[0] % 4)]
                for half, nn in enumerate((n - 1, n)):
                    k.op("pe", lambda e, half=half, nn=nn: e.matmul(ob[:, 0:128], V[:, p, r * nb + nn, :], pt[:, half * 128:(half + 1) * 128], start=(half == 0), stop=(half == 1)),
                         reads=[dV, dpt], writes=[dob], inc=False)
                for half in range(2):
                    k.op("pe", lambda e, half=half: e.matmul(ob[:, 128:256], cx.ones[:], pt[:, half * 128:(half + 1) * 128], start=(half == 0), stop=(half == 1)),
                         reads=[cx.dones, dpt], writes=[dob], inc=(half == 1))
                src3 = ob[:, 0:256].rearrange("p (a q) -> p a q", a=2)
                if p == 0:
                    k.op("dve", lambda e: e.tensor_copy(ACC[:, :, qs], src3), reads=[dob], writes=[dACC])
                else:
                    k.op("dve", lambda e: e.tensor_tensor(ACC[:, :, qs], src3, ACC[:, :, qs], ALU.add), reads=[dob], writes=[dACC])

            for u in units:
                pend.append(qk(u))
                if len(pend) > 2:
                    av(*pend.pop(0))
            while pend:
                av(*pend.pop(0))
            rec, drec = recr.next()
            o, do = outr.next()
            k.op("dve", lambda e, rec=rec, ACC=ACC: e.reciprocal(rec[:], ACC[:, 1, :]), reads=[dACC], writes=[drec])
            k.op("dve", lambda e, o=o, rec=rec, ACC=ACC: e.tensor_tensor(o[:], ACC[:, 0, :], rec[:], ALU.mult), reads=[dACC, drec], writes=[do])
            k.dma("sp", A["oaT"][h], o[:], reads=[do])

        for h in range(6):
            head(h)
        cx.pop()
    k.cut()


def outproj_tail(cx, T, t0, KC, W, dW, M, dM, hin, dhin, hout, dhout, xr_r, o_r, houtb=None, ob_r=None):
    k = cx.k
    for dc in range(16):
        ds_ = slice(dc * 128, (dc + 1) * 128)
        py, dpy = cx.chain(128, NT, [(W[:, c, ds_], M[:, c, :]) for c in range(KC)], dW + [dM])
        xr, dxr = xr_r.next()
        o, do = o_r.next()
        k.dma("sp", xr[:], hin[ds_, t0:t0 + NT], reads=[dhin], writes=[dxr])
        k.op("dve", lambda e, o=o, py=py, xr=xr: e.tensor_tensor(o[:], py, xr[:], ALU.add), reads=[dpy, dxr], writes=[do])
        k.dma("act", hout[ds_, t0:t0 + NT], o[:], reads=[do], writes=[dhout])
        if houtb is not None:
            ob16, dob16 = ob_r.next()
            k.op("act", lambda e, ob16=ob16, o=o: e.activation(ob16[:], o[:], AF.Copy), reads=[do], writes=[dob16])
            k.dma("act", houtb[ds_, t0:t0 + NT], ob16[:], reads=[dob16], writes=[dhout])


def emit_outproj_even(cx, T, mT, w_out, hin, dhin, hout, dhout, houtb=None):
    k = cx.k
    KC = 10
    with ExitStack() as ps:
        cx.push(ps)
        W, dW = load_w(cx, w_out, KC, D)
        Mr = cx.rot([128, KC, NT], BF16, 2)
        xr_r = cx.rot([128, NT], F32, 4)
        o_r = cx.rot([128, NT], F32, 4)
        ob_r = cx.rot([128, NT], BF16, 4)
        mv = mT.rearrange("(c p) t -> p c t", p=128)
        for tt in range(T // NT):
            t0 = tt * NT
            M, dM = Mr.next()
            k.dma("sp", M[:], mv[:, :, t0:t0 + NT], writes=[dM])
            outproj_tail(cx, T, t0, KC, W, dW, M, dM, hin, dhin, hout, dhout, xr_r, o_r, houtb, ob_r)
        cx.pop()
    k.cut()


def emit_inproj_odd(cx, T, src, dsrc, A, srcb=None):
    k = cx.k
    with ExitStack() as ps:
        cx.push(ps)
        gcol, dg = load_small(cx, A["g1col"], [128, 16])
        Win, dWin = load_w(cx, A["w_in"], 16, 2048, gcol, dg)
        srcv = src.rearrange("(kc p) t -> p kc t", p=128)
        Xr = cx.rot([128, 16, NT], BF16, 2)
        SQr = cx.rot([128, 16, NT], BF16, 1)
        L = Long(cx, ["rstd", "tmp"])
        f32r = cx.rot([128, NT], F32, 4)
        obr = cx.rot([128, NT], BF16, 6)
        otr = cx.rot([128, 512], BF16, 3)
        colr = cx.rot([128, 1], F32, 8)
        SA = 128 ** -0.5
        for tt in range(T // NT):
            t0 = tt * NT
            X, dX = Xr.next()
            SQ, dSQ = SQr.next()
            rstd, drs = L["rstd"]; tmp, dtmp = L["tmp"]
            load_x_stats(cx, srcv, t0, NT, X, dX, SQ, dSQ, 16, rstd[:], drs, tmp[:], dtmp, D, dsrc, srcb)
            for c in range(12):
                cs = slice(c * 128, (c + 1) * 128)
                p1, dp1 = cx.chain(128, NT, [(Win[:, kc, cs], X[:, kc, :]) for kc in range(16)], dWin + [dX])
                if c < 4:
                    t1, dt1 = f32r.next()
                    k.op("dve", lambda e, t1=t1, p1=p1: e.tensor_tensor(t1[:], p1, rstd[:], ALU.mult), reads=[dp1, drs], writes=[dt1])
                    k.dma("act", A["uT"][cs, t0:t0 + NT], t1[:], reads=[dt1])
                else:
                    ob, dob = obr.next()
                    sc = SA if c < 8 else 1.0
                    k.op("dve", lambda e, ob=ob, p1=p1, sc=sc: e.scalar_tensor_tensor(ob[:], p1, sc, rstd[:], ALU.mult, ALU.mult), reads=[dp1, drs], writes=[dob])
                    dstT = A["qdT"] if c < 8 else A["kdT"]
                    k.dma("act", dstT[c % 4, :, t0:t0 + NT], ob[:], reads=[dob])
            rcols = rstd_cols(cx, rstd, drs, colr, NT // 128)
            for s in range(NT // 128):
                ts_ = slice(s * 128, (s + 1) * 128)
                rc, drc = rcols[s]
                pv, dpv = cx.chain(128, 512, [(X[:, kc, ts_], Win[:, kc, 1536:2048]) for kc in range(16)], dWin + [dX])
                ot, dot = otr.next()
                k.op("act", lambda e, ot=ot, pv=pv, rc=rc: e.activation(ot[:], pv, AF.Copy, scale=rc[:, 0:1]), reads=[dpv, drc], writes=[dot])
                k.dma("act", A["vd"][t0 + s * 128:t0 + (s + 1) * 128, :], ot[:], reads=[dot])
        cx.pop()
    k.cut()


def emit_outproj_odd(cx, T, A, hin, dhin, hout, dhout, houtb=None):
    k = cx.k
    KC = 8
    HL = 16
    with ExitStack() as ps:
        cx.push(ps)
        W, dW = load_w(cx, A["w_out"], KC, D)
        PW, dPW = load_w(cx, A["pool_w"], 1, 512)
        psc, dpsc = load_small(cx, A["pscol"], [128, 4])
        Mr = cx.rot([128, KC, NT], BF16, 2)
        Ur = cx.rot([128, 4, NT + HL], F32, 2)
        S1r = cx.rot([128, 4, NT + HL], F32, 1)
        S2r = cx.rot([128, 4, NT + HL], F32, 1)
        ICr = cx.rot([128, 4, NT], F32, 2)
        PBr = cx.rot([128, 4, NT], BF16, 2)
        xr_r = cx.rot([128, NT], F32, 4)
        o_r = cx.rot([128, NT], F32, 4)
        ob_r = cx.rot([128, NT], BF16, 4)
        uv = A["uTh"].rearrange("(g p) t -> p g t", p=128)
        ov = A["odT"].rearrange("(c p) t -> p c t", p=128)
        for tt in range(T // NT):
            t0 = tt * NT
            M, dM = Mr.next()
            U, dU = Ur.next(); S1, dS1 = S1r.next(); S2, dS2 = S2r.next(); IC, dIC = ICr.next(); PB, dPB = PBr.next()
            k.dma("sp", M[:, 4:8, :], ov[:, :, t0:t0 + NT], writes=[dM])
            k.dma("sp", U[:], uv[:, :, t0:t0 + NT + HL], writes=[dU])
            k.dma("sp", IC[:], A["invcnt"][:, :, t0:t0 + NT], writes=[dIC])
            W_ = NT + HL
            src_t, dsrc_t = U, dU
            cur, dcur = None, None
            bufs = [(S1, dS1), (S2, dS2)]
            sh = 1
            for lvl in range(4):
                dstt, ddst = bufs[lvl % 2]
                g0 = lvl
                a, da = (U, dU) if lvl == 0 else bufs[(lvl - 1) % 2]
                k.op("dve", lambda e, dstt=dstt, a=a, g0=g0, sh=sh: e.tensor_tensor(dstt[:, g0:4, sh:W_], a[:, g0:4, sh:W_], a[:, g0:4, 0:W_ - sh], ALU.add),
                     reads=[da], writes=[ddst])
                t1 = dstt
                k.op("pool", lambda e, t1=t1, lvl=lvl, IC=IC: e.tensor_tensor(t1[:, lvl, HL:W_], t1[:, lvl, HL:W_], IC[:, lvl, :], ALU.mult), reads=[dIC], writes=[ddst])
                k.op("pool", lambda e, t1=t1, lvl=lvl, PB=PB, U=U: e.tensor_tensor(PB[:, lvl, :], t1[:, lvl, HL:W_], U[:, lvl, HL:W_], ALU.subtract), reads=[ddst, dU], writes=[dPB])
                sh *= 2
            for g in range(4):
                pm, dpm = cx.chain(128, NT, [(PW[:, 0, g * 128:(g + 1) * 128], PB[:, g, :])], dPW + [dPB])
                k.op("act", lambda e, M=M, g=g, pm=pm: e.activation(M[:, g, :], pm, AF.Copy, scale=psc[:, g:g + 1]), reads=[dpm, dpsc], writes=[dM])
            outproj_tail(cx, T, t0, KC, W, dW, M, dM, hin, dhin, hout, dhout, xr_r, o_r, houtb, ob_r)
        cx.pop()
    k.cut()


def emit_final_norm(cx, T, hin, dhin, gcol_ap, out):
    k = cx.k
    with ExitStack() as ps:
        cx.push(ps)
        gcol, dg = load_small(cx, gcol_ap, [128, 16])
        hv = hin.rearrange("(kc p) t -> p kc t", p=128)
        ov = out.rearrange("(kc p) t -> p kc t", p=128)
        Xr = cx.rot([128, 16, NT], F32, 2)
        SQr = cx.rot([128, 16, NT], BF16, 1)
        Or = cx.rot([128, 16, NT], F32, 2)
        L = Long(cx, ["rstd", "tmp"])
        for tt in range(T // NT):
            t0 = tt * NT
            X, dX = Xr.next(); SQ, dSQ = SQr.next(); O, dO = Or.next()
            rstd, drs = L["rstd"]; tmp, dtmp = L["tmp"]
            k.dma("sp", X[:], hv[:, :, t0:t0 + NT], reads=[dhin], writes=[dX])
            k.op("pool", lambda e, SQ=SQ, X=X: e.tensor_tensor(SQ[:], X[:], X[:], ALU.mult), reads=[dX], writes=[dSQ])
            ss, dss = cx.chain(128, NT, [(cx.ones[:], SQ[:, kc, :]) for kc in range(16)], [dSQ, cx.dones])
            rstd_from_ss(cx, ss, dss, rstd[:], drs, D, tmp[:], dtmp)
            for kc in range(16):
                k.op("dve", lambda e, O=O, X=X, kc=kc: e.scalar_tensor_tensor(O[:, kc, :], X[:, kc, :], gcol[:, kc:kc + 1], rstd[:], ALU.mult, ALU.mult), reads=[dX, dg, drs], writes=[dO])
            k.dma("act", ov[:, :, t0:t0 + NT], O[:], reads=[dO])
        cx.pop()
    k.cut()


def emit_sb_attn(cx, S, A):
    k = cx.k
    NB = S // 128
    with ExitStack() as ps:
        cx.push(ps)
        QT = cx.sb([128, S], BF16); dQT = Dep()
        KT = cx.sb([128, S], BF16); dKT = Dep()
        V = cx.sb([128, NB, 128], BF16); dV = Dep()
        MK = cx.sb([128, 4, 512], BF16); dMK = Dep()
        TRI = cx.sb([128, 2, 128], BF16); dTRI = Dep()
        k.dma("sp", QT[:], A["qT"], writes=[dQT])
        k.dma("sp", KT[:], A["kT"], writes=[dKT])
        k.dma("sp", V[:], A["v"].rearrange("(nb p) d -> p nb d", p=128), writes=[dV])
        k.dma("sp", MK[:], A["masks"].rearrange("a p q -> p a q"), writes=[dMK])
        k.dma("sp", TRI[:], A["tri"].rearrange("a p q -> p a q"), writes=[dTRI])
        exr = cx.rot([128, 512], F32, 2)
        spr = cx.rot([128, 512], BF16, 5)
        zsr = cx.rot([128, 512], F32, 4)
        ebr = cx.rot([128, 512], F32, 2)
        ar = cx.rot([128, 512], BF16, 5)
        outr = cx.rot([128, 512], BF16, 2)
        blocks = [(i, kb) for i in range(S // 512) for kb in range(4 * i + 3, -1, -1)]
        N = len(blocks)
        st = [dict() for _ in range(N)]

        def sZ(n):
            i, kb = blocks[n]
            bank, dep = cx.banks[n % 3]
            z = bank[:, 0:512]
            k.op("pe", lambda e: e.matmul(z, KT[:, kb * 128:(kb + 1) * 128], QT[:, i * 512:(i + 1) * 512], start=True, stop=True), reads=[dKT, dQT], writes=[dep])
            st[n]["z"] = (z, dep)

        def sSP(n):
            i, kb = blocks[n]
            z, dz = st[n]["z"]
            ex, dex = exr.next(); sp, dsp = spr.next(); zs, dzs = zsr.next()
            k.op("dve", lambda e: e.tensor_copy(zs[:], z), reads=[dz], writes=[dzs])
            k.op("act", lambda e: e.activation(ex[:], zs[:], AF.Exp), reads=[dzs], writes=[dex])
            k.op("act", lambda e: e.activation(sp[:], ex[:], AF.Ln, bias=1.0), reads=[dex], writes=[dsp])
            if kb >= 4 * i:
                a = kb - 4 * i
                k.op("pool", lambda e: e.tensor_tensor(sp[:], sp[:], MK[:, a, :], ALU.mult), reads=[dMK], writes=[dsp])
            st[n]["sp"] = (sp, dsp); st[n]["zs"] = (zs, dzs)

        def sTI(n):
            i, kb = blocks[n]
            sp, dsp = st[n]["sp"]
            cb, dcb = cx.banks[3 + (i % 2)]
            k.op("pe", lambda e: e.matmul(cb[:, 0:512], TRI[:, 0, :], sp[:], start=(kb == 4 * i + 3), stop=False, skip_group_check=True), reads=[dTRI, dsp], writes=[dcb])

        def sE(n):
            i, kb = blocks[n]
            zs, dzs = st[n]["zs"]
            cb, dcb = cx.banks[3 + (i % 2)]
            eb, deb = ebr.next(); a_, da_ = ar.next()
            k.op("dve", lambda e: e.scalar_tensor_tensor(eb[:], cb[:, 0:512], -1.0, zs[:], ALU.mult, ALU.add), reads=[dcb, dzs], writes=[deb])
            k.op("act", lambda e: e.activation(a_[:], eb[:], AF.Exp), reads=[deb], writes=[da_])
            if kb >= 4 * i:
                a = kb - 4 * i
                k.op("pool", lambda e: e.tensor_tensor(a_[:], a_[:], MK[:, a, :], ALU.mult), reads=[dMK], writes=[da_])
            st[n]["a"] = (a_, da_)

        def sTR(n):
            i, kb = blocks[n]
            sp, dsp = st[n]["sp"]
            cb, dcb = cx.banks[3 + (i % 2)]
            k.op("pe", lambda e: e.matmul(cb[:, 0:512], TRI[:, 1, :], sp[:], start=False, stop=(kb == 0), skip_group_check=True), reads=[dTRI, dsp], writes=[dcb])

        def sAV(n):
            i, kb = blocks[n]
            a_, da_ = st[n]["a"]
            ob, dob = cx.banks[5 + (i % 2)]
            k.op("pe", lambda e: e.matmul(ob[:, 0:512], V[:, kb, :], a_[:], start=(kb == 4 * i + 3), stop=(kb == 0)), reads=[dV, da_], writes=[dob])
            if kb == 0:
                o, do = outr.next()
                k.op("dve", lambda e: e.tensor_copy(o[:], ob[:, 0:512]), reads=[dob], writes=[do])
                k.dma("sp", A["oT"][:, i * 512:(i + 1) * 512], o[:], reads=[do])
            st[n].clear()

        for t in range(N + 4):
            if t < N:
                sZ(t)
            if 0 <= t - 1 < N:
                sSP(t - 1)
            if 0 <= t - 3 < N:
                sTR(t - 3)
            if 0 <= t - 2 < N:
                sTI(t - 2)
                sE(t - 2)
            if 0 <= t - 4 < N:
                sAV(t - 4)
        cx.pop()
    k.cut()


import numpy as np
import ml_dtypes
BF = ml_dtypes.bfloat16

def mla_masks():
    kk = np.arange(128)[:, None]; q = np.arange(512)[None, :]
    return np.stack([((a * 128 + kk) <= q) for a in range(4)]).astype(BF)

def sb_masks():
    kk = np.arange(128)[:, None]; q = np.arange(512)[None, :]
    return np.stack([((a * 128 + kk) < q) for a in range(4)]).astype(BF)

def dil_mask():
    kk = np.arange(128)[:, None]; q = np.arange(128)[None, :]
    return np.concatenate([(kk >= q), (kk <= q)], axis=1).astype(BF)

def vreg_layout(v_h, Lk):
    out = []
    for dil in (1, 4, 16):
        nb = Lk // (128 * dil)
        t = v_h.reshape(6, nb, 128, dil, 128)
        t = t.transpose(0, 2, 3, 1, 4).reshape(6, 128, dil * nb, 128)
        out.append(t)
    return np.ascontiguousarray(np.stack(out))

def tri_mats():
    j = np.arange(128)[:, None]; s = np.arange(128)[None, :]
    return np.stack([(j >= s), (j < s)]).astype(BF)


def _colT(v, n):
    return np.ascontiguousarray(np.asarray(v, np.float32).reshape(n, 128).T)


def _inv_cols():
    invA = (10000.0 ** (-np.arange(0, 128, 2, dtype=np.float32) / 128)).astype(np.float32)
    invA = np.concatenate([invA, invA]).reshape(128, 1)
    inv16 = (10000.0 ** (-np.arange(0, 32, 2, dtype=np.float32) / 32)).astype(np.float32)
    invB = np.zeros((128, 1), np.float32)
    invB[64:80, 0] = inv16
    invB[80:96, 0] = inv16
    return invA, invB


def _ffn_ins(cx, tag):
    return (cx.din("g_" + tag, [128, 16], F32), cx.din("wg_" + tag, [D, DFF], F32), cx.din("wu_" + tag, [D, DFF], F32), cx.din("wd_" + tag, [DFF, D], F32))


def _ffn_vals(tag, g, wg, wu, wd):
    return {"g_" + tag: _colT(g, 16), "wg_" + tag: np.ascontiguousarray(wg), "wu_" + tag: np.ascontiguousarray(wu), "wd_" + tag: np.ascontiguousarray(wd)}


def build_L1(T):
    nc = bass.Bass("TRN2", target_bir_lowering=False)
    with ExitStack() as st:
        cx = Cx(nc, st)
        xT = cx.din("xT", [D, T], F32)
        f = _ffn_ins(cx, "a")
        h1o = cx.dout("h1T", [D, T], F32)
        h1T = cx.dscr("h1s", [D, T], F32)
        dh = Dep()
        h1b = cx.dscr("h1b", [D, T], BF16)
        emit_ffn(cx, T, xT, f[0], f[1], f[2], f[3], h1T, ddst=dh, dst2=h1o)
        A = {"g1col": cx.din("g1col", [128, 16], F32), "qncol": cx.din("qncol", [128, 3], F32), "kvncol": cx.din("kvncol", [128, 2], F32),
             "invA": cx.din("invA", [128, 1], F32), "invB": cx.din("invB", [128, 1], F32), "w_in": cx.din("w_in", [D, 2976], F32),
             "w_q_up": cx.din("w_q_up", [384, 384], F32), "w_kv_up": cx.din("w_kv_up", [256, 768], F32), "posrep": cx.din("posrep", [128, T], I32),
             "qaT": cx.dout("qaT", [6, 128, T], BF16), "kaT": cx.dout("kaT", [6, 128, T], BF16), "va": cx.dout("va", [T, 768], BF16),
             "qbT": cx.dout("qbT", [4, 96, T], BF16), "kbT": cx.dout("kbT", [4, 96, T], BF16), "vb": cx.dout("vb", [T, 512], BF16)}
        emit_inproj_even_A(cx, T, h1T, dh, A)
        emit_inproj_even_B(cx, T, h1T, dh, A)
        cx.k.finish()
    return nc


def build_L2(S, T):
    nc = bass.Bass("TRN2", target_bir_lowering=False)
    Lk = T + 2048
    with ExitStack() as st:
        cx = Cx(nc, st)
        A = {"qT": cx.din("qT", [96, S], BF16), "kT": cx.din("kT", [96, S], BF16), "v": cx.din("v", [S, 128], BF16),
             "onesrow": cx.din("onesrow", [1, S], BF16), "masks": cx.din("masks", [4, 128, 512], BF16), "oT": cx.dout("oT", [128, S], BF16)}
        emit_mla_attn(cx, S, A)
        B = {"qaT": cx.din("qaT", [6, 128, T], BF16), "kaT": cx.din("kaT", [6, 128, Lk], BF16), "vreg": cx.din("vreg", [3, 6, 128, Lk // 128, 128], BF16),
             "dmask": cx.din("dmask", [128, 256], BF16), "kbias2": cx.din("kbias2", [2, Lk], BF16), "onesrow": A["onesrow"], "oaT": cx.dout("oaT", [6, 128, T], BF16)}
        emit_dil_attn(cx, T, B)
        cx.k.finish()
    return nc


def build_L3(T):
    nc = bass.Bass("TRN2", target_bir_lowering=False)
    with ExitStack() as st:
        cx = Cx(nc, st)
        mT = cx.din("mT", [1280, T], BF16)
        w_out = cx.din("w_out", [1280, D], F32)
        h1T = cx.din("h1T", [D, T], F32)
        h2T = cx.dscr("h2T", [D, T], F32); d2 = Dep()
        h3T = cx.dscr("h3T", [D, T], F32); d3 = Dep()
        h4o = cx.dout("h4T", [D, T], F32)
        h4T = cx.dscr("h4s", [D, T], F32); d4 = Dep()
        h2b = cx.dscr("h2b", [D, T], BF16); h3b = cx.dscr("h3b", [D, T], BF16); h4b = cx.dscr("h4b", [D, T], BF16)
        emit_outproj_even(cx, T, mT, w_out, h1T, Dep(), h2T, d2)
        f = _ffn_ins(cx, "b")
        emit_ffn(cx, T, h2T, f[0], f[1], f[2], f[3], h3T, dsrc=d2, ddst=d3)
        f = _ffn_ins(cx, "c")
        emit_ffn(cx, T, h3T, f[0], f[1], f[2], f[3], h4T, dsrc=d3, ddst=d4, dst2=h4o)
        A = {"g1col": cx.din("g1col", [128, 16], F32), "w_in": cx.din("w_in", [D, 2048], F32),
             "uT": cx.dout("uT", [512, T], F32), "qdT": cx.dout("qdT", [4, 128, T], BF16), "kdT": cx.dout("kdT", [4, 128, T], BF16), "vd": cx.dout("vd", [T, 512], BF16)}
        emit_inproj_odd(cx, T, h4T, d4, A)
        cx.k.finish()
    return nc


def build_L4(S):
    nc = bass.Bass("TRN2", target_bir_lowering=False)
    with ExitStack() as st:
        cx = Cx(nc, st)
        A = {"qT": cx.din("qT", [128, S], BF16), "kT": cx.din("kT", [128, S], BF16), "v": cx.din("v", [S, 128], BF16),
             "masks": cx.din("masks", [4, 128, 512], BF16), "tri": cx.din("tri", [2, 128, 128], BF16), "oT": cx.dout("oT", [128, S], BF16)}
        emit_sb_attn(cx, S, A)
        cx.k.finish()
    return nc


def build_L5(T):
    nc = bass.Bass("TRN2", target_bir_lowering=False)
    with ExitStack() as st:
        cx = Cx(nc, st)
        A = {"w_out": cx.din("w_out", [1024, D], F32), "pool_w": cx.din("pool_w", [128, 512], F32), "pscol": cx.din("pscol", [128, 4], F32),
             "uTh": cx.din("uTh", [512, T + 16], F32), "odT": cx.din("odT", [512, T], BF16), "invcnt": cx.din("invcnt", [128, 4, T], F32)}
        h4T = cx.din("h4T", [D, T], F32)
        h5T = cx.dscr("h5T", [D, T], F32); d5 = Dep()
        h6T = cx.dscr("h6T", [D, T], F32); d6 = Dep()
        outT = cx.dout("outT", [D, T], F32)
        h5b = cx.dscr("h5b", [D, T], BF16)
        emit_outproj_odd(cx, T, A, h4T, Dep(), h5T, d5)
        f = _ffn_ins(cx, "d")
        emit_ffn(cx, T, h5T, f[0], f[1], f[2], f[3], h6T, dsrc=d5, ddst=d6)
        emit_final_norm(cx, T, h6T, d6, cx.din("gfin", [128, 16], F32), outT)
        cx.k.finish()
    return nc


def _run(nc, ims):
    res = run_bass_kernel_spmd(nc, ims, core_ids=list(range(8)))
    return res.results


TH = 4096


def kernel(x, positions, norm_g, ffn_w_gate, ffn_w_up, ffn_w_down, even_w_in, even_q_norm, even_w_q_up,
           even_kv_norm, even_w_kv_up, even_w_out, odd_w_in, odd_pool_w, odd_pool_scale, odd_w_out, final_norm):
    x = np.asarray(x)
    Bn, S, _ = x.shape
    T = S // 4
    Th = min(TH, T)
    NS = S // Th
    shards = [(b, jj) for b in range(Bn) for jj in range(NS)]
    rounds = [shards[i:i + 8] for i in range(0, len(shards), 8)]
    Lk = T + 2048
    positions = np.asarray(positions)
    norm_g = np.asarray(norm_g); wg = np.asarray(ffn_w_gate); wu = np.asarray(ffn_w_up); wd = np.asarray(ffn_w_down)
    invA, invB = _inv_cols()
    cores = [(c // 4, c % 4) for c in range(8)]

    def tok(jj):
        return slice(jj * Th, (jj + 1) * Th)

    nc1 = build_L1(Th)
    H1T = np.empty((Bn, D, S), np.float32)
    QAT = np.empty((Bn, 6, 128, S), BF); KAT = np.empty((Bn, 6, 128, S), BF); VA = np.empty((Bn, S, 768), BF)
    QBT = np.empty((Bn, 4, 96, S), BF); KBT = np.empty((Bn, 4, 96, S), BF); VB = np.empty((Bn, S, 512), BF)
    common1 = {}
    common1.update(_ffn_vals("a", norm_g[0, 0], wg[0, 0], wu[0, 0], wd[0, 0]))
    common1.update({"g1col": _colT(norm_g[0, 1], 16), "qncol": _colT(np.asarray(even_q_norm)[0], 3), "kvncol": _colT(np.asarray(even_kv_norm)[0], 2),
                    "invA": invA, "invB": invB, "w_in": np.ascontiguousarray(np.asarray(even_w_in)[0]), "w_q_up": np.ascontiguousarray(np.asarray(even_w_q_up)[0]),
                    "w_kv_up": np.ascontiguousarray(np.asarray(even_w_kv_up)[0])})
    for rd in rounds:
        ims = []
        for b, jj in rd:
            im = dict(common1)
            im["xT"] = np.ascontiguousarray(x[b, tok(jj)].T)
            im["posrep"] = np.ascontiguousarray(np.broadcast_to(positions[b, tok(jj)][None, :], (128, Th))).astype(np.int32)
            ims.append(im)
        r = _run(nc1, ims)
        for (b, jj), rr in zip(rd, r):
            H1T[b][:, tok(jj)] = np.asarray(rr["h1T"])
            QAT[b][:, :, tok(jj)] = np.asarray(rr["qaT"]); KAT[b][:, :, tok(jj)] = np.asarray(rr["kaT"]); VA[b][tok(jj)] = np.asarray(rr["va"])
            QBT[b][:, :, tok(jj)] = np.asarray(rr["qbT"]); KBT[b][:, :, tok(jj)] = np.asarray(rr["kbT"]); VB[b][tok(jj)] = np.asarray(rr["vb"])
        del r
    ones_row = np.ones((1, S), BF)
    mm = mla_masks(); dm = dil_mask()
    ims = []
    for c, (b, j) in enumerate(cores):
        h = j
        q0 = j * T
        lo = q0 - 2048
        a = max(lo, 0)
        kaT = np.zeros((6, 128, Lk), BF)
        kaT[:, :, a - lo:] = KAT[b][:, :, a:q0 + T]
        vh = np.zeros((6, Lk, 128), BF)
        vh[:, a - lo:] = VA[b][a:q0 + T].reshape(-1, 6, 128).transpose(1, 0, 2)
        kb2 = np.zeros((2, Lk), np.float32); kb2[0] = 1.0; kb2[1, :a - lo] = -30000.0
        ims.append({"qT": np.ascontiguousarray(QBT[b][h]), "kT": np.ascontiguousarray(KBT[b][h]),
                    "v": np.ascontiguousarray(VB[b][:, h * 128:(h + 1) * 128]), "onesrow": ones_row, "masks": mm,
                    "qaT": np.ascontiguousarray(QAT[b][:, :, q0:q0 + T]), "kaT": kaT, "vreg": vreg_layout(vh, Lk), "dmask": dm, "kbias2": kb2.astype(BF)})
    r2 = _run(build_L2(S, T), ims)
    MT = np.empty((Bn, 1280, S), BF)
    for c, (b, j) in enumerate(cores):
        MT[b][0:768, j * T:(j + 1) * T] = np.asarray(r2[c]["oaT"]).reshape(768, T)
        MT[b][768 + j * 128:768 + (j + 1) * 128, :] = np.asarray(r2[c]["oT"])
    del r2, ims, QAT, KAT, VA, QBT, KBT, VB
    nc3 = build_L3(Th)
    H4T = np.empty((Bn, D, S), np.float32); UT = np.empty((Bn, 512, S), np.float32)
    QDT = np.empty((Bn, 4, 128, S), BF); KDT = np.empty((Bn, 4, 128, S), BF); VD = np.empty((Bn, S, 512), BF)
    common3 = {"w_out": np.ascontiguousarray(np.asarray(even_w_out)[0])}
    common3.update(_ffn_vals("b", norm_g[0, 2], wg[0, 1], wu[0, 1], wd[0, 1]))
    common3.update(_ffn_vals("c", norm_g[1, 0], wg[1, 0], wu[1, 0], wd[1, 0]))
    common3.update({"g1col": _colT(norm_g[1, 1], 16), "w_in": np.ascontiguousarray(np.asarray(odd_w_in)[0])})
    for rd in rounds:
        ims = []
        for b, jj in rd:
            im = dict(common3)
            im["mT"] = np.ascontiguousarray(MT[b][:, tok(jj)])
            im["h1T"] = np.ascontiguousarray(H1T[b][:, tok(jj)])
            ims.append(im)
        r = _run(nc3, ims)
        for (b, jj), rr in zip(rd, r):
            H4T[b][:, tok(jj)] = np.asarray(rr["h4T"]); UT[b][:, tok(jj)] = np.asarray(rr["uT"])
            QDT[b][:, :, tok(jj)] = np.asarray(rr["qdT"]); KDT[b][:, :, tok(jj)] = np.asarray(rr["kdT"]); VD[b][tok(jj)] = np.asarray(rr["vd"])
        del r
    del H1T, MT
    sm = sb_masks(); tm = tri_mats()
    ims = []
    for c, (b, j) in enumerate(cores):
        h = j
        ims.append({"qT": np.ascontiguousarray(QDT[b][h]), "kT": np.ascontiguousarray(KDT[b][h]),
                    "v": np.ascontiguousarray(VD[b][:, h * 128:(h + 1) * 128]), "masks": sm, "tri": tm})
    r4 = _run(build_L4(S), ims)
    ODT = np.empty((Bn, 512, S), BF)
    for c, (b, j) in enumerate(cores):
        ODT[b][j * 128:(j + 1) * 128, :] = np.asarray(r4[c]["oT"])
    del r4, ims
    nc5 = build_L5(Th)
    pw = np.ascontiguousarray(np.asarray(odd_pool_w)[0].transpose(1, 0, 2).reshape(128, 512))
    psc = _colT(np.asarray(odd_pool_scale)[0], 4)
    common5 = {"w_out": np.ascontiguousarray(np.asarray(odd_w_out)[0]), "pool_w": pw, "pscol": psc, "gfin": _colT(final_norm, 16)}
    common5.update(_ffn_vals("d", norm_g[1, 2], wg[1, 1], wu[1, 1], wd[1, 1]))
    out = np.empty((Bn, S, D), np.float32)
    for rd in rounds:
        ims = []
        for b, jj in rd:
            im = dict(common5)
            uTh = np.zeros((512, Th + 16), np.float32)
            uTh[:, 16:] = UT[b][:, tok(jj)]
            if jj > 0:
                uTh[:, :16] = UT[b][:, jj * Th - 16:jj * Th]
            tg = np.arange(jj * Th, (jj + 1) * Th)
            ic = np.stack([1.0 / np.minimum(tg + 1, w) for w in (2, 4, 8, 16)]).astype(np.float32)
            im.update({"uTh": uTh, "odT": np.ascontiguousarray(ODT[b][:, tok(jj)]),
                       "invcnt": np.ascontiguousarray(np.broadcast_to(ic[None], (128, 4, Th))), "h4T": np.ascontiguousarray(H4T[b][:, tok(jj)])})
            ims.append(im)
        r = _run(nc5, ims)
        for (b, jj), rr in zip(rd, r):
            out[b, tok(jj)] = np.asarray(rr["outT"]).T
        del r
    return out
```

```python
import numpy as np
import concourse.bass as bass
import concourse.mybir as mybir
from concourse.bass_utils import run_bass_kernel_spmd

F32 = mybir.dt.float32
BF16 = mybir.dt.bfloat16
I32 = mybir.dt.int32
AF = mybir.ActivationFunctionType
ALU = mybir.AluOpType
AX = mybir.AxisListType

SAME_ENGINE_SYNC = {"pe": False, "act": False, "dve": False, "pool": False, "sp": False}
EPOCH = 30000


class Dep:
    __slots__ = ("w", "r")

    def __init__(self):
        self.w = None
        self.r = {}


def deps(n):
    return [Dep() for _ in range(n)]


class K:
    def __init__(self, nc, stack, n_dma_sems=16):
        self.nc = nc
        self.stack = stack
        self.engs = {"pe": nc.tensor, "act": nc.scalar, "dve": nc.vector, "pool": nc.gpsimd, "sp": nc.sync}
        self.prog = {e: [] for e in self.engs}
        self.cnt = {e: 0 for e in self.engs}
        self.sem = {e: self._newsem(e) for e in self.engs}
        self.known = {e: {} for e in self.engs}
        self.dq = {}
        for q in ("sp", "pool", "act"):
            self.dq[q] = {"sems": [self._newsem("d%s%d" % (q, i)) for i in range(n_dma_sems)], "n": 0}
        self.allsems = {}
        self.ninst = 0
        self.pending = {}
        self.cuts = []

    def _newsem(self, name):
        self._semn = getattr(self, "_semn", 0) + 1
        return self.stack.enter_context(self.nc.semaphore("s_%s_%d" % (name, self._semn)))

    def _collect(self, eng, reads, writes, extra=(), same=None):
        waits = {}
        own = id(self.sem[eng])
        kn = self.known[eng]

        def need(ev):
            if ev is None:
                return
            sem, val = ev
            sid = id(sem)
            if sid == own and not (SAME_ENGINE_SYNC[eng] if same is None else same):
                return
            if kn.get(sid, 0) >= val:
                return
            if sid not in waits or waits[sid][1] < val:
                waits[sid] = (sem, val)

        for d in reads:
            need(d.w)
        for d in writes:
            need(d.w)
            for ev in d.r.values():
                need(ev)
        for ev in extra:
            need(ev)
        for sid, (sem, val) in waits.items():
            kn[sid] = val
        return list(waits.values())

    def op(self, eng, fn, reads=(), writes=(), inc=True):
        wl = self._collect(eng, reads, writes)
        if inc and self.cnt[eng] >= EPOCH and not self.pending.get(eng):
            self.sem[eng] = self._newsem(eng)
            self.cnt[eng] = 0
        if inc:
            self.cnt[eng] += 1
            my = (self.sem[eng], self.cnt[eng])
        else:
            my = (self.sem[eng], self.cnt[eng] + 1)
        self.allsems[id(my[0])] = (my[0], max(my[1], self.allsems.get(id(my[0]), (None, 0))[1])) if inc else self.allsems.get(id(my[0]), (my[0], 0))

        def emit(e, fn=fn, wl=wl, my=my, inc=inc):
            for sem, val in wl:
                e.wait_ge(sem, val)
            if inc:
                fn(e).then_inc(my[0], 1)
            else:
                fn(e)

        self.pending[eng] = not inc
        self.prog[eng].append(emit)
        self.ninst += 1
        sid = id(my[0])
        for d in reads:
            d.r[sid] = my
        for d in writes:
            d.w = my
            d.r = {}
        return my

    def dma(self, q, out_ap, in_ap, reads=(), writes=(), **kw):
        st = self.dq[q]
        n = st["n"]
        st["n"] += 1
        P = len(st["sems"])
        sem = st["sems"][n % P]
        val = 16 * (n // P + 1)
        extra = [(sem, val - 16)] if n >= P else []
        wl = self._collect(q, reads, writes, extra, same=True)
        my = (sem, val)
        self.allsems[id(sem)] = my

        def emit(e, wl=wl, my=my, out_ap=out_ap, in_ap=in_ap, kw=kw):
            for s, v in wl:
                e.wait_ge(s, v)
            e.dma_start(out=out_ap, in_=in_ap, **kw).then_inc(my[0], 16)

        self.prog[q].append(emit)
        self.ninst += 1
        sid = id(sem)
        for d in reads:
            d.r[sid] = my
        for d in writes:
            d.w = my
            d.r = {}
        return my

    def coll(self, kind, in_ap, out_ap, groups, reads=(), writes=()):
        q = "pool"
        st = self.dq[q]
        n = st["n"]
        st["n"] += 1
        P = len(st["sems"])
        sem = st["sems"][n % P]
        val = 16 * (n // P + 1)
        extra = [(sem, val - 16)] if n >= P else []
        wl = self._collect(q, reads, writes, extra, same=True)
        my = (sem, val)
        self.allsems[id(sem)] = my

        def emit(e, wl=wl, my=my):
            for s_, v in wl:
                e.wait_ge(s_, v)
            e.collective_compute(kind, ALU.bypass, replica_groups=groups, ins=[in_ap], outs=[out_ap]).then_inc(my[0], 16)

        self.prog[q].append(emit)
        self.ninst += 1
        sid = id(sem)
        for d in reads:
            d.r[sid] = my
        for d in writes:
            d.w = my
            d.r = {}
        return my

    def barrier(self):
        finals = [v for v in self.allsems.values() if v[1] > 0]
        for eng in self.engs:
            def emit(e, finals=finals):
                for sem, val in finals:
                    e.wait_ge(sem, val)
            self.prog[eng].append(emit)
            for sem, val in finals:
                if self.known[eng].get(id(sem), 0) < val:
                    self.known[eng][id(sem)] = val

    def cut(self):
        self.barrier()
        self.cuts.append({e: len(self.prog[e]) for e in self.engs})

    def finish(self):
        finals = [v for v in self.allsems.values() if v[1] > 0]

        def emit_final(e):
            for sem, val in finals:
                e.wait_ge(sem, val)

        self.prog["sp"].append(emit_final)
        nc = self.nc
        bounds = self.cuts + [{e: len(self.prog[e]) for e in self.engs}]
        prev = {e: 0 for e in self.engs}
        for bd in bounds:
            seg = {e: self.prog[e][prev[e]:bd[e]] for e in self.engs}
            prev = bd
            if not any(seg.values()):
                continue
            with nc.Block() as block:
                @block.tensor
                def _(e, seg=seg):
                    for f in seg["pe"]:
                        f(e)

                @block.scalar
                def _(e, seg=seg):
                    for f in seg["act"]:
                        f(e)

                @block.vector
                def _(e, seg=seg):
                    for f in seg["dve"]:
                        f(e)

                @block.gpsimd
                def _(e, seg=seg):
                    for f in seg["pool"]:
                        f(e)

                @block.sync
                def _(e, seg=seg):
                    for f in seg["sp"]:
                        f(e)


import math
from contextlib import ExitStack
import numpy as np

D = 2048
DFF = 1536
NT = 256
EPS = 1e-6
TWO_PI = 2 * math.pi
CW1 = 6.28125
CW2 = float(np.float32(TWO_PI - CW1))
CW3 = float(TWO_PI - CW1 - CW2)
MAGIC = 12582912.0


class Cx:
    def __init__(self, nc, st):
        self.nc = nc
        self.st = st
        self.k = K(nc, st)
        self._n = 0
        self.stk = [st]
        self.banks = []
        for i in range(8):
            t = st.enter_context(nc.psum_tensor("bank%d" % i, [128, 512], F32))
            self.banks.append((t, Dep()))
        self._b = 0
        self.ones = self.sb([128, 128], BF16)
        self.dones = Dep()
        self.k.op("dve", lambda e: e.memset(self.ones[:], 1.0), writes=[self.dones])
        self.one32 = self.sb([128, 1], F32)
        self.done32 = Dep()
        self.k.op("dve", lambda e: e.memset(self.one32[:], 1.0), writes=[self.done32])

    def push(self, st):
        self.stk.append(st)

    def pop(self):
        self.stk.pop()
        self._stage_key = None

    def sb(self, shape, dt):
        self._n += 1
        return self.stk[-1].enter_context(self.nc.sbuf_tensor("t%d" % self._n, list(shape), dt))

    def rot(self, shape, dt, n):
        return Rot([(self.sb(shape, dt), Dep()) for _ in range(n)])

    def din(self, name, shape, dt):
        return self.nc.dram_tensor(name, list(shape), dt, kind="ExternalInput").ap()

    def dout(self, name, shape, dt):
        return self.nc.dram_tensor(name, list(shape), dt, kind="ExternalOutput").ap()

    def dscr(self, name, shape, dt):
        return self.nc.dram_tensor(name, list(shape), dt, kind="Internal").ap()

    def bank(self):
        b = self.banks[self._b % 8]
        self._b += 1
        return b

    def chain(self, M, N, pairs, reads, skip=False):
        bank, dep = self.bank()
        out = bank[0:M, 0:N]
        n = len(pairs)
        for i, (l, r) in enumerate(pairs):
            self.k.op("pe", lambda e, l=l, r=r, i=i: e.matmul(out, l, r, start=(i == 0), stop=(i == n - 1)),
                      reads=reads, writes=[dep], inc=(i == n - 1))
        return out, dep


class Rot:
    def __init__(self, items):
        self.items = items
        self.i = 0

    def next(self):
        it = self.items[self.i % len(self.items)]
        self.i += 1
        return it


def load_w(cx, w_ap, kc_n, F, gcol=None, dg=None, extra=None):
    k = cx.k
    W = cx.sb([128, kc_n, F], BF16)
    dW = deps(kc_n)
    wv = w_ap.rearrange("(kc p) f -> p kc f", p=128)
    key = id(cx.stk[-1])
    if getattr(cx, "_stage_key", None) != key:
        cx._stage_key = key
        cx._stage = cx.rot([128, 512], F32, 4)
        cx._stage_n = 0
    for kc in range(kc_n):
        for c0 in range(0, F, 512):
            w = min(512, F - c0)
            st, dst_ = cx._stage.next()
            n = cx._stage_n
            cx._stage_n += 1
            k.dma(("sp", "act")[n % 2], st[:, 0:w], wv[:, kc, c0:c0 + w], writes=[dst_])
            eng = ("dve", "act")[n % 2]
            out = W[:, kc, c0:c0 + w]
            rd = [dst_] + ([dg] if gcol is not None else [])
            if eng == "act":
                if gcol is not None:
                    k.op("act", lambda e, out=out, st=st, w=w, kc=kc: e.activation(out, st[:, 0:w], AF.Copy, scale=gcol[:, kc:kc + 1]), reads=rd, writes=[dW[kc]])
                else:
                    k.op("act", lambda e, out=out, st=st, w=w: e.activation(out, st[:, 0:w], AF.Copy), reads=rd, writes=[dW[kc]])
            else:
                if gcol is not None:
                    k.op(eng, lambda e, out=out, st=st, w=w, kc=kc: e.tensor_scalar(out, st[:, 0:w], gcol[:, kc:kc + 1], None, ALU.mult), reads=rd, writes=[dW[kc]])
                else:
                    k.op(eng, lambda e, out=out, st=st, w=w: e.tensor_copy(out, st[:, 0:w]), reads=rd, writes=[dW[kc]])
            if extra is not None:
                extra(kc, c0, w, st, dst_)
    return W, dW


def fold_g(cx, W, dW, gcol, dg, kc_n, eng="pool"):
    return


def load_small(cx, ap, shape, dt=F32, q="sp"):
    t = cx.sb(shape, dt)
    d = Dep()
    cx.k.dma(q, t[:], ap, writes=[d])
    return t, d


def rstd_from_ss(cx, ss_ap, dss, out, dout, n_feat, tmp, dtmp):
    k = cx.k
    k.op("dve", lambda e: e.tensor_scalar(tmp, ss_ap, 1.0 / n_feat, EPS, ALU.mult, ALU.add), reads=[dss], writes=[dtmp])
    k.op("act", lambda e: e.activation(tmp, tmp, AF.Sqrt), reads=[dtmp], writes=[dtmp])
    k.op("dve", lambda e: e.reciprocal(out, tmp), reads=[dtmp], writes=[dout])


def load_x_stats(cx, srcv, t0, nt, X, dX, SQ, dSQ, kc_n, rstd, drstd, tmp, dtmp, n_feat, dsrc, srcb=None):
    k = cx.k
    if srcb is not None:
        k.dma("sp", X[:, :, 0:nt], srcb.rearrange("(kc p) t -> p kc t", p=128)[:, :, t0:t0 + nt], reads=[dsrc], writes=[dX])
    else:
        k.dma("pool", X[:, :, 0:nt], srcv[:, :, t0:t0 + nt], reads=[dsrc], writes=[dX])
    k.op("pool", lambda e: e.tensor_tensor(SQ[:, :, 0:nt], X[:, :, 0:nt], X[:, :, 0:nt], ALU.mult), reads=[dX], writes=[dSQ])
    ss, dss = cx.chain(128, nt, [(cx.ones[:], SQ[:, kc, 0:nt]) for kc in range(kc_n)], [dSQ, cx.dones])
    rstd_from_ss(cx, ss, dss, rstd, drstd, n_feat, tmp, dtmp)


def emit_ffn(cx, T, src, gcol_ap, wg, wu, wd, dst, dsrc=None, ddst=None, dst2=None, srcb=None, dstb=None):
    k = cx.k
    dsrc = dsrc or Dep()
    ddst = ddst or Dep()
    with ExitStack() as ps:
        cx.push(ps)
        gcol, dg = load_small(cx, gcol_ap, [128, 16])
        Wg, dWg = load_w(cx, wg, 16, DFF, gcol, dg)
        Wu, dWu = load_w(cx, wu, 16, DFF, gcol, dg)
        Wd, dWd = load_w(cx, wd, 12, D)
        srcv = src.rearrange("(kc p) t -> p kc t", p=128)
        Xr = cx.rot([128, 16, NT], BF16, 2)
        SQr = cx.rot([128, 16, NT], BF16, 1)
        Hr = cx.rot([128, 12, NT], BF16, 2)
        rs_r = cx.rot([128, NT], F32, 2)
        tmp_r = cx.rot([128, NT], F32, 2)
        g1r = cx.rot([128, NT], F32, 2)
        u1r = cx.rot([128, NT], F32, 2)
        xr_r = cx.rot([128, NT], F32, 3)
        o_r = cx.rot([128, NT], F32, 3)
        ob_r = cx.rot([128, NT], BF16, 2)
        for tt in range(T // NT):
            t0 = tt * NT
            X, dX = Xr.next()
            SQ, dSQ = SQr.next()
            H, dH = Hr.next()
            rstd, drs = rs_r.next()
            tmp, dtmp = tmp_r.next()
            load_x_stats(cx, srcv, t0, NT, X, dX, SQ, dSQ, 16, rstd[:], drs, tmp[:], dtmp, D, dsrc, srcb)
            for fc in range(12):
                fs = slice(fc * 128, (fc + 1) * 128)
                pg, dpg = cx.chain(128, NT, [(Wg[:, kc, fs], X[:, kc, :]) for kc in range(16)], dWg + [dX])
                pu, dpu = cx.chain(128, NT, [(Wu[:, kc, fs], X[:, kc, :]) for kc in range(16)], dWu + [dX])
                g1, dg1 = g1r.next()
                u1, du1 = u1r.next()
                k.op("dve", lambda e, g1=g1, pg=pg, rstd=rstd: e.tensor_tensor(g1[:], pg, rstd[:], ALU.mult), reads=[dpg, drs], writes=[dg1])
                k.op("act", lambda e, g1=g1: e.activation(g1[:], g1[:], AF.Silu), reads=[dg1], writes=[dg1])
                k.op("dve", lambda e, u1=u1, pu=pu, rstd=rstd: e.tensor_tensor(u1[:], pu, rstd[:], ALU.mult), reads=[dpu, drs], writes=[du1])
                k.op("pool", lambda e, H=H, fc=fc, g1=g1, u1=u1: e.tensor_tensor(H[:, fc, :], g1[:], u1[:], ALU.mult), reads=[dg1, du1], writes=[dH])
            for dc in range(16):
                ds_ = slice(dc * 128, (dc + 1) * 128)
                py, dpy = cx.chain(128, NT, [(Wd[:, fc, ds_], H[:, fc, :]) for fc in range(12)], dWd + [dH])
                xr, dxr = xr_r.next()
                o, do = o_r.next()
                k.dma("sp", xr[:], src[ds_, t0:t0 + NT], reads=[dsrc], writes=[dxr])
                k.op("dve", lambda e, o=o, py=py, xr=xr: e.scalar_tensor_tensor(o[:], py, 0.5, xr[:], ALU.mult, ALU.add), reads=[dpy, dxr], writes=[do])
                k.dma("act", dst[ds_, t0:t0 + NT], o[:], reads=[do], writes=[ddst])
                if dst2 is not None:
                    k.dma("act", dst2[ds_, t0:t0 + NT], o[:], reads=[do])
                if dstb is not None:
                    ob16, dob16 = ob_r.next()
                    k.op("act", lambda e, ob16=ob16, o=o: e.activation(ob16[:], o[:], AF.Copy), reads=[do], writes=[dob16])
                    k.dma("act", dstb[ds_, t0:t0 + NT], ob16[:], reads=[dob16], writes=[ddst])
        cx.pop()
    k.cut()


def rope_tables(cx, posf, dposf, inv, dinv, P, nt, rk, sin_t, dsin, cos_t, dcos, ang, dang, kk, dkk):
    k = cx.k
    k.op("dve", lambda e: e.tensor_scalar(ang, posf, inv, None, ALU.mult), reads=[dposf, dinv], writes=[dang])
    k.op("dve", lambda e: e.tensor_scalar(kk, ang, 1.0 / TWO_PI, MAGIC, ALU.mult, ALU.add), reads=[dang], writes=[dkk])
    k.op("dve", lambda e: e.tensor_scalar(kk, kk, -MAGIC, None, ALU.add), reads=[dkk], writes=[dkk])
    for cc in (CW1, CW2, CW3):
        k.op("dve", lambda e, cc=cc: e.scalar_tensor_tensor(ang, kk, -cc, ang, ALU.mult, ALU.add), reads=[dkk, dang], writes=[dang])
    k.op("dve", lambda e: e.tensor_scalar(ang, ang, math.pi, -math.pi, ALU.min, ALU.max), reads=[dang], writes=[dang])
    k.op("act", lambda e: e.activation(sin_t, ang, AF.Sin), reads=[dang], writes=[dsin])
    k.op("act", lambda e: e.activation(kk, ang, AF.Abs), reads=[dang], writes=[dkk])
    k.op("dve", lambda e: e.tensor_scalar(kk, kk, -1.0, math.pi / 2, ALU.mult, ALU.add), reads=[dkk], writes=[dkk])
    k.op("act", lambda e: e.activation(cos_t, kk, AF.Sin), reads=[dkk], writes=[dcos])


class Long:
    def __init__(self, cx, names):
        self.t = {n: (cx.sb([128, NT], F32), Dep()) for n in names}

    def __getitem__(self, n):
        return self.t[n]


def pos_tables(cx, A, t0, posr, L, invc, dinv, P, sname, cname):
    k = cx.k
    pi_, dpi = posr.next()
    posf, dposf = L["posf"]
    k.dma("sp", pi_[:], A["posrep"][:, t0:t0 + NT], writes=[dpi])
    k.op("dve", lambda e: e.tensor_copy(posf[:], pi_[:]), reads=[dpi], writes=[dposf])
    sn, dsn = L[sname]; cs, dcs = L[cname]; ang, dang = L["ang"]; kk, dkk = L["kk"]
    rope_tables(cx, posf[0:P, :], dposf, invc[0:P, 0:1], dinv, P, NT, None, sn[0:P, :], dsn, cs[0:P, :], dcs, ang[0:P, :], dang, kk[0:P, :], dkk)


def emit_inproj_even_A(cx, T, src, dsrc, A, srcb=None):
    k = cx.k
    with ExitStack() as ps:
        cx.push(ps)
        gcol, dg = load_small(cx, A["g1col"], [128, 16])
        invA, dinvA = load_small(cx, A["invA"], [128, 1])
        w_in = A["w_in"]
        Wrot = cx.sb([128, 16, 1536], BF16)
        dWrot = deps(16)
        Wr5 = Wrot[:].rearrange("p kc (h two i) -> p kc h two i", two=2, i=64)

        def rot_extra(kc, c0, w, st, dst_):
            h0 = c0 // 128
            nh = w // 128
            sv = st[:, 0:w].rearrange("p (h two i) -> p h two i", two=2, i=64)
            k.op("dve", lambda e: e.tensor_scalar(Wr5[:, kc, h0:h0 + nh, 0, :], sv[:, :, 1, :], gcol[:, kc:kc + 1], -1.0, ALU.mult, ALU.mult), reads=[dst_, dg], writes=[dWrot[kc]])
            k.op("act", lambda e: e.activation(Wr5[:, kc, h0:h0 + nh, 1, :], sv[:, :, 0, :], AF.Copy, scale=gcol[:, kc:kc + 1]), reads=[dst_, dg], writes=[dWrot[kc]])

        Win, dWin = load_w(cx, w_in[:, 0:1536], 16, 1536, gcol, dg, extra=rot_extra)
        srcv = src.rearrange("(kc p) t -> p kc t", p=128)
        Xr = cx.rot([128, 16, NT], BF16, 2)
        SQr = cx.rot([128, 16, NT], BF16, 1)
        L = Long(cx, ["rstd", "tmp", "posf", "sinA", "cosA", "ang", "kk", "cq", "sq"])
        f32r = cx.rot([128, NT], F32, 6)
        obr = cx.rot([128, NT], BF16, 6)
        posr = cx.rot([128, NT], I32, 2)
        SA = 128 ** -0.5
        for tt in range(T // NT):
            t0 = tt * NT
            X, dX = Xr.next()
            SQ, dSQ = SQr.next()
            rstd, drs = L["rstd"]; tmp, dtmp = L["tmp"]
            load_x_stats(cx, srcv, t0, NT, X, dX, SQ, dSQ, 16, rstd[:], drs, tmp[:], dtmp, D, dsrc, srcb)
            pos_tables(cx, A, t0, posr, L, invA, dinvA, 128, "sinA", "cosA")
            sinA, dsinA = L["sinA"]; cosA, dcosA = L["cosA"]; cq, dcq = L["cq"]; sq_, dsq_ = L["sq"]
            k.op("dve", lambda e: e.scalar_tensor_tensor(cq[:], cosA[:], SA, rstd[:], ALU.mult, ALU.mult), reads=[dcosA, drs], writes=[dcq])
            k.op("dve", lambda e: e.scalar_tensor_tensor(sq_[:], sinA[:], SA, rstd[:], ALU.mult, ALU.mult), reads=[dsinA, drs], writes=[dsq_])
            k.op("dve", lambda e: e.tensor_tensor(cosA[:], cosA[:], rstd[:], ALU.mult), reads=[drs], writes=[dcosA])
            k.op("dve", lambda e: e.tensor_tensor(sinA[:], sinA[:], rstd[:], ALU.mult), reads=[drs], writes=[dsinA])
            for hh in range(12):
                cs = slice(hh * 128, (hh + 1) * 128)
                p1, dp1 = cx.chain(128, NT, [(Win[:, kc, cs], X[:, kc, :]) for kc in range(16)], dWin + [dX])
                p2, dp2 = cx.chain(128, NT, [(Wrot[:, kc, cs], X[:, kc, :]) for kc in range(16)], dWrot + [dX])
                ct, dct, stb, dst_ = (cq, dcq, sq_, dsq_) if hh < 6 else (cosA, dcosA, sinA, dsinA)
                t1, dt1 = f32r.next(); t2, dt2 = f32r.next()
                k.op("dve", lambda e, t1=t1, p1=p1, ct=ct: e.tensor_tensor(t1[:], p1, ct[:], ALU.mult), reads=[dp1, dct], writes=[dt1])
                k.op("dve", lambda e, t2=t2, p2=p2, stb=stb: e.tensor_tensor(t2[:], p2, stb[:], ALU.mult), reads=[dp2, dst_], writes=[dt2])
                ob, dob = obr.next()
                k.op("pool", lambda e, ob=ob, t1=t1, t2=t2: e.tensor_tensor(ob[:], t1[:], t2[:], ALU.add), reads=[dt1, dt2], writes=[dob])
                dstT = A["qaT"] if hh < 6 else A["kaT"]
                k.dma("act", dstT[hh % 6, :, t0:t0 + NT], ob[:], reads=[dob])
        cx.pop()
    k.cut()


def rstd_cols(cx, rstd, drs, colr, n_sub):
    out = []
    for s in range(n_sub):
        pc, dpc = cx.chain(128, 1, [(rstd[0:1, s * 128:(s + 1) * 128], cx.one32[0:1, 0:1])], [drs, cx.done32])
        rc, drc = colr.next()
        cx.k.op("dve", lambda e, rc=rc, pc=pc: e.tensor_copy(rc[:], pc), reads=[dpc], writes=[drc])
        out.append((rc, drc))
    return out


def emit_inproj_even_B(cx, T, src, dsrc, A, srcb=None):
    k = cx.k
    with ExitStack() as ps:
        cx.push(ps)
        gcol, dg = load_small(cx, A["g1col"], [128, 16])
        qn, dqn = load_small(cx, A["qncol"], [128, 3])
        kvn, dkvn = load_small(cx, A["kvncol"], [128, 2])
        invB, dinvB = load_small(cx, A["invB"], [128, 1])
        w_in = A["w_in"]
        OFF = 1536
        Win, dWin = load_w(cx, w_in[:, OFF:2976], 16, 2976 - OFF, gcol, dg)
        Wkr = cx.sb([128, 16, 96], BF16)
        Wkrr = cx.sb([128, 16, 96], BF16)
        dWkr = Dep()
        k.op("dve", lambda e: e.memset(Wkr[:], 0.0), writes=[dWkr])
        k.op("dve", lambda e: e.memset(Wkrr[:], 0.0), writes=[dWkr])
        wkv = w_in[:, 2944:2976].rearrange("(kc p) f -> p kc f", p=128)
        k.dma("pool", Wkr[:, :, 64:96], wkv, writes=[dWkr])
        k.dma("pool", Wkrr[:, :, 64:80], wkv[:, :, 16:32], writes=[dWkr])
        k.dma("pool", Wkrr[:, :, 80:96], wkv[:, :, 0:16], writes=[dWkr])
        for kc in range(16):
            k.op("pool", lambda e, kc=kc: e.tensor_scalar(Wkr[:, kc, 64:96], Wkr[:, kc, 64:96], gcol[:, kc:kc + 1], None, ALU.mult), reads=[dg], writes=[dWkr])
            k.op("pool", lambda e, kc=kc: e.tensor_scalar(Wkrr[:, kc, 64:80], Wkrr[:, kc, 64:80], gcol[:, kc:kc + 1], -1.0, ALU.mult, ALU.mult), reads=[dg], writes=[dWkr])
            k.op("pool", lambda e, kc=kc: e.tensor_scalar(Wkrr[:, kc, 80:96], Wkrr[:, kc, 80:96], gcol[:, kc:kc + 1], None, ALU.mult), reads=[dg], writes=[dWkr])
        Wq, dWq = load_w(cx, A["w_q_up"], 3, 384, qn, dqn)
        Wqr = cx.sb([128, 3, 384], BF16)
        dWqr = Dep()
        k.op("dve", lambda e: e.memset(Wqr[:], 0.0), writes=[dWqr])
        wq4 = A["w_q_up"].rearrange("(kc p) (h c) -> p kc h c", p=128, c=96)
        Wqr4 = Wqr[:].rearrange("p kc (h c) -> p kc h c", c=96)
        for kc in range(3):
            k.dma("pool", Wqr4[:, kc, :, 64:80], wq4[:, kc, :, 80:96], writes=[dWqr])
            k.dma("pool", Wqr4[:, kc, :, 80:96], wq4[:, kc, :, 64:80], writes=[dWqr])
            k.op("pool", lambda e, kc=kc: e.tensor_scalar(Wqr4[:, kc, :, 64:80], Wqr4[:, kc, :, 64:80], qn[:, kc:kc + 1], -1.0, ALU.mult, ALU.mult), reads=[dqn], writes=[dWqr])
            k.op("pool", lambda e, kc=kc: e.tensor_scalar(Wqr4[:, kc, :, 80:96], Wqr4[:, kc, :, 80:96], qn[:, kc:kc + 1], None, ALU.mult), reads=[dqn], writes=[dWqr])
        Wkv, dWkv = load_w(cx, A["w_kv_up"], 2, 768, kvn, dkvn)
        Wkv4 = Wkv[:].rearrange("p kc (h c) -> p kc h c", c=192)

        srcv = src.rearrange("(kc p) t -> p kc t", p=128)
        Xr = cx.rot([128, 16, NT], BF16, 2)
        SQr = cx.rot([128, 16, NT], BF16, 1)
        L = Long(cx, ["rstd", "tmp", "posf", "sinB", "cosB", "ang", "kk", "rstd2", "tmp2", "cB2", "sB2", "rstd3", "tmp3", "cB1", "sB1"])
        f32r = cx.rot([128, NT], F32, 6)
        obr = cx.rot([128, NT], BF16, 6)
        otr = cx.rot([128, 512], BF16, 3)
        colr = cx.rot([128, 1], F32, 8)
        cqr = cx.rot([128, 3, NT], BF16, 2)
        ckr = cx.rot([128, 2, NT], BF16, 2)
        sq2r = cx.rot([128, 3, NT], BF16, 2)
        posr = cx.rot([128, NT], I32, 2)
        SB_ = 96 ** -0.5
        CV, CQ, CKV = 0, 768, 1152
        for tt in range(T // NT):
            t0 = tt * NT
            X, dX = Xr.next()
            SQ, dSQ = SQr.next()
            rstd, drs = L["rstd"]; tmp, dtmp = L["tmp"]
            load_x_stats(cx, srcv, t0, NT, X, dX, SQ, dSQ, 16, rstd[:], drs, tmp[:], dtmp, D, dsrc, srcb)
            pos_tables(cx, A, t0, posr, L, invB, dinvB, 96, "sinB", "cosB")
            sinB, dsinB = L["sinB"]; cosB, dcosB = L["cosB"]
            rcols = rstd_cols(cx, rstd, drs, colr, NT // 128)
            for s in range(NT // 128):
                ts_ = slice(s * 128, (s + 1) * 128)
                rc, drc = rcols[s]
                for (f0, f1) in ((0, 512), (512, 768)):
                    pv, dpv = cx.chain(128, f1 - f0, [(X[:, kc, ts_], Win[:, kc, CV + f0:CV + f1]) for kc in range(16)], dWin + [dX])
                    ot, dot = otr.next()
                    k.op("act", lambda e, ot=ot, pv=pv, rc=rc, w=f1 - f0: e.activation(ot[:, 0:w], pv, AF.Copy, scale=rc[:, 0:1]), reads=[dpv, drc], writes=[dot])
                    k.dma("act", A["va"][t0 + s * 128:t0 + (s + 1) * 128, f0:f1], ot[:, 0:f1 - f0], reads=[dot])
            cqb, dcqb = cqr.next()
            sq2, dsq2 = sq2r.next()
            for c in range(3):
                pcq, dpcq = cx.chain(128, NT, [(Win[:, kc, CQ + c * 128:CQ + (c + 1) * 128], X[:, kc, :]) for kc in range(16)], dWin + [dX])
                k.op("dve", lambda e, c=c, cqb=cqb, pcq=pcq: e.tensor_tensor(cqb[:, c, :], pcq, rstd[:], ALU.mult), reads=[dpcq, drs], writes=[dcqb])
            k.op("pool", lambda e, sq2=sq2, cqb=cqb: e.tensor_tensor(sq2[:], cqb[:], cqb[:], ALU.mult), reads=[dcqb], writes=[dsq2])
            ss2, dss2 = cx.chain(128, NT, [(cx.ones[:], sq2[:, c, :]) for c in range(3)], [dsq2, cx.dones])
            rstd2, drs2 = L["rstd2"]; tmp2, dtmp2 = L["tmp2"]
            rstd_from_ss(cx, ss2, dss2, rstd2[:], drs2, 384, tmp2[:], dtmp2)
            cB2, dcB2 = L["cB2"]; sB2, dsB2 = L["sB2"]
            k.op("dve", lambda e: e.scalar_tensor_tensor(cB2[0:96, :], cosB[0:96, :], SB_, rstd2[0:96, :], ALU.mult, ALU.mult), reads=[dcosB, drs2], writes=[dcB2])
            k.op("dve", lambda e: e.scalar_tensor_tensor(sB2[0:96, :], sinB[0:96, :], SB_, rstd2[0:96, :], ALU.mult, ALU.mult), reads=[dsinB, drs2], writes=[dsB2])
            for h in range(4):
                cs = slice(h * 96, (h + 1) * 96)
                p1, dp1 = cx.chain(96, NT, [(Wq[:, c, cs], cqb[:, c, :]) for c in range(3)], dWq + [dcqb])
                p2, dp2 = cx.chain(96, NT, [(Wqr[:, c, cs], cqb[:, c, :]) for c in range(3)], [dWqr, dcqb])
                t1, dt1 = f32r.next(); t2, dt2 = f32r.next()
                k.op("dve", lambda e, t1=t1, p1=p1: e.tensor_tensor(t1[0:96, :], p1, cB2[0:96, :], ALU.mult), reads=[dp1, dcB2], writes=[dt1])
                k.op("dve", lambda e, t2=t2, p2=p2: e.tensor_tensor(t2[0:96, :], p2, sB2[0:96, :], ALU.mult), reads=[dp2, dsB2], writes=[dt2])
                ob, dob = obr.next()
                k.op("pool", lambda e, ob=ob, t1=t1, t2=t2: e.tensor_tensor(ob[0:96, :], t1[0:96, :], t2[0:96, :], ALU.add), reads=[dt1, dt2], writes=[dob])
                k.dma("act", A["qbT"][h, :, t0:t0 + NT], ob[0:96, :], reads=[dob])
            ckb, dckb = ckr.next()
            sq3, dsq3 = sq2r.next()
            for c in range(2):
                pck, dpck = cx.chain(128, NT, [(Win[:, kc, CKV + c * 128:CKV + (c + 1) * 128], X[:, kc, :]) for kc in range(16)], dWin + [dX])
                k.op("dve", lambda e, c=c, ckb=ckb, pck=pck: e.tensor_tensor(ckb[:, c, :], pck, rstd[:], ALU.mult), reads=[dpck, drs], writes=[dckb])
            k.op("pool", lambda e, sq3=sq3, ckb=ckb: e.tensor_tensor(sq3[:, 0:2, :], ckb[:], ckb[:], ALU.mult), reads=[dckb], writes=[dsq3])
            ss3, dss3 = cx.chain(128, NT, [(cx.ones[:], sq3[:, c, :]) for c in range(2)], [dsq3, cx.dones])
            rstd3, drs3 = L["rstd3"]; tmp3, dtmp3 = L["tmp3"]
            rstd_from_ss(cx, ss3, dss3, rstd3[:], drs3, 256, tmp3[:], dtmp3)
            for h in range(4):
                pk, dpk = cx.chain(64, NT, [(Wkv4[:, c, h, 0:64], ckb[:, c, :]) for c in range(2)], dWkv + [dckb])
                ob, dob = obr.next()
                k.op("dve", lambda e, ob=ob, pk=pk: e.tensor_tensor(ob[0:64, :], pk, rstd3[0:64, :], ALU.mult), reads=[dpk, drs3], writes=[dob])
                k.dma("act", A["kbT"][h, 0:64, t0:t0 + NT], ob[0:64, :], reads=[dob])
            rcols3 = rstd_cols(cx, rstd3, drs3, colr, NT // 128)
            for s in range(NT // 128):
                ts_ = slice(s * 128, (s + 1) * 128)
                rc, drc = rcols3[s]
                pv, dpv = cx.chain(128, 512, [(ckb[:, c, ts_], Wkv4[:, c, :, 64:192]) for c in range(2)], dWkv + [dckb])
                ot, dot = otr.next()
                k.op("act", lambda e, ot=ot, pv=pv, rc=rc: e.activation(ot[:], pv, AF.Copy, scale=rc[:, 0:1]), reads=[dpv, drc], writes=[dot])
                k.dma("act", A["vb"][t0 + s * 128:t0 + (s + 1) * 128, :], ot[:], reads=[dot])
            p1, dp1 = cx.chain(96, NT, [(Wkr[:, kc, :], X[:, kc, :]) for kc in range(16)], [dWkr, dX])
            p2, dp2 = cx.chain(96, NT, [(Wkrr[:, kc, :], X[:, kc, :]) for kc in range(16)], [dWkr, dX])
            cB1, dcB1 = L["cB1"]; sB1, dsB1 = L["sB1"]
            k.op("dve", lambda e: e.tensor_tensor(cB1[64:96, :], cosB[64:96, :], rstd[64:96, :], ALU.mult), reads=[dcosB, drs], writes=[dcB1])
            k.op("dve", lambda e: e.tensor_tensor(sB1[64:96, :], sinB[64:96, :], rstd[64:96, :], ALU.mult), reads=[dsinB, drs], writes=[dsB1])
            t1, dt1 = f32r.next(); t2, dt2 = f32r.next()
            k.op("dve", lambda e, t1=t1, p1=p1: e.tensor_tensor(t1[64:96, :], p1[64:96, :], cB1[64:96, :], ALU.mult), reads=[dp1, dcB1], writes=[dt1])
            k.op("dve", lambda e, t2=t2, p2=p2: e.tensor_tensor(t2[64:96, :], p2[64:96, :], sB1[64:96, :], ALU.mult), reads=[dp2, dsB1], writes=[dt2])
            ob, dob = obr.next()
            k.op("pool", lambda e, ob=ob, t1=t1, t2=t2: e.tensor_tensor(ob[64:96, :], t1[64:96, :], t2[64:96, :], ALU.add), reads=[dt1, dt2], writes=[dob])
            for h in range(4):
                k.dma("act", A["kbT"][h, 64:96, t0:t0 + NT], ob[64:96, :], reads=[dob])
        cx.pop()
    k.cut()


def norm2_bcast(cx, XT, dXT, P, c0, n, sqr):
    sq, dsq = sqr.next()
    cx.k.op("pool", lambda e: e.tensor_tensor(sq[0:P, 0:n], XT[0:P, c0:c0 + n], XT[0:P, c0:c0 + n], ALU.mult), reads=[dXT], writes=[dsq])
    return cx.chain(128, n, [(cx.ones[0:P, :], sq[0:P, 0:n])], [dsq, cx.dones])


def kmax2(cx, KT, dKT, P, Lk, sqr, f32r, km, dkm):
    k = cx.k
    first = True
    for c0 in range(0, Lk, 512):
        n = min(512, Lk - c0)
        pn, dpn = norm2_bcast(cx, KT, dKT, P, c0, n, sqr)
        if first:
            k.op("dve", lambda e, pn=pn: e.reduce_max(km[:], pn, AX.X), reads=[dpn], writes=[dkm])
            first = False
        else:
            t, dt = f32r.next()
            k.op("dve", lambda e, t=t, pn=pn: e.reduce_max(t[:, 0:1], pn, AX.X), reads=[dpn], writes=[dt])
            k.op("dve", lambda e, t=t: e.tensor_tensor(km[:], km[:], t[:, 0:1], ALU.max), reads=[dt], writes=[dkm])


def negm_rows(cx, QT, dQT, P, Lq, km, dkm, sqr, f32r, out_row, dout):
    k = cx.k
    for c0 in range(0, Lq, 512):
        n = min(512, Lq - c0)
        pn, dpn = norm2_bcast(cx, QT, dQT, P, c0, n, sqr)
        t, dt = f32r.next()
        k.op("act", lambda e, t=t, pn=pn, n=n: e.activation(t[0:1, 0:n], pn[0:1, :], AF.Sqrt, scale=km[0:1, 0:1]), reads=[dpn, dkm], writes=[dt])
        k.op("dve", lambda e, t=t, c0=c0, n=n: e.tensor_scalar(out_row[0:1, c0:c0 + n], t[0:1, 0:n], -1.0, None, ALU.mult), reads=[dt], writes=[dout])


def emit_mla_attn(cx, S, A):
    k = cx.k
    NB = S // 128
    with ExitStack() as ps:
        cx.push(ps)
        QT = cx.sb([128, S], BF16); dQT = Dep()
        KT = cx.sb([128, S], BF16); dKT = Dep()
        V = cx.sb([128, NB, 128], BF16); dV = Dep()
        MK = cx.sb([128, 4, 512], BF16); dMK = Dep()
        k.dma("sp", QT[0:96, :], A["qT"], writes=[dQT])
        k.dma("sp", KT[0:96, :], A["kT"], writes=[dKT])
        k.dma("sp", KT[96:97, :], A["onesrow"][0:1, 0:S], writes=[dKT])
        k.dma("sp", V[:], A["v"].rearrange("(nb p) d -> p nb d", p=128), writes=[dV])
        k.dma("sp", MK[:], A["masks"].rearrange("a p q -> p a q"), writes=[dMK])
        sqr = cx.rot([128, 512], BF16, 2)
        f32r = cx.rot([128, 512], F32, 3)
        km = cx.sb([128, 1], F32); dkm = Dep()
        negm = cx.sb([1, S], BF16); dnegm = Dep()
        kmax2(cx, KT, dKT, 96, S, sqr, f32r, km, dkm)
        negm_rows(cx, QT, dQT, 96, S, km, dkm, sqr, f32r, negm, dnegm)
        k.dma("sp", QT[96:97, :], negm[0:1, :], reads=[dnegm], writes=[dQT])
        ptr = cx.rot([128, 512], BF16, 4)
        recr = cx.rot([128, 512], F32, 2)
        outr = cx.rot([128, 512], BF16, 2)
        blocks = [(i, kb) for i in range(S // 512) for kb in range(4 * i + 4)]
        sb_i = 0
        pend = []

        def qk(i, kb, n):
            bank, dep = cx.banks[n % 4]
            out = bank[:, 0:512]
            k.op("pe", lambda e: e.matmul(out, KT[0:97, kb * 128:(kb + 1) * 128], QT[0:97, i * 512:(i + 1) * 512], start=True, stop=True),
                 reads=[dKT, dQT], writes=[dep])
            return out, dep

        def av(i, kb, sT, dsT):
            pt, dpt = ptr.next()
            k.op("act", lambda e: e.activation(pt[:], sT, AF.Exp), reads=[dsT], writes=[dpt])
            if kb >= 4 * i:
                a = kb - 4 * i
                k.op("pool", lambda e: e.tensor_tensor(pt[:], pt[:], MK[:, a, :], ALU.mult), reads=[dMK], writes=[dpt])
            ob, dob = cx.banks[4 + (i % 2)]
            db, ddb = cx.banks[6 + (i % 2)]
            first, last = (kb == 0), (kb == 4 * i + 3)
            k.op("pe", lambda e: e.matmul(ob[:, 0:512], V[:, kb, :], pt[:], start=first, stop=last), reads=[dV, dpt], writes=[dob], inc=False)
            k.op("pe", lambda e: e.matmul(db[:, 0:512], cx.ones[:], pt[:], start=first, stop=last), reads=[cx.dones, dpt], writes=[ddb])
            if last:
                rec, drec = recr.next()
                o, do = outr.next()
                k.op("dve", lambda e: e.reciprocal(rec[:], db[:, 0:512]), reads=[ddb], writes=[drec])
                k.op("dve", lambda e: e.tensor_tensor(o[:], ob[:, 0:512], rec[:], ALU.mult), reads=[dob, drec], writes=[do])
                k.dma("sp", A["oT"][:, i * 512:(i + 1) * 512], o[:], reads=[do])

        LOOK = 2
        for n, (i, kb) in enumerate(blocks):
            sT, dsT = qk(i, kb, n)
            pend.append((i, kb, sT, dsT))
            if len(pend) > LOOK:
                av(*pend.pop(0))
        while pend:
            av(*pend.pop(0))
        cx.pop()
    k.cut()


def emit_dil_attn(cx, T, A):
    k = cx.k
    HALO = 2048
    Lk = T + HALO
    NBK = Lk // 128
    with ExitStack() as ps:
        cx.push(ps)
        MK = cx.sb([128, 256], BF16); dMK = Dep()
        k.dma("sp", MK[:], A["dmask"], writes=[dMK])
        BK = cx.sb([2, Lk], BF16); dBK = Dep()
        k.dma("sp", BK[:], A["kbias2"], writes=[dBK])
        QB = cx.sb([2, T], BF16); dQB = Dep()
        KTr = cx.rot([128, Lk], BF16, 2)
        QTr = cx.rot([128, T], BF16, 2)
        Vr = cx.rot([128, 3, NBK, 128], BF16, 2)
        ACCr = cx.rot([128, 2, T], F32, 1)
        sqr = cx.rot([128, 512], BF16, 2)
        f32r = cx.rot([128, 512], F32, 3)
        kmr = cx.rot([128, 1], F32, 2)
        ptr = cx.rot([128, 256], BF16, 4)
        recr = cx.rot([128, T], F32, 1)
        outr = cx.rot([128, T], BF16, 2)
        def head(h):
            KT, dKT = KTr.next(); QT, dQT = QTr.next(); V, dV = Vr.next(); ACC, dACC = ACCr.next()
            k.dma("sp", KT[:], A["kaT"][h], writes=[dKT])
            k.dma("sp", QT[:], A["qaT"][h], writes=[dQT])
            for p in range(3):
                k.dma("sp", V[:, p, :, :], A["vreg"][p, h], writes=[dV])
            km, dkm = kmr.next()
            kmax2(cx, KT, dKT, 128, Lk, sqr, f32r, km, dkm)
            k.dma("sp", QB[1:2, :], A["onesrow"][0:1, 0:T], writes=[dQB])
            negm_rows(cx, QT, dQT, 128, T, km, dkm, sqr, f32r, QB, dQB)
            units = []
            for p, dil in enumerate((1, 4, 16)):
                nb = Lk // (128 * dil)
                nbh = HALO // (128 * dil)
                for r in range(dil):
                    for n in range(nbh, nb):
                        units.append((p, dil, nb, nbh, r, n))
            pend = []
            cnt = [0]

            def qk(u):
                p, dil, nb, nbh, r, n = u
                bank, dep = cx.banks[cnt[0] % 4]
                cnt[0] += 1
                q0 = (n - nbh) * 128 * dil + r
                qs = slice(q0, q0 + 127 * dil + 1, dil)
                for half, nn in enumerate((n - 1, n)):
                    k0 = nn * 128 * dil + r
                    ks = slice(k0, k0 + 127 * dil + 1, dil)
                    out = bank[:, half * 128:(half + 1) * 128]
                    k.op("pe", lambda e, out=out, ks=ks: e.matmul(out, KT[:, ks], QT[:, qs], start=True, stop=False), reads=[dKT, dQT], writes=[dep], inc=False)
                    k.op("pe", lambda e, out=out, ks=ks: e.matmul(out, BK[0:2, ks], QB[0:2, qs], start=False, stop=True), reads=[dBK, dQB], writes=[dep], inc=(half == 1))
                return (u, bank, dep, qs)

            def av(u, bank, dep, qs):
                p, dil, nb, nbh, r, n = u
                pt, dpt = ptr.next()
                k.op("act", lambda e: e.activation(pt[:], bank[:, 0:256], AF.Exp), reads=[dep], writes=[dpt])
                k.op("pool", lambda e: e.tensor_tensor(pt[:], pt[:], MK[:], ALU.mult), reads=[dMK], writes=[dpt])
                ob, dob = cx.banks[4 + (cnt[0] % 4)]
                for half, nn in enumerate((n - 1, n)):
                    k.op("pe", lambda e, half=half, nn=nn: e.matmul(ob[:, 0:128], V[:, p, r * nb + nn, :], pt[:, half * 128:(half + 1) * 128], start=(half == 0), stop=(half == 1)),
                         reads=[dV, dpt], writes=[dob], inc=False)
                for half in range(2):
                    k.op("pe", lambda e, half=half: e.matmul(ob[:, 128:256], cx.ones[:], pt[:, half * 128:(half + 1) * 128], start=(half == 0), stop=(half == 1)),
                         reads=[cx.dones, dpt], writes=[dob], inc=(half == 1))
                src3 = ob[:, 0:256].rearrange("p (a q) -> p a q", a=2)
                if p == 0:
                    k.op("dve", lambda e: e.tensor_copy(ACC[:, :, qs], src3), reads=[dob], writes=[dACC])
                else:
                    k.op("dve", lambda e: e.tensor_tensor(ACC[:, :, qs], src3, ACC[:, :, qs], ALU.add), reads=[dob], writes=[dACC])

            for u in units:
                pend.append(qk(u))
                if len(pend) > 2:
                    av(*pend.pop(0))
            while pend:
                av(*pend.pop(0))
            rec, drec = recr.next()
            o, do = outr.next()
            k.op("dve", lambda e, rec=rec, ACC=ACC: e.reciprocal(rec[:], ACC[:, 1, :]), reads=[dACC], writes=[drec])
            k.op("dve", lambda e, o=o, rec=rec, ACC=ACC: e.tensor_tensor(o[:], ACC[:, 0, :], rec[:], ALU.mult), reads=[dACC, drec], writes=[do])
            k.dma("sp", A["oaT"][h], o[:], reads=[do])

        for h in range(6):
            head(h)
        cx.pop()
    k.cut()


def outproj_tail(cx, T, t0, KC, W, dW, M, dM, hin, dhin, hout, dhout, xr_r, o_r, houtb=None, ob_r=None):
    k = cx.k
    for dc in range(16):
        ds_ = slice(dc * 128, (dc + 1) * 128)
        py, dpy = cx.chain(128, NT, [(W[:, c, ds_], M[:, c, :]) for c in range(KC)], dW + [dM])
        xr, dxr = xr_r.next()
        o, do = o_r.next()
        k.dma("sp", xr[:], hin[ds_, t0:t0 + NT], reads=[dhin], writes=[dxr])
        k.op("dve", lambda e, o=o, py=py, xr=xr: e.tensor_tensor(o[:], py, xr[:], ALU.add), reads=[dpy, dxr], writes=[do])
        k.dma("act", hout[ds_, t0:t0 + NT], o[:], reads=[do], writes=[dhout])
        if houtb is not None:
            ob16, dob16 = ob_r.next()
            k.op("act", lambda e, ob16=ob16, o=o: e.activation(ob16[:], o[:], AF.Copy), reads=[do], writes=[dob16])
            k.dma("act", houtb[ds_, t0:t0 + NT], ob16[:], reads=[dob16], writes=[dhout])


def emit_outproj_even(cx, T, mT, w_out, hin, dhin, hout, dhout, houtb=None):
    k = cx.k
    KC = 10
    with ExitStack() as ps:
        cx.push(ps)
        W, dW = load_w(cx, w_out, KC, D)
        Mr = cx.rot([128, KC, NT], BF16, 2)
        xr_r = cx.rot([128, NT], F32, 4)
        o_r = cx.rot([128, NT], F32, 4)
        ob_r = cx.rot([128, NT], BF16, 4)
        mv = mT.rearrange("(c p) t -> p c t", p=128)
        for tt in range(T // NT):
            t0 = tt * NT
            M, dM = Mr.next()
            k.dma("sp", M[:], mv[:, :, t0:t0 + NT], writes=[dM])
            outproj_tail(cx, T, t0, KC, W, dW, M, dM, hin, dhin, hout, dhout, xr_r, o_r, houtb, ob_r)
        cx.pop()
    k.cut()


def emit_inproj_odd(cx, T, src, dsrc, A, srcb=None):
    k = cx.k
    with ExitStack() as ps:
        cx.push(ps)
        gcol, dg = load_small(cx, A["g1col"], [128, 16])
        Win, dWin = load_w(cx, A["w_in"], 16, 2048, gcol, dg)
        srcv = src.rearrange("(kc p) t -> p kc t", p=128)
        Xr = cx.rot([128, 16, NT], BF16, 2)
        SQr = cx.rot([128, 16, NT], BF16, 1)
        L = Long(cx, ["rstd", "tmp"])
        f32r = cx.rot([128, NT], F32, 4)
        obr = cx.rot([128, NT], BF16, 6)
        otr = cx.rot([128, 512], BF16, 3)
        colr = cx.rot([128, 1], F32, 8)
        SA = 128 ** -0.5
        for tt in range(T // NT):
            t0 = tt * NT
            X, dX = Xr.next()
            SQ, dSQ = SQr.next()
            rstd, drs = L["rstd"]; tmp, dtmp = L["tmp"]
            load_x_stats(cx, srcv, t0, NT, X, dX, SQ, dSQ, 16, rstd[:], drs, tmp[:], dtmp, D, dsrc, srcb)
            for c in range(12):
                cs = slice(c * 128, (c + 1) * 128)
                p1, dp1 = cx.chain(128, NT, [(Win[:, kc, cs], X[:, kc, :]) for kc in range(16)], dWin + [dX])
                if c < 4:
                    t1, dt1 = f32r.next()
                    k.op("dve", lambda e, t1=t1, p1=p1: e.tensor_tensor(t1[:], p1, rstd[:], ALU.mult), reads=[dp1, drs], writes=[dt1])
                    k.dma("act", A["uT"][cs, t0:t0 + NT], t1[:], reads=[dt1])
                else:
                    ob, dob = obr.next()
                    sc = SA if c < 8 else 1.0
                    k.op("dve", lambda e, ob=ob, p1=p1, sc=sc: e.scalar_tensor_tensor(ob[:], p1, sc, rstd[:], ALU.mult, ALU.mult), reads=[dp1, drs], writes=[dob])
                    dstT = A["qdT"] if c < 8 else A["kdT"]
                    k.dma("act", dstT[c % 4, :, t0:t0 + NT], ob[:], reads=[dob])
            rcols = rstd_cols(cx, rstd, drs, colr, NT // 128)
            for s in range(NT // 128):
                ts_ = slice(s * 128, (s + 1) * 128)
                rc, drc = rcols[s]
                pv, dpv = cx.chain(128, 512, [(X[:, kc, ts_], Win[:, kc, 1536:2048]) for kc in range(16)], dWin + [dX])
                ot, dot = otr.next()
                k.op("act", lambda e, ot=ot, pv=pv, rc=rc: e.activation(ot[:], pv, AF.Copy, scale=rc[:, 0:1]), reads=[dpv, drc], writes=[dot])
                k.dma("act", A["vd"][t0 + s * 128:t0 + (s + 1) * 128, :], ot[:], reads=[dot])
        cx.pop()
    k.cut()


def emit_outproj_odd(cx, T, A, hin, dhin, hout, dhout, houtb=None):
    k = cx.k
    KC = 8
    HL = 16
    with ExitStack() as ps:
        cx.push(ps)
        W, dW = load_w(cx, A["w_out"], KC, D)
        PW, dPW = load_w(cx, A["pool_w"], 1, 512)
        psc, dpsc = load_small(cx, A["pscol"], [128, 4])
        Mr = cx.rot([128, KC, NT], BF16, 2)
        Ur = cx.rot([128, 4, NT + HL], F32, 2)
        S1r = cx.rot([128, 4, NT + HL], F32, 1)
        S2r = cx.rot([128, 4, NT + HL], F32, 1)
        ICr = cx.rot([128, 4, NT], F32, 2)
        PBr = cx.rot([128, 4, NT], BF16, 2)
        xr_r = cx.rot([128, NT], F32, 4)
        o_r = cx.rot([128, NT], F32, 4)
        ob_r = cx.rot([128, NT], BF16, 4)
        uv = A["uTh"].rearrange("(g p) t -> p g t", p=128)
        ov = A["odT"].rearrange("(c p) t -> p c t", p=128)
        for tt in range(T // NT):
            t0 = tt * NT
            M, dM = Mr.next()
            U, dU = Ur.next(); S1, dS1 = S1r.next(); S2, dS2 = S2r.next(); IC, dIC = ICr.next(); PB, dPB = PBr.next()
            k.dma("sp", M[:, 4:8, :], ov[:, :, t0:t0 + NT], writes=[dM])
            k.dma("sp", U[:], uv[:, :, t0:t0 + NT + HL], writes=[dU])
            k.dma("sp", IC[:], A["invcnt"][:, :, t0:t0 + NT], writes=[dIC])
            W_ = NT + HL
            src_t, dsrc_t = U, dU
            cur, dcur = None, None
            bufs = [(S1, dS1), (S2, dS2)]
            sh = 1
            for lvl in range(4):
                dstt, ddst = bufs[lvl % 2]
                g0 = lvl
                a, da = (U, dU) if lvl == 0 else bufs[(lvl - 1) % 2]
                k.op("dve", lambda e, dstt=dstt, a=a, g0=g0, sh=sh: e.tensor_tensor(dstt[:, g0:4, sh:W_], a[:, g0:4, sh:W_], a[:, g0:4, 0:W_ - sh], ALU.add),
                     reads=[da], writes=[ddst])
                t1 = dstt
                k.op("pool", lambda e, t1=t1, lvl=lvl, IC=IC: e.tensor_tensor(t1[:, lvl, HL:W_], t1[:, lvl, HL:W_], IC[:, lvl, :], ALU.mult), reads=[dIC], writes=[ddst])
                k.op("pool", lambda e, t1=t1, lvl=lvl, PB=PB, U=U: e.tensor_tensor(PB[:, lvl, :], t1[:, lvl, HL:W_], U[:, lvl, HL:W_], ALU.subtract), reads=[ddst, dU], writes=[dPB])
                sh *= 2
            for g in range(4):
                pm, dpm = cx.chain(128, NT, [(PW[:, 0, g * 128:(g + 1) * 128], PB[:, g, :])], dPW + [dPB])
                k.op("act", lambda e, M=M, g=g, pm=pm: e.activation(M[:, g, :], pm, AF.Copy, scale=psc[:, g:g + 1]), reads=[dpm, dpsc], writes=[dM])
            outproj_tail(cx, T, t0, KC, W, dW, M, dM, hin, dhin, hout, dhout, xr_r, o_r, houtb, ob_r)
        cx.pop()
    k.cut()


def emit_final_norm(cx, T, hin, dhin, gcol_ap, out):
    k = cx.k
    with ExitStack() as ps:
        cx.push(ps)
        gcol, dg = load_small(cx, gcol_ap, [128, 16])
        hv = hin.rearrange("(kc p) t -> p kc t", p=128)
        ov = out.rearrange("(kc p) t -> p kc t", p=128)
        Xr = cx.rot([128, 16, NT], F32, 2)
        SQr = cx.rot([128, 16, NT], BF16, 1)
        Or = cx.rot([128, 16, NT], F32, 2)
        L = Long(cx, ["rstd", "tmp"])
        for tt in range(T // NT):
            t0 = tt * NT
            X, dX = Xr.next(); SQ, dSQ = SQr.next(); O, dO = Or.next()
            rstd, drs = L["rstd"]; tmp, dtmp = L["tmp"]
            k.dma("sp", X[:], hv[:, :, t0:t0 + NT], reads=[dhin], writes=[dX])
            k.op("pool", lambda e, SQ=SQ, X=X: e.tensor_tensor(SQ[:], X[:], X[:], ALU.mult), reads=[dX], writes=[dSQ])
            ss, dss = cx.chain(128, NT, [(cx.ones[:], SQ[:, kc, :]) for kc in range(16)], [dSQ, cx.dones])
            rstd_from_ss(cx, ss, dss, rstd[:], drs, D, tmp[:], dtmp)
            for kc in range(16):
                k.op("dve", lambda e, O=O, X=X, kc=kc: e.scalar_tensor_tensor(O[:, kc, :], X[:, kc, :], gcol[:, kc:kc + 1], rstd[:], ALU.mult, ALU.mult), reads=[dX, dg, drs], writes=[dO])
            k.dma("act", ov[:, :, t0:t0 + NT], O[:], reads=[dO])
        cx.pop()
    k.cut()


def emit_sb_attn(cx, S, A):
    k = cx.k
    NB = S // 128
    with ExitStack() as ps:
        cx.push(ps)
        QT = cx.sb([128, S], BF16); dQT = Dep()
        KT = cx.sb([128, S], BF16); dKT = Dep()
        V = cx.sb([128, NB, 128], BF16); dV = Dep()
        MK = cx.sb([128, 4, 512], BF16); dMK = Dep()
        TRI = cx.sb([128, 2, 128], BF16); dTRI = Dep()
        k.dma("sp", QT[:], A["qT"], writes=[dQT])
        k.dma("sp", KT[:], A["kT"], writes=[dKT])
        k.dma("sp", V[:], A["v"].rearrange("(nb p) d -> p nb d", p=128), writes=[dV])
        k.dma("sp", MK[:], A["masks"].rearrange("a p q -> p a q"), writes=[dMK])
        k.dma("sp", TRI[:], A["tri"].rearrange("a p q -> p a q"), writes=[dTRI])
        exr = cx.rot([128, 512], F32, 2)
        spr = cx.rot([128, 512], BF16, 5)
        zsr = cx.rot([128, 512], F32, 4)
        ebr = cx.rot([128, 512], F32, 2)
        ar = cx.rot([128, 512], BF16, 5)
        outr = cx.rot([128, 512], BF16, 2)
        blocks = [(i, kb) for i in range(S // 512) for kb in range(4 * i + 3, -1, -1)]
        N = len(blocks)
        st = [dict() for _ in range(N)]

        def sZ(n):
            i, kb = blocks[n]
            bank, dep = cx.banks[n % 3]
            z = bank[:, 0:512]
            k.op("pe", lambda e: e.matmul(z, KT[:, kb * 128:(kb + 1) * 128], QT[:, i * 512:(i + 1) * 512], start=True, stop=True), reads=[dKT, dQT], writes=[dep])
            st[n]["z"] = (z, dep)

        def sSP(n):
            i, kb = blocks[n]
            z, dz = st[n]["z"]
            ex, dex = exr.next(); sp, dsp = spr.next(); zs, dzs = zsr.next()
            k.op("dve", lambda e: e.tensor_copy(zs[:], z), reads=[dz], writes=[dzs])
            k.op("act", lambda e: e.activation(ex[:], zs[:], AF.Exp), reads=[dzs], writes=[dex])
            k.op("act", lambda e: e.activation(sp[:], ex[:], AF.Ln, bias=1.0), reads=[dex], writes=[dsp])
            if kb >= 4 * i:
                a = kb - 4 * i
                k.op("pool", lambda e: e.tensor_tensor(sp[:], sp[:], MK[:, a, :], ALU.mult), reads=[dMK], writes=[dsp])
            st[n]["sp"] = (sp, dsp); st[n]["zs"] = (zs, dzs)

        def sTI(n):
            i, kb = blocks[n]
            sp, dsp = st[n]["sp"]
            cb, dcb = cx.banks[3 + (i % 2)]
            k.op("pe", lambda e: e.matmul(cb[:, 0:512], TRI[:, 0, :], sp[:], start=(kb == 4 * i + 3), stop=False, skip_group_check=True), reads=[dTRI, dsp], writes=[dcb])

        def sE(n):
            i, kb = blocks[n]
            zs, dzs = st[n]["zs"]
            cb, dcb = cx.banks[3 + (i % 2)]
            eb, deb = ebr.next(); a_, da_ = ar.next()
            k.op("dve", lambda e: e.scalar_tensor_tensor(eb[:], cb[:, 0:512], -1.0, zs[:], ALU.mult, ALU.add), reads=[dcb, dzs], writes=[deb])
            k.op("act", lambda e: e.activation(a_[:], eb[:], AF.Exp), reads=[deb], writes=[da_])
            if kb >= 4 * i:
                a = kb - 4 * i
                k.op("pool", lambda e: e.tensor_tensor(a_[:], a_[:], MK[:, a, :], ALU.mult), reads=[dMK], writes=[da_])
            st[n]["a"] = (a_, da_)

        def sTR(n):
            i, kb = blocks[n]
            sp, dsp = st[n]["sp"]
            cb, dcb = cx.banks[3 + (i % 2)]
            k.op("pe", lambda e: e.matmul(cb[:, 0:512], TRI[:, 1, :], sp[:], start=False, stop=(kb == 0), skip_group_check=True), reads=[dTRI, dsp], writes=[dcb])

        def sAV(n):
            i, kb = blocks[n]
            a_, da_ = st[n]["a"]
            ob, dob = cx.banks[5 + (i % 2)]
            k.op("pe", lambda e: e.matmul(ob[:, 0:512], V[:, kb, :], a_[:], start=(kb == 4 * i + 3), stop=(kb == 0)), reads=[dV, da_], writes=[dob])
            if kb == 0:
                o, do = outr.next()
                k.op("dve", lambda e: e.tensor_copy(o[:], ob[:, 0:512]), reads=[dob], writes=[do])
                k.dma("sp", A["oT"][:, i * 512:(i + 1) * 512], o[:], reads=[do])
            st[n].clear()

        for t in range(N + 4):
            if t < N:
                sZ(t)
            if 0 <= t - 1 < N:
                sSP(t - 1)
            if 0 <= t - 3 < N:
                sTR(t - 3)
            if 0 <= t - 2 < N:
                sTI(t - 2)
                sE(t - 2)
            if 0 <= t - 4 < N:
                sAV(t - 4)
        cx.pop()
    k.cut()


import numpy as np
import ml_dtypes
BF = ml_dtypes.bfloat16

def mla_masks():
    kk = np.arange(128)[:, None]; q = np.arange(512)[None, :]
    return np.stack([((a * 128 + kk) <= q) for a in range(4)]).astype(BF)

def sb_masks():
    kk = np.arange(128)[:, None]; q = np.arange(512)[None, :]
    return np.stack([((a * 128 + kk) < q) for a in range(4)]).astype(BF)

def dil_mask():
    kk = np.arange(128)[:, None]; q = np.arange(128)[None, :]
    return np.concatenate([(kk >= q), (kk <= q)], axis=1).astype(BF)

def vreg_layout(v_h, Lk):
    out = []
    for dil in (1, 4, 16):
        nb = Lk // (128 * dil)
        t = v_h.reshape(6, nb, 128, dil, 128)
        t = t.transpose(0, 2, 3, 1, 4).reshape(6, 128, dil * nb, 128)
        out.append(t)
    return np.ascontiguousarray(np.stack(out))

def tri_mats():
    j = np.arange(128)[:, None]; s = np.arange(128)[None, :]
    return np.stack([(j >= s), (j < s)]).astype(BF)


def _colT(v, n):
    return np.ascontiguousarray(np.asarray(v, np.float32).reshape(n, 128).T)


def _inv_cols():
    invA = (10000.0 ** (-np.arange(0, 128, 2, dtype=np.float32) / 128)).astype(np.float32)
    invA = np.concatenate([invA, invA]).reshape(128, 1)
    inv16 = (10000.0 ** (-np.arange(0, 32, 2, dtype=np.float32) / 32)).astype(np.float32)
    invB = np.zeros((128, 1), np.float32)
    invB[64:80, 0] = inv16
    invB[80:96, 0] = inv16
    return invA, invB


def _ffn_ins(cx, tag):
    return (cx.din("g_" + tag, [128, 16], F32), cx.din("wg_" + tag, [D, DFF], F32), cx.din("wu_" + tag, [D, DFF], F32), cx.din("wd_" + tag, [DFF, D], F32))


def _ffn_vals(tag, g, wg, wu, wd):
    return {"g_" + tag: _colT(g, 16), "wg_" + tag: np.ascontiguousarray(wg), "wu_" + tag: np.ascontiguousarray(wu), "wd_" + tag: np.ascontiguousarray(wd)}


def build_L1(T):
    nc = bass.Bass("TRN2", target_bir_lowering=False)
    with ExitStack() as st:
        cx = Cx(nc, st)
        xT = cx.din("xT", [D, T], F32)
        f = _ffn_ins(cx, "a")
        h1o = cx.dout("h1T", [D, T], F32)
        h1T = cx.dscr("h1s", [D, T], F32)
        dh = Dep()
        h1b = cx.dscr("h1b", [D, T], BF16)
        emit_ffn(cx, T, xT, f[0], f[1], f[2], f[3], h1T, ddst=dh, dst2=h1o)
        A = {"g1col": cx.din("g1col", [128, 16], F32), "qncol": cx.din("qncol", [128, 3], F32), "kvncol": cx.din("kvncol", [128, 2], F32),
             "invA": cx.din("invA", [128, 1], F32), "invB": cx.din("invB", [128, 1], F32), "w_in": cx.din("w_in", [D, 2976], F32),
             "w_q_up": cx.din("w_q_up", [384, 384], F32), "w_kv_up": cx.din("w_kv_up", [256, 768], F32), "posrep": cx.din("posrep", [128, T], I32),
             "qaT": cx.dout("qaT", [6, 128, T], BF16), "kaT": cx.dout("kaT", [6, 128, T], BF16), "va": cx.dout("va", [T, 768], BF16),
             "qbT": cx.dout("qbT", [4, 96, T], BF16), "kbT": cx.dout("kbT", [4, 96, T], BF16), "vb": cx.dout("vb", [T, 512], BF16)}
        emit_inproj_even_A(cx, T, h1T, dh, A)
        emit_inproj_even_B(cx, T, h1T, dh, A)
        cx.k.finish()
    return nc


def build_L2(S, T):
    nc = bass.Bass("TRN2", target_bir_lowering=False)
    Lk = T + 2048
    with ExitStack() as st:
        cx = Cx(nc, st)
        A = {"qT": cx.din("qT", [96, S], BF16), "kT": cx.din("kT", [96, S], BF16), "v": cx.din("v", [S, 128], BF16),
             "onesrow": cx.din("onesrow", [1, S], BF16), "masks": cx.din("masks", [4, 128, 512], BF16), "oT": cx.dout("oT", [128, S], BF16)}
        emit_mla_attn(cx, S, A)
        B = {"qaT": cx.din("qaT", [6, 128, T], BF16), "kaT": cx.din("kaT", [6, 128, Lk], BF16), "vreg": cx.din("vreg", [3, 6, 128, Lk // 128, 128], BF16),
             "dmask": cx.din("dmask", [128, 256], BF16), "kbias2": cx.din("kbias2", [2, Lk], BF16), "onesrow": A["onesrow"], "oaT": cx.dout("oaT", [6, 128, T], BF16)}
        emit_dil_attn(cx, T, B)
        cx.k.finish()
    return nc


def build_L3(T):
    nc = bass.Bass("TRN2", target_bir_lowering=False)
    with ExitStack() as st:
        cx = Cx(nc, st)
        mT = cx.din("mT", [1280, T], BF16)
        w_out = cx.din("w_out", [1280, D], F32)
        h1T = cx.din("h1T", [D, T], F32)
        h2T = cx.dscr("h2T", [D, T], F32); d2 = Dep()
        h3T = cx.dscr("h3T", [D, T], F32); d3 = Dep()
        h4o = cx.dout("h4T", [D, T], F32)
        h4T = cx.dscr("h4s", [D, T], F32); d4 = Dep()
        h2b = cx.dscr("h2b", [D, T], BF16); h3b = cx.dscr("h3b", [D, T], BF16); h4b = cx.dscr("h4b", [D, T], BF16)
        emit_outproj_even(cx, T, mT, w_out, h1T, Dep(), h2T, d2)
        f = _ffn_ins(cx, "b")
        emit_ffn(cx, T, h2T, f[0], f[1], f[2], f[3], h3T, dsrc=d2, ddst=d3)
        f = _ffn_ins(cx, "c")
        emit_ffn(cx, T, h3T, f[0], f[1], f[2], f[3], h4T, dsrc=d3, ddst=d4, dst2=h4o)
        A = {"g1col": cx.din("g1col", [128, 16], F32), "w_in": cx.din("w_in", [D, 2048], F32),
             "uT": cx.dout("uT", [512, T], F32), "qdT": cx.dout("qdT", [4, 128, T], BF16), "kdT": cx.dout("kdT", [4, 128, T], BF16), "vd": cx.dout("vd", [T, 512], BF16)}
        emit_inproj_odd(cx, T, h4T, d4, A)
        cx.k.finish()
    return nc


def build_L4(S):
    nc = bass.Bass("TRN2", target_bir_lowering=False)
    with ExitStack() as st:
        cx = Cx(nc, st)
        A = {"qT": cx.din("qT", [128, S], BF16), "kT": cx.din("kT", [128, S], BF16), "v": cx.din("v", [S, 128], BF16),
             "masks": cx.din("masks", [4, 128, 512], BF16), "tri": cx.din("tri", [2, 128, 128], BF16), "oT": cx.dout("oT", [128, S], BF16)}
        emit_sb_attn(cx, S, A)
        cx.k.finish()
    return nc


def build_L5(T):
    nc = bass.Bass("TRN2", target_bir_lowering=False)
    with ExitStack() as st:
        cx = Cx(nc, st)
        A = {"w_out": cx.din("w_out", [1024, D], F32), "pool_w": cx.din("pool_w", [128, 512], F32), "pscol": cx.din("pscol", [128, 4], F32),
             "uTh": cx.din("uTh", [512, T + 16], F32), "odT": cx.din("odT", [512, T], BF16), "invcnt": cx.din("invcnt", [128, 4, T], F32)}
        h4T = cx.din("h4T", [D, T], F32)
        h5T = cx.dscr("h5T", [D, T], F32); d5 = Dep()
        h6T = cx.dscr("h6T", [D, T], F32); d6 = Dep()
        outT = cx.dout("outT", [D, T], F32)
        h5b = cx.dscr("h5b", [D, T], BF16)
        emit_outproj_odd(cx, T, A, h4T, Dep(), h5T, d5)
        f = _ffn_ins(cx, "d")
        emit_ffn(cx, T, h5T, f[0], f[1], f[2], f[3], h6T, dsrc=d5, ddst=d6)
        emit_final_norm(cx, T, h6T, d6, cx.din("gfin", [128, 16], F32), outT)
        cx.k.finish()
    return nc


def _run(nc, ims):
    res = run_bass_kernel_spmd(nc, ims, core_ids=list(range(8)))
    return res.results


TH = 4096


def kernel(x, positions, norm_g, ffn_w_gate, ffn_w_up, ffn_w_down, even_w_in, even_q_norm, even_w_q_up,
           even_kv_norm, even_w_kv_up, even_w_out, odd_w_in, odd_pool_w, odd_pool_scale, odd_w_out, final_norm):
    x = np.asarray(x)
    Bn, S, _ = x.shape
    T = S // 4
    Th = min(TH, T)
    NS = S // Th
    shards = [(b, jj) for b in range(Bn) for jj in range(NS)]
    rounds = [shards[i:i + 8] for i in range(0, len(shards), 8)]
    Lk = T + 2048
    positions = np.asarray(positions)
    norm_g = np.asarray(norm_g); wg = np.asarray(ffn_w_gate); wu = np.asarray(ffn_w_up); wd = np.asarray(ffn_w_down)
    invA, invB = _inv_cols()
    cores = [(c // 4, c % 4) for c in range(8)]

    def tok(jj):
        return slice(jj * Th, (jj + 1) * Th)

    nc1 = build_L1(Th)
    H1T = np.empty((Bn, D, S), np.float32)
    QAT = np.empty((Bn, 6, 128, S), BF); KAT = np.empty((Bn, 6, 128, S), BF); VA = np.empty((Bn, S, 768), BF)
    QBT = np.empty((Bn, 4, 96, S), BF); KBT = np.empty((Bn, 4, 96, S), BF); VB = np.empty((Bn, S, 512), BF)
    common1 = {}
    common1.update(_ffn_vals("a", norm_g[0, 0], wg[0, 0], wu[0, 0], wd[0, 0]))
    common1.update({"g1col": _colT(norm_g[0, 1], 16), "qncol": _colT(np.asarray(even_q_norm)[0], 3), "kvncol": _colT(np.asarray(even_kv_norm)[0], 2),
                    "invA": invA, "invB": invB, "w_in": np.ascontiguousarray(np.asarray(even_w_in)[0]), "w_q_up": np.ascontiguousarray(np.asarray(even_w_q_up)[0]),
                    "w_kv_up": np.ascontiguousarray(np.asarray(even_w_kv_up)[0])})
    for rd in rounds:
        ims = []
        for b, jj in rd:
            im = dict(common1)
            im["xT"] = np.ascontiguousarray(x[b, tok(jj)].T)
            im["posrep"] = np.ascontiguousarray(np.broadcast_to(positions[b, tok(jj)][None, :], (128, Th))).astype(np.int32)
            ims.append(im)
        r = _run(nc1, ims)
        for (b, jj), rr in zip(rd, r):
            H1T[b][:, tok(jj)] = np.asarray(rr["h1T"])
            QAT[b][:, :, tok(jj)] = np.asarray(rr["qaT"]); KAT[b][:, :, tok(jj)] = np.asarray(rr["kaT"]); VA[b][tok(jj)] = np.asarray(rr["va"])
            QBT[b][:, :, tok(jj)] = np.asarray(rr["qbT"]); KBT[b][:, :, tok(jj)] = np.asarray(rr["kbT"]); VB[b][tok(jj)] = np.asarray(rr["vb"])
        del r
    ones_row = np.ones((1, S), BF)
    mm = mla_masks(); dm = dil_mask()
    ims = []
    for c, (b, j) in enumerate(cores):
        h = j
        q0 = j * T
        lo = q0 - 2048
        a = max(lo, 0)
        kaT = np.zeros((6, 128, Lk), BF)
        kaT[:, :, a - lo:] = KAT[b][:, :, a:q0 + T]
        vh = np.zeros((6, Lk, 128), BF)
        vh[:, a - lo:] = VA[b][a:q0 + T].reshape(-1, 6, 128).transpose(1, 0, 2)
        kb2 = np.zeros((2, Lk), np.float32); kb2[0] = 1.0; kb2[1, :a - lo] = -30000.0
        ims.append({"qT": np.ascontiguousarray(QBT[b][h]), "kT": np.ascontiguousarray(KBT[b][h]),
                    "v": np.ascontiguousarray(VB[b][:, h * 128:(h + 1) * 128]), "onesrow": ones_row, "masks": mm,
                    "qaT": np.ascontiguousarray(QAT[b][:, :, q0:q0 + T]), "kaT": kaT, "vreg": vreg_layout(vh, Lk), "dmask": dm, "kbias2": kb2.astype(BF)})
    r2 = _run(build_L2(S, T), ims)
    MT = np.empty((Bn, 1280, S), BF)
    for c, (b, j) in enumerate(cores):
        MT[b][0:768, j * T:(j + 1) * T] = np.asarray(r2[c]["oaT"]).reshape(768, T)
        MT[b][768 + j * 128:768 + (j + 1) * 128, :] = np.asarray(r2[c]["oT"])
    del r2, ims, QAT, KAT, VA, QBT, KBT, VB
    nc3 = build_L3(Th)
    H4T = np.empty((Bn, D, S), np.float32); UT = np.empty((Bn, 512, S), np.float32)
    QDT = np.empty((Bn, 4, 128, S), BF); KDT = np.empty((Bn, 4, 128, S), BF); VD = np.empty((Bn, S, 512), BF)
    common3 = {"w_out": np.ascontiguousarray(np.asarray(even_w_out)[0])}
    common3.update(_ffn_vals("b", norm_g[0, 2], wg[0, 1], wu[0, 1], wd[0, 1]))
    common3.update(_ffn_vals("c", norm_g[1, 0], wg[1, 0], wu[1, 0], wd[1, 0]))
    common3.update({"g1col": _colT(norm_g[1, 1], 16), "w_in": np.ascontiguousarray(np.asarray(odd_w_in)[0])})
    for rd in rounds:
        ims = []
        for b, jj in rd:
            im = dict(common3)
            im["mT"] = np.ascontiguousarray(MT[b][:, tok(jj)])
            im["h1T"] = np.ascontiguousarray(H1T[b][:, tok(jj)])
            ims.append(im)
        r = _run(nc3, ims)
        for (b, jj), rr in zip(rd, r):
            H4T[b][:, tok(jj)] = np.asarray(rr["h4T"]); UT[b][:, tok(jj)] = np.asarray(rr["uT"])
            QDT[b][:, :, tok(jj)] = np.asarray(rr["qdT"]); KDT[b][:, :, tok(jj)] = np.asarray(rr["kdT"]); VD[b][tok(jj)] = np.asarray(rr["vd"])
        del r
    del H1T, MT
    sm = sb_masks(); tm = tri_mats()
    ims = []
    for c, (b, j) in enumerate(cores):
        h = j
        ims.append({"qT": np.ascontiguousarray(QDT[b][h]), "kT": np.ascontiguousarray(KDT[b][h]),
                    "v": np.ascontiguousarray(VD[b][:, h * 128:(h + 1) * 128]), "masks": sm, "tri": tm})
    r4 = _run(build_L4(S), ims)
    ODT = np.empty((Bn, 512, S), BF)
    for c, (b, j) in enumerate(cores):
        ODT[b][j * 128:(j + 1) * 128, :] = np.asarray(r4[c]["oT"])
    del r4, ims
    nc5 = build_L5(Th)
    pw = np.ascontiguousarray(np.asarray(odd_pool_w)[0].transpose(1, 0, 2).reshape(128, 512))
    psc = _colT(np.asarray(odd_pool_scale)[0], 4)
    common5 = {"w_out": np.ascontiguousarray(np.asarray(odd_w_out)[0]), "pool_w": pw, "pscol": psc, "gfin": _colT(final_norm, 16)}
    common5.update(_ffn_vals("d", norm_g[1, 2], wg[1, 1], wu[1, 1], wd[1, 1]))
    out = np.empty((Bn, S, D), np.float32)
    for rd in rounds:
        ims = []
        for b, jj in rd:
            im = dict(common5)
            uTh = np.zeros((512, Th + 16), np.float32)
            uTh[:, 16:] = UT[b][:, tok(jj)]
            if jj > 0:
                uTh[:, :16] = UT[b][:, jj * Th - 16:jj * Th]
            tg = np.arange(jj * Th, (jj + 1) * Th)
            ic = np.stack([1.0 / np.minimum(tg + 1, w) for w in (2, 4, 8, 16)]).astype(np.float32)
            im.update({"uTh": uTh, "odT": np.ascontiguousarray(ODT[b][:, tok(jj)]),
                       "invcnt": np.ascontiguousarray(np.broadcast_to(ic[None], (128, 4, Th))), "h4T": np.ascontiguousarray(H4T[b][:, tok(jj)])})
            ims.append(im)
        r = _run(nc5, ims)
        for (b, jj), rr in zip(rd, r):
            out[b, tok(jj)] = np.asarray(rr["outT"]).T
        del r
    return out
```

```python
import numpy as np
import concourse.bass as bass
import concourse.mybir as mybir
from concourse.bass_utils import run_bass_kernel_spmd

F32 = mybir.dt.float32
BF16 = mybir.dt.bfloat16
I32 = mybir.dt.int32
AF = mybir.ActivationFunctionType
ALU = mybir.AluOpType
AX = mybir.AxisListType

SAME_ENGINE_SYNC = {"pe": False, "act": False, "dve": False, "pool": False, "sp": False}
EPOCH = 30000


class Dep:
    __slots__ = ("w", "r")

    def __init__(self):
        self.w = None
        self.r = {}


def deps(n):
    return [Dep() for _ in range(n)]


class K:
    def __init__(self, nc, stack, n_dma_sems=16):
        self.nc = nc
        self.stack = stack
        self.engs = {"pe": nc.tensor, "act": nc.scalar, "dve": nc.vector, "pool": nc.gpsimd, "sp": nc.sync}
        self.prog = {e: [] for e in self.engs}
        self.cnt = {e: 0 for e in self.engs}
        self.sem = {e: self._newsem(e) for e in self.engs}
        self.known = {e: {} for e in self.engs}
        self.dq = {}
        for q in ("sp", "pool", "act"):
            self.dq[q] = {"sems": [self._newsem("d%s%d" % (q, i)) for i in range(n_dma_sems)], "n": 0}
        self.allsems = {}
        self.ninst = 0
        self.pending = {}
        self.cuts = []

    def _newsem(self, name):
        self._semn = getattr(self, "_semn", 0) + 1
        return self.stack.enter_context(self.nc.semaphore("s_%s_%d" % (name, self._semn)))

    def _collect(self, eng, reads, writes, extra=(), same=None):
        waits = {}
        own = id(self.sem[eng])
        kn = self.known[eng]

        def need(ev):
            if ev is None:
                return
            sem, val = ev
            sid = id(sem)
            if sid == own and not (SAME_ENGINE_SYNC[eng] if same is None else same):
                return
            if kn.get(sid, 0) >= val:
                return
            if sid not in waits or waits[sid][1] < val:
                waits[sid] = (sem, val)

        for d in reads:
            need(d.w)
        for d in writes:
            need(d.w)
            for ev in d.r.values():
                need(ev)
        for ev in extra:
            need(ev)
        for sid, (sem, val) in waits.items():
            kn[sid] = val
        return list(waits.values())

    def op(self, eng, fn, reads=(), writes=(), inc=True):
        wl = self._collect(eng, reads, writes)
        if inc and self.cnt[eng] >= EPOCH and not self.pending.get(eng):
            self.sem[eng] = self._newsem(eng)
            self.cnt[eng] = 0
        if inc:
            self.cnt[eng] += 1
            my = (self.sem[eng], self.cnt[eng])
        else:
            my = (self.sem[eng], self.cnt[eng] + 1)
        self.allsems[id(my[0])] = (my[0], max(my[1], self.allsems.get(id(my[0]), (None, 0))[1])) if inc else self.allsems.get(id(my[0]), (my[0], 0))

        def emit(e, fn=fn, wl=wl, my=my, inc=inc):
            for sem, val in wl:
                e.wait_ge(sem, val)
            if inc:
                fn(e).then_inc(my[0], 1)
            else:
                fn(e)

        self.pending[eng] = not inc
        self.prog[eng].append(emit)
        self.ninst += 1
        sid = id(my[0])
        for d in reads:
            d.r[sid] = my
        for d in writes:
            d.w = my
            d.r = {}
        return my

    def dma(self, q, out_ap, in_ap, reads=(), writes=(), **kw):
        st = self.dq[q]
        n = st["n"]
        st["n"] += 1
        P = len(st["sems"])
        sem = st["sems"][n % P]
        val = 16 * (n // P + 1)
        extra = [(sem, val - 16)] if n >= P else []
        wl = self._collect(q, reads, writes, extra, same=True)
        my = (sem, val)
        self.allsems[id(sem)] = my

        def emit(e, wl=wl, my=my, out_ap=out_ap, in_ap=in_ap, kw=kw):
            for s, v in wl:
                e.wait_ge(s, v)
            e.dma_start(out=out_ap, in_=in_ap, **kw).then_inc(my[0], 16)

        self.prog[q].append(emit)
        self.ninst += 1
        sid = id(sem)
        for d in reads:
            d.r[sid] = my
        for d in writes:
            d.w = my
            d.r = {}
        return my

    def coll(self, kind, in_ap, out_ap, groups, reads=(), writes=()):
        q = "pool"
        st = self.dq[q]
        n = st["n"]
        st["n"] += 1
        P = len(st["sems"])
        sem = st["sems"][n % P]
        val = 16 * (n // P + 1)
        extra = [(sem, val - 16)] if n >= P else []
        wl = self._collect(q, reads, writes, extra, same=True)
        my = (sem, val)
        self.allsems[id(sem)] = my

        def emit(e, wl=wl, my=my):
            for s_, v in wl:
                e.wait_ge(s_, v)
            e.collective_compute(kind, ALU.bypass, replica_groups=groups, ins=[in_ap], outs=[out_ap]).then_inc(my[0], 16)

        self.prog[q].append(emit)
        self.ninst += 1
        sid = id(sem)
        for d in reads:
            d.r[sid] = my
        for d in writes:
            d.w = my
            d.r = {}
        return my

    def barrier(self):
        finals = [v for v in self.allsems.values() if v[1] > 0]
        for eng in self.engs:
            def emit(e, finals=finals):
                for sem, val in finals:
                    e.wait_ge(sem, val)
            self.prog[eng].append(emit)
            for sem, val in finals:
                if self.known[eng].get(id(sem), 0) < val:
                    self.known[eng][id(sem)] = val

    def cut(self):
        self.barrier()
        self.cuts.append({e: len(self.prog[e]) for e in self.engs})

    def finish(self):
        finals = [v for v in self.allsems.values() if v[1] > 0]

        def emit_final(e):
            for sem, val in finals:
                e.wait_ge(sem, val)

        self.prog["sp"].append(emit_final)
        nc = self.nc
        bounds = self.cuts + [{e: len(self.prog[e]) for e in self.engs}]
        prev = {e: 0 for e in self.engs}
        for bd in bounds:
            seg = {e: self.prog[e][prev[e]:bd[e]] for e in self.engs}
            prev = bd
            if not any(seg.values()):
                continue
            with nc.Block() as block:
                @block.tensor
                def _(e, seg=seg):
                    for f in seg["pe"]:
                        f(e)

                @block.scalar
                def _(e, seg=seg):
                    for f in seg["act"]:
                        f(e)

                @block.vector
                def _(e, seg=seg):
                    for f in seg["dve"]:
                        f(e)

                @block.gpsimd
                def _(e, seg=seg):
                    for f in seg["pool"]:
                        f(e)

                @block.sync
                def _(e, seg=seg):
                    for f in seg["sp"]:
                        f(e)


import math
from contextlib import ExitStack
import numpy as np

D = 2048
DFF = 1536
NT = 256
EPS = 1e-6
TWO_PI = 2 * math.pi
CW1 = 6.28125
CW2 = float(np.float32(TWO_PI - CW1))
CW3 = float(TWO_PI - CW1 - CW2)
MAGIC = 12582912.0


class Cx:
    def __init__(self, nc, st):
        self.nc = nc
        self.st = st
        self.k = K(nc, st)
        self._n = 0
        self.stk = [st]
        self.banks = []
        for i in range(8):
            t = st.enter_context(nc.psum_tensor("bank%d" % i, [128, 512], F32))
            self.banks.append((t, Dep()))
        self._b = 0
        self.ones = self.sb([128, 128], BF16)
        self.dones = Dep()
        self.k.op("dve", lambda e: e.memset(self.ones[:], 1.0), writes=[self.dones])
        self.one32 = self.sb([128, 1], F32)
        self.done32 = Dep()
        self.k.op("dve", lambda e: e.memset(self.one32[:], 1.0), writes=[self.done32])

    def push(self, st):
        self.stk.append(st)

    def pop(self):
        self.stk.pop()
        self._stage_key = None

    def sb(self, shape, dt):
        self._n += 1
        return self.stk[-1].enter_context(self.nc.sbuf_tensor("t%d" % self._n, list(shape), dt))

    def rot(self, shape, dt, n):
        return Rot([(self.sb(shape, dt), Dep()) for _ in range(n)])

    def din(self, name, shape, dt):
        return self.nc.dram_tensor(name, list(shape), dt, kind="ExternalInput").ap()

    def dout(self, name, shape, dt):
        return self.nc.dram_tensor(name, list(shape), dt, kind="ExternalOutput").ap()

    def dscr(self, name, shape, dt):
        return self.nc.dram_tensor(name, list(shape), dt, kind="Internal").ap()

    def bank(self):
        b = self.banks[self._b % 8]
        self._b += 1
        return b

    def chain(self, M, N, pairs, reads, skip=False):
        bank, dep = self.bank()
        out = bank[0:M, 0:N]
        n = len(pairs)
        for i, (l, r) in enumerate(pairs):
            self.k.op("pe", lambda e, l=l, r=r, i=i: e.matmul(out, l, r, start=(i == 0), stop=(i == n - 1)),
                      reads=reads, writes=[dep], inc=(i == n - 1))
        return out, dep


class Rot:
    def __init__(self, items):
        self.items = items
        self.i = 0

    def next(self):
        it = self.items[self.i % len(self.items)]
        self.i += 1
        return it


def load_w(cx, w_ap, kc_n, F, gcol=None, dg=None, extra=None):
    k = cx.k
    W = cx.sb([128, kc_n, F], BF16)
    dW = deps(kc_n)
    wv = w_ap.rearrange("(kc p) f -> p kc f", p=128)
    key = id(cx.stk[-1])
    if getattr(cx, "_stage_key", None) != key:
        cx._stage_key = key
        cx._stage = cx.rot([128, 512], F32, 5)
        cx._stage_n = 0
    for kc in range(kc_n):
        for c0 in range(0, F, 512):
            w = min(512, F - c0)
            st, dst_ = cx._stage.next()
            n = cx._stage_n
            cx._stage_n += 1
            k.dma(("sp", "act", "pool")[n % 3], st[:, 0:w], wv[:, kc, c0:c0 + w], writes=[dst_])
            eng = ("dve", "act")[n % 2]
            out = W[:, kc, c0:c0 + w]
            rd = [dst_] + ([dg] if gcol is not None else [])
            if eng == "act":
                if gcol is not None:
                    k.op("act", lambda e, out=out, st=st, w=w, kc=kc: e.activation(out, st[:, 0:w], AF.Copy, scale=gcol[:, kc:kc + 1]), reads=rd, writes=[dW[kc]])
                else:
                    k.op("act", lambda e, out=out, st=st, w=w: e.activation(out, st[:, 0:w], AF.Copy), reads=rd, writes=[dW[kc]])
            else:
                if gcol is not None:
                    k.op(eng, lambda e, out=out, st=st, w=w, kc=kc: e.tensor_scalar(out, st[:, 0:w], gcol[:, kc:kc + 1], None, ALU.mult), reads=rd, writes=[dW[kc]])
                else:
                    k.op(eng, lambda e, out=out, st=st, w=w: e.tensor_copy(out, st[:, 0:w]), reads=rd, writes=[dW[kc]])
            if extra is not None:
                extra(kc, c0, w, st, dst_)
    return W, dW


def fold_g(cx, W, dW, gcol, dg, kc_n, eng="pool"):
    return


def load_small(cx, ap, shape, dt=F32, q="sp"):
    t = cx.sb(shape, dt)
    d = Dep()
    cx.k.dma(q, t[:], ap, writes=[d])
    return t, d


def rstd_from_ss(cx, ss_ap, dss, out, dout, n_feat, tmp, dtmp):
    k = cx.k
    k.op("dve", lambda e: e.tensor_scalar(tmp, ss_ap, 1.0 / n_feat, EPS, ALU.mult, ALU.add), reads=[dss], writes=[dtmp])
    k.op("act", lambda e: e.activation(tmp, tmp, AF.Sqrt), reads=[dtmp], writes=[dtmp])
    k.op("dve", lambda e: e.reciprocal(out, tmp), reads=[dtmp], writes=[dout])


def load_x_stats(cx, srcv, t0, nt, X, dX, SQ, dSQ, kc_n, rstd, drstd, tmp, dtmp, n_feat, dsrc, srcb=None):
    k = cx.k
    if srcb is not None:
        k.dma("sp", X[:, :, 0:nt], srcb.rearrange("(kc p) t -> p kc t", p=128)[:, :, t0:t0 + nt], reads=[dsrc], writes=[dX])
    else:
        k.dma("pool", X[:, :, 0:nt], srcv[:, :, t0:t0 + nt], reads=[dsrc], writes=[dX])
    k.op("pool", lambda e: e.tensor_tensor(SQ[:, :, 0:nt], X[:, :, 0:nt], X[:, :, 0:nt], ALU.mult), reads=[dX], writes=[dSQ])
    ss, dss = cx.chain(128, nt, [(cx.ones[:], SQ[:, kc, 0:nt]) for kc in range(kc_n)], [dSQ, cx.dones])
    rstd_from_ss(cx, ss, dss, rstd, drstd, n_feat, tmp, dtmp)


def emit_ffn(cx, T, src, gcol_ap, wg, wu, wd, dst, dsrc=None, ddst=None, dst2=None, srcb=None, dstb=None):
    k = cx.k
    dsrc = dsrc or Dep()
    ddst = ddst or Dep()
    with ExitStack() as ps:
        cx.push(ps)
        gcol, dg = load_small(cx, gcol_ap, [128, 16])
        Wg, dWg = load_w(cx, wg, 16, DFF, gcol, dg)
        Wu, dWu = load_w(cx, wu, 16, DFF, gcol, dg)
        Wd, dWd = load_w(cx, wd, 12, D)
        srcv = src.rearrange("(kc p) t -> p kc t", p=128)
        Xr = cx.rot([128, 16, NT], BF16, 2)
        SQr = cx.rot([128, 16, NT], BF16, 1)
        Hr = cx.rot([128, 12, NT], BF16, 2)
        rs_r = cx.rot([128, NT], F32, 2)
        tmp_r = cx.rot([128, NT], F32, 2)
        g1r = cx.rot([128, NT], F32, 2)
        u1r = cx.rot([128, NT], F32, 2)
        xr_r = cx.rot([128, NT], F32, 3)
        o_r = cx.rot([128, NT], F32, 3)
        ob_r = cx.rot([128, NT], BF16, 2)
        for tt in range(T // NT):
            t0 = tt * NT
            X, dX = Xr.next()
            SQ, dSQ = SQr.next()
            H, dH = Hr.next()
            rstd, drs = rs_r.next()
            tmp, dtmp = tmp_r.next()
            load_x_stats(cx, srcv, t0, NT, X, dX, SQ, dSQ, 16, rstd[:], drs, tmp[:], dtmp, D, dsrc, srcb)
            for fc in range(12):
                fs = slice(fc * 128, (fc + 1) * 128)
                pg, dpg = cx.chain(128, NT, [(Wg[:, kc, fs], X[:, kc, :]) for kc in range(16)], dWg + [dX])
                pu, dpu = cx.chain(128, NT, [(Wu[:, kc, fs], X[:, kc, :]) for kc in range(16)], dWu + [dX])
                g1, dg1 = g1r.next()
                u1, du1 = u1r.next()
                k.op("dve", lambda e, g1=g1, pg=pg, rstd=rstd: e.tensor_tensor(g1[:], pg, rstd[:], ALU.mult), reads=[dpg, drs], writes=[dg1])
                k.op("act", lambda e, g1=g1: e.activation(g1[:], g1[:], AF.Silu), reads=[dg1], writes=[dg1])
                k.op("dve", lambda e, u1=u1, pu=pu, rstd=rstd: e.tensor_tensor(u1[:], pu, rstd[:], ALU.mult), reads=[dpu, drs], writes=[du1])
                k.op("pool", lambda e, H=H, fc=fc, g1=g1, u1=u1: e.tensor_tensor(H[:, fc, :], g1[:], u1[:], ALU.mult), reads=[dg1, du1], writes=[dH])
            for dc in range(16):
                ds_ = slice(dc * 128, (dc + 1) * 128)
                py, dpy = cx.chain(128, NT, [(Wd[:, fc, ds_], H[:, fc, :]) for fc in range(12)], dWd + [dH])
                xr, dxr = xr_r.next()
                o, do = o_r.next()
                k.dma("sp", xr[:], src[ds_, t0:t0 + NT], reads=[dsrc], writes=[dxr])
                k.op("dve", lambda e, o=o, py=py, xr=xr: e.scalar_tensor_tensor(o[:], py, 0.5, xr[:], ALU.mult, ALU.add), reads=[dpy, dxr], writes=[do])
                k.dma("act", dst[ds_, t0:t0 + NT], o[:], reads=[do], writes=[ddst])
                if dst2 is not None:
                    k.dma("act", dst2[ds_, t0:t0 + NT], o[:], reads=[do])
                if dstb is not None:
                    ob16, dob16 = ob_r.next()
                    k.op("act", lambda e, ob16=ob16, o=o: e.activation(ob16[:], o[:], AF.Copy), reads=[do], writes=[dob16])
                    k.dma("act", dstb[ds_, t0:t0 + NT], ob16[:], reads=[dob16], writes=[ddst])
        cx.pop()
    k.cut()


def rope_tables(cx, posf, dposf, inv, dinv, P, nt, rk, sin_t, dsin, cos_t, dcos, ang, dang, kk, dkk):
    k = cx.k
    k.op("dve", lambda e: e.tensor_scalar(ang, posf, inv, None, ALU.mult), reads=[dposf, dinv], writes=[dang])
    k.op("dve", lambda e: e.tensor_scalar(kk, ang, 1.0 / TWO_PI, MAGIC, ALU.mult, ALU.add), reads=[dang], writes=[dkk])
    k.op("dve", lambda e: e.tensor_scalar(kk, kk, -MAGIC, None, ALU.add), reads=[dkk], writes=[dkk])
    for cc in (CW1, CW2, CW3):
        k.op("dve", lambda e, cc=cc: e.scalar_tensor_tensor(ang, kk, -cc, ang, ALU.mult, ALU.add), reads=[dkk, dang], writes=[dang])
    k.op("dve", lambda e: e.tensor_scalar(ang, ang, math.pi, -math.pi, ALU.min, ALU.max), reads=[dang], writes=[dang])
    k.op("act", lambda e: e.activation(sin_t, ang, AF.Sin), reads=[dang], writes=[dsin])
    k.op("act", lambda e: e.activation(kk, ang, AF.Abs), reads=[dang], writes=[dkk])
    k.op("dve", lambda e: e.tensor_scalar(kk, kk, -1.0, math.pi / 2, ALU.mult, ALU.add), reads=[dkk], writes=[dkk])
    k.op("act", lambda e: e.activation(cos_t, kk, AF.Sin), reads=[dkk], writes=[dcos])


class Long:
    def __init__(self, cx, names):
        self.t = {n: (cx.sb([128, NT], F32), Dep()) for n in names}

    def __getitem__(self, n):
        return self.t[n]


def pos_tables(cx, A, t0, posr, L, invc, dinv, P, sname, cname):
    k = cx.k
    pi_, dpi = posr.next()
    posf, dposf = L["posf"]
    k.dma("sp", pi_[:], A["posrep"][:, t0:t0 + NT], writes=[dpi])
    k.op("dve", lambda e: e.tensor_copy(posf[:], pi_[:]), reads=[dpi], writes=[dposf])
    sn, dsn = L[sname]; cs, dcs = L[cname]; ang, dang = L["ang"]; kk, dkk = L["kk"]
    rope_tables(cx, posf[0:P, :], dposf, invc[0:P, 0:1], dinv, P, NT, None, sn[0:P, :], dsn, cs[0:P, :], dcs, ang[0:P, :], dang, kk[0:P, :], dkk)


def emit_inproj_even_A(cx, T, src, dsrc, A, srcb=None):
    k = cx.k
    with ExitStack() as ps:
        cx.push(ps)
        gcol, dg = load_small(cx, A["g1col"], [128, 16])
        invA, dinvA = load_small(cx, A["invA"], [128, 1])
        w_in = A["w_in"]
        Wrot = cx.sb([128, 16, 1536], BF16)
        dWrot = deps(16)
        Wr5 = Wrot[:].rearrange("p kc (h two i) -> p kc h two i", two=2, i=64)

        def rot_extra(kc, c0, w, st, dst_):
            h0 = c0 // 128
            nh = w // 128
            sv = st[:, 0:w].rearrange("p (h two i) -> p h two i", two=2, i=64)
            k.op("dve", lambda e: e.tensor_scalar(Wr5[:, kc, h0:h0 + nh, 0, :], sv[:, :, 1, :], gcol[:, kc:kc + 1], -1.0, ALU.mult, ALU.mult), reads=[dst_, dg], writes=[dWrot[kc]])
            k.op("act", lambda e: e.activation(Wr5[:, kc, h0:h0 + nh, 1, :], sv[:, :, 0, :], AF.Copy, scale=gcol[:, kc:kc + 1]), reads=[dst_, dg], writes=[dWrot[kc]])

        Win, dWin = load_w(cx, w_in[:, 0:1536], 16, 1536, gcol, dg, extra=rot_extra)
        srcv = src.rearrange("(kc p) t -> p kc t", p=128)
        Xr = cx.rot([128, 16, NT], BF16, 2)
        SQr = cx.rot([128, 16, NT], BF16, 1)
        L = Long(cx, ["rstd", "tmp", "posf", "sinA", "cosA", "ang", "kk", "cq", "sq"])
        f32r = cx.rot([128, NT], F32, 6)
        obr = cx.rot([128, NT], BF16, 6)
        posr = cx.rot([128, NT], I32, 2)
        SA = 128 ** -0.5
        for tt in range(T // NT):
            t0 = tt * NT
            X, dX = Xr.next()
            SQ, dSQ = SQr.next()
            rstd, drs = L["rstd"]; tmp, dtmp = L["tmp"]
            load_x_stats(cx, srcv, t0, NT, X, dX, SQ, dSQ, 16, rstd[:], drs, tmp[:], dtmp, D, dsrc, srcb)
            pos_tables(cx, A, t0, posr, L, invA, dinvA, 128, "sinA", "cosA")
            sinA, dsinA = L["sinA"]; cosA, dcosA = L["cosA"]; cq, dcq = L["cq"]; sq_, dsq_ = L["sq"]
            k.op("dve", lambda e: e.scalar_tensor_tensor(cq[:], cosA[:], SA, rstd[:], ALU.mult, ALU.mult), reads=[dcosA, drs], writes=[dcq])
            k.op("dve", lambda e: e.scalar_tensor_tensor(sq_[:], sinA[:], SA, rstd[:], ALU.mult, ALU.mult), reads=[dsinA, drs], writes=[dsq_])
            k.op("dve", lambda e: e.tensor_tensor(cosA[:], cosA[:], rstd[:], ALU.mult), reads=[drs], writes=[dcosA])
            k.op("dve", lambda e: e.tensor_tensor(sinA[:], sinA[:], rstd[:], ALU.mult), reads=[drs], writes=[dsinA])
            for hh in range(12):
                cs = slice(hh * 128, (hh + 1) * 128)
                p1, dp1 = cx.chain(128, NT, [(Win[:, kc, cs], X[:, kc, :]) for kc in range(16)], dWin + [dX])
                p2, dp2 = cx.chain(128, NT, [(Wrot[:, kc, cs], X[:, kc, :]) for kc in range(16)], dWrot + [dX])
                ct, dct, stb, dst_ = (cq, dcq, sq_, dsq_) if hh < 6 else (cosA, dcosA, sinA, dsinA)
                t1, dt1 = f32r.next(); t2, dt2 = f32r.next()
                k.op("dve", lambda e, t1=t1, p1=p1, ct=ct: e.tensor_tensor(t1[:], p1, ct[:], ALU.mult), reads=[dp1, dct], writes=[dt1])
                k.op("dve", lambda e, t2=t2, p2=p2, stb=stb: e.tensor_tensor(t2[:], p2, stb[:], ALU.mult), reads=[dp2, dst_], writes=[dt2])
                ob, dob = obr.next()
                k.op("pool", lambda e, ob=ob, t1=t1, t2=t2: e.tensor_tensor(ob[:], t1[:], t2[:], ALU.add), reads=[dt1, dt2], writes=[dob])
                dstT = A["qaT"] if hh < 6 else A["kaT"]
                k.dma("act", dstT[hh % 6, :, t0:t0 + NT], ob[:], reads=[dob])
        cx.pop()
    k.cut()


def rstd_cols(cx, rstd, drs, colr, n_sub):
    out = []
    for s in range(n_sub):
        pc, dpc = cx.chain(128, 1, [(rstd[0:1, s * 128:(s + 1) * 128], cx.one32[0:1, 0:1])], [drs, cx.done32])
        rc, drc = colr.next()
        cx.k.op("dve", lambda e, rc=rc, pc=pc: e.tensor_copy(rc[:], pc), reads=[dpc], writes=[drc])
        out.append((rc, drc))
    return out


def emit_inproj_even_B(cx, T, src, dsrc, A, srcb=None):
    k = cx.k
    with ExitStack() as ps:
        cx.push(ps)
        gcol, dg = load_small(cx, A["g1col"], [128, 16])
        qn, dqn = load_small(cx, A["qncol"], [128, 3])
        kvn, dkvn = load_small(cx, A["kvncol"], [128, 2])
        invB, dinvB = load_small(cx, A["invB"], [128, 1])
        w_in = A["w_in"]
        OFF = 1536
        Win, dWin = load_w(cx, w_in[:, OFF:2976], 16, 2976 - OFF, gcol, dg)
        Wkr = cx.sb([128, 16, 96], BF16)
        Wkrr = cx.sb([128, 16, 96], BF16)
        dWkr = Dep()
        k.op("dve", lambda e: e.memset(Wkr[:], 0.0), writes=[dWkr])
        k.op("dve", lambda e: e.memset(Wkrr[:], 0.0), writes=[dWkr])
        wkv = w_in[:, 2944:2976].rearrange("(kc p) f -> p kc f", p=128)
        k.dma("pool", Wkr[:, :, 64:96], wkv, writes=[dWkr])
        k.dma("pool", Wkrr[:, :, 64:80], wkv[:, :, 16:32], writes=[dWkr])
        k.dma("pool", Wkrr[:, :, 80:96], wkv[:, :, 0:16], writes=[dWkr])
        for kc in range(16):
            k.op("pool", lambda e, kc=kc: e.tensor_scalar(Wkr[:, kc, 64:96], Wkr[:, kc, 64:96], gcol[:, kc:kc + 1], None, ALU.mult), reads=[dg], writes=[dWkr])
            k.op("pool", lambda e, kc=kc: e.tensor_scalar(Wkrr[:, kc, 64:80], Wkrr[:, kc, 64:80], gcol[:, kc:kc + 1], -1.0, ALU.mult, ALU.mult), reads=[dg], writes=[dWkr])
            k.op("pool", lambda e, kc=kc: e.tensor_scalar(Wkrr[:, kc, 80:96], Wkrr[:, kc, 80:96], gcol[:, kc:kc + 1], None, ALU.mult), reads=[dg], writes=[dWkr])
        Wq, dWq = load_w(cx, A["w_q_up"], 3, 384, qn, dqn)
        Wqr = cx.sb([128, 3, 384], BF16)
        dWqr = Dep()
        k.op("dve", lambda e: e.memset(Wqr[:], 0.0), writes=[dWqr])
        wq4 = A["w_q_up"].rearrange("(kc p) (h c) -> p kc h c", p=128, c=96)
        Wqr4 = Wqr[:].rearrange("p kc (h c) -> p kc h c", c=96)
        for kc in range(3):
            k.dma("pool", Wqr4[:, kc, :, 64:80], wq4[:, kc, :, 80:96], writes=[dWqr])
            k.dma("pool", Wqr4[:, kc, :, 80:96], wq4[:, kc, :, 64:80], writes=[dWqr])
            k.op("pool", lambda e, kc=kc: e.tensor_scalar(Wqr4[:, kc, :, 64:80], Wqr4[:, kc, :, 64:80], qn[:, kc:kc + 1], -1.0, ALU.mult, ALU.mult), reads=[dqn], writes=[dWqr])
            k.op("pool", lambda e, kc=kc: e.tensor_scalar(Wqr4[:, kc, :, 80:96], Wqr4[:, kc, :, 80:96], qn[:, kc:kc + 1], None, ALU.mult), reads=[dqn], writes=[dWqr])
        Wkv, dWkv = load_w(cx, A["w_kv_up"], 2, 768, kvn, dkvn)
        Wkv4 = Wkv[:].rearrange("p kc (h c) -> p kc h c", c=192)

        srcv = src.rearrange("(kc p) t -> p kc t", p=128)
        Xr = cx.rot([128, 16, NT], BF16, 2)
        SQr = cx.rot([128, 16, NT], BF16, 1)
        L = Long(cx, ["rstd", "tmp", "posf", "sinB", "cosB", "ang", "kk", "rstd2", "tmp2", "cB2", "sB2", "rstd3", "tmp3", "cB1", "sB1"])
        f32r = cx.rot([128, NT], F32, 6)
        obr = cx.rot([128, NT], BF16, 6)
        otr = cx.rot([128, 512], BF16, 3)
        colr = cx.rot([128, 1], F32, 8)
        cqr = cx.rot([128, 3, NT], BF16, 2)
        ckr = cx.rot([128, 2, NT], BF16, 2)
        sq2r = cx.rot([128, 3, NT], BF16, 2)
        posr = cx.rot([128, NT], I32, 2)
        SB_ = 96 ** -0.5
        CV, CQ, CKV = 0, 768, 1152
        for tt in range(T // NT):
            t0 = tt * NT
            X, dX = Xr.next()
            SQ, dSQ = SQr.next()
            rstd, drs = L["rstd"]; tmp, dtmp = L["tmp"]
            load_x_stats(cx, srcv, t0, NT, X, dX, SQ, dSQ, 16, rstd[:], drs, tmp[:], dtmp, D, dsrc, srcb)
            pos_tables(cx, A, t0, posr, L, invB, dinvB, 96, "sinB", "cosB")
            sinB, dsinB = L["sinB"]; cosB, dcosB = L["cosB"]
            rcols = rstd_cols(cx, rstd, drs, colr, NT // 128)
            for s in range(NT // 128):
                ts_ = slice(s * 128, (s + 1) * 128)
                rc, drc = rcols[s]
                for (f0, f1) in ((0, 512), (512, 768)):
                    pv, dpv = cx.chain(128, f1 - f0, [(X[:, kc, ts_], Win[:, kc, CV + f0:CV + f1]) for kc in range(16)], dWin + [dX])
                    ot, dot = otr.next()
                    k.op("act", lambda e, ot=ot, pv=pv, rc=rc, w=f1 - f0: e.activation(ot[:, 0:w], pv, AF.Copy, scale=rc[:, 0:1]), reads=[dpv, drc], writes=[dot])
                    k.dma("act", A["va"][t0 + s * 128:t0 + (s + 1) * 128, f0:f1], ot[:, 0:f1 - f0], reads=[dot])
            cqb, dcqb = cqr.next()
            sq2, dsq2 = sq2r.next()
            for c in range(3):
                pcq, dpcq = cx.chain(128, NT, [(Win[:, kc, CQ + c * 128:CQ + (c + 1) * 128], X[:, kc, :]) for kc in range(16)], dWin + [dX])
                k.op("dve", lambda e, c=c, cqb=cqb, pcq=pcq: e.tensor_tensor(cqb[:, c, :], pcq, rstd[:], ALU.mult), reads=[dpcq, drs], writes=[dcqb])
            k.op("pool", lambda e, sq2=sq2, cqb=cqb: e.tensor_tensor(sq2[:], cqb[:], cqb[:], ALU.mult), reads=[dcqb], writes=[dsq2])
            ss2, dss2 = cx.chain(128, NT, [(cx.ones[:], sq2[:, c, :]) for c in range(3)], [dsq2, cx.dones])
            rstd2, drs2 = L["rstd2"]; tmp2, dtmp2 = L["tmp2"]
            rstd_from_ss(cx, ss2, dss2, rstd2[:], drs2, 384, tmp2[:], dtmp2)
            cB2, dcB2 = L["cB2"]; sB2, dsB2 = L["sB2"]
            k.op("dve", lambda e: e.scalar_tensor_tensor(cB2[0:96, :], cosB[0:96, :], SB_, rstd2[0:96, :], ALU.mult, ALU.mult), reads=[dcosB, drs2], writes=[dcB2])
            k.op("dve", lambda e: e.scalar_tensor_tensor(sB2[0:96, :], sinB[0:96, :], SB_, rstd2[0:96, :], ALU.mult, ALU.mult), reads=[dsinB, drs2], writes=[dsB2])
            for h in range(4):
                cs = slice(h * 96, (h + 1) * 96)
                p1, dp1 = cx.chain(96, NT, [(Wq[:, c, cs], cqb[:, c, :]) for c in range(3)], dWq + [dcqb])
                p2, dp2 = cx.chain(96, NT, [(Wqr[:, c, cs], cqb[:, c, :]) for c in range(3)], [dWqr, dcqb])
                t1, dt1 = f32r.next(); t2, dt2 = f32r.next()
                k.op("dve", lambda e, t1=t1, p1=p1: e.tensor_tensor(t1[0:96, :], p1, cB2[0:96, :], ALU.mult), reads=[dp1, dcB2], writes=[dt1])
                k.op("dve", lambda e, t2=t2, p2=p2: e.tensor_tensor(t2[0:96, :], p2, sB2[0:96, :], ALU.mult), reads=[dp2, dsB2], writes=[dt2])
                ob, dob = obr.next()
                k.op("pool", lambda e, ob=ob, t1=t1, t2=t2: e.tensor_tensor(ob[0:96, :], t1[0:96, :], t2[0:96, :], ALU.add), reads=[dt1, dt2], writes=[dob])
                k.dma("act", A["qbT"][h, :, t0:t0 + NT], ob[0:96, :], reads=[dob])
            ckb, dckb = ckr.next()
            sq3, dsq3 = sq2r.next()
            for c in range(2):
                pck, dpck = cx.chain(128, NT, [(Win[:, kc, CKV + c * 128:CKV + (c + 1) * 128], X[:, kc, :]) for kc in range(16)], dWin + [dX])
                k.op("dve", lambda e, c=c, ckb=ckb, pck=pck: e.tensor_tensor(ckb[:, c, :], pck, rstd[:], ALU.mult), reads=[dpck, drs], writes=[dckb])
            k.op("pool", lambda e, sq3=sq3, ckb=ckb: e.tensor_tensor(sq3[:, 0:2, :], ckb[:], ckb[:], ALU.mult), reads=[dckb], writes=[dsq3])
            ss3, dss3 = cx.chain(128, NT, [(cx.ones[:], sq3[:, c, :]) for c in range(2)], [dsq3, cx.dones])
            rstd3, drs3 = L["rstd3"]; tmp3, dtmp3 = L["tmp3"]
            rstd_from_ss(cx, ss3, dss3, rstd3[:], drs3, 256, tmp3[:], dtmp3)
            for h in range(4):
                pk, dpk = cx.chain(64, NT, [(Wkv4[:, c, h, 0:64], ckb[:, c, :]) for c in range(2)], dWkv + [dckb])
                ob, dob = obr.next()
                k.op("dve", lambda e, ob=ob, pk=pk: e.tensor_tensor(ob[0:64, :], pk, rstd3[0:64, :], ALU.mult), reads=[dpk, drs3], writes=[dob])
                k.dma("act", A["kbT"][h, 0:64, t0:t0 + NT], ob[0:64, :], reads=[dob])
            rcols3 = rstd_cols(cx, rstd3, drs3, colr, NT // 128)
            for s in range(NT // 128):
                ts_ = slice(s * 128, (s + 1) * 128)
                rc, drc = rcols3[s]
                pv, dpv = cx.chain(128, 512, [(ckb[:, c, ts_], Wkv4[:, c, :, 64:192]) for c in range(2)], dWkv + [dckb])
                ot, dot = otr.next()
                k.op("act", lambda e, ot=ot, pv=pv, rc=rc: e.activation(ot[:], pv, AF.Copy, scale=rc[:, 0:1]), reads=[dpv, drc], writes=[dot])
                k.dma("act", A["vb"][t0 + s * 128:t0 + (s + 1) * 128, :], ot[:], reads=[dot])
            p1, dp1 = cx.chain(96, NT, [(Wkr[:, kc, :], X[:, kc, :]) for kc in range(16)], [dWkr, dX])
            p2, dp2 = cx.chain(96, NT, [(Wkrr[:, kc, :], X[:, kc, :]) for kc in range(16)], [dWkr, dX])
            cB1, dcB1 = L["cB1"]; sB1, dsB1 = L["sB1"]
            k.op("dve", lambda e: e.tensor_tensor(cB1[64:96, :], cosB[64:96, :], rstd[64:96, :], ALU.mult), reads=[dcosB, drs], writes=[dcB1])
            k.op("dve", lambda e: e.tensor_tensor(sB1[64:96, :], sinB[64:96, :], rstd[64:96, :], ALU.mult), reads=[dsinB, drs], writes=[dsB1])
            t1, dt1 = f32r.next(); t2, dt2 = f32r.next()
            k.op("dve", lambda e, t1=t1, p1=p1: e.tensor_tensor(t1[64:96, :], p1[64:96, :], cB1[64:96, :], ALU.mult), reads=[dp1, dcB1], writes=[dt1])
            k.op("dve", lambda e, t2=t2, p2=p2: e.tensor_tensor(t2[64:96, :], p2[64:96, :], sB1[64:96, :], ALU.mult), reads=[dp2, dsB1], writes=[dt2])
            ob, dob = obr.next()
            k.op("pool", lambda e, ob=ob, t1=t1, t2=t2: e.tensor_tensor(ob[64:96, :], t1[64:96, :], t2[64:96, :], ALU.add), reads=[dt1, dt2], writes=[dob])
            for h in range(4):
                k.dma("act", A["kbT"][h, 64:96, t0:t0 + NT], ob[64:96, :], reads=[dob])
        cx.pop()
    k.cut()


def norm2_bcast(cx, XT, dXT, P, c0, n, sqr):
    sq, dsq = sqr.next()
    cx.k.op("pool", lambda e: e.tensor_tensor(sq[0:P, 0:n], XT[0:P, c0:c0 + n], XT[0:P, c0:c0 + n], ALU.mult), reads=[dXT], writes=[dsq])
    return cx.chain(128, n, [(cx.ones[0:P, :], sq[0:P, 0:n])], [dsq, cx.dones])


def kmax2(cx, KT, dKT, P, Lk, sqr, f32r, km, dkm):
    k = cx.k
    first = True
    for c0 in range(0, Lk, 512):
        n = min(512, Lk - c0)
        pn, dpn = norm2_bcast(cx, KT, dKT, P, c0, n, sqr)
        if first:
            k.op("dve", lambda e, pn=pn: e.reduce_max(km[:], pn, AX.X), reads=[dpn], writes=[dkm])
            first = False
        else:
            t, dt = f32r.next()
            k.op("dve", lambda e, t=t, pn=pn: e.reduce_max(t[:, 0:1], pn, AX.X), reads=[dpn], writes=[dt])
            k.op("dve", lambda e, t=t: e.tensor_tensor(km[:], km[:], t[:, 0:1], ALU.max), reads=[dt], writes=[dkm])


def negm_rows(cx, QT, dQT, P, Lq, km, dkm, sqr, f32r, out_row, dout):
    k = cx.k
    for c0 in range(0, Lq, 512):
        n = min(512, Lq - c0)
        pn, dpn = norm2_bcast(cx, QT, dQT, P, c0, n, sqr)
        t, dt = f32r.next()
        k.op("act", lambda e, t=t, pn=pn, n=n: e.activation(t[0:1, 0:n], pn[0:1, :], AF.Sqrt, scale=km[0:1, 0:1]), reads=[dpn, dkm], writes=[dt])
        k.op("dve", lambda e, t=t, c0=c0, n=n: e.tensor_scalar(out_row[0:1, c0:c0 + n], t[0:1, 0:n], -1.0, None, ALU.mult), reads=[dt], writes=[dout])


def emit_mla_attn(cx, S, A):
    k = cx.k
    NB = S // 128
    with ExitStack() as ps:
        cx.push(ps)
        QT = cx.sb([128, S], BF16); dQT = Dep()
        KT = cx.sb([128, S], BF16); dKT = Dep()
        V = cx.sb([128, NB, 128], BF16); dV = Dep()
        MK = cx.sb([128, 4, 512], BF16); dMK = Dep()
        k.dma("sp", QT[0:96, :], A["qT"], writes=[dQT])
        k.dma("sp", KT[0:96, :], A["kT"], writes=[dKT])
        k.dma("sp", KT[96:97, :], A["onesrow"][0:1, 0:S], writes=[dKT])
        k.dma("sp", V[:], A["v"].rearrange("(nb p) d -> p nb d", p=128), writes=[dV])
        k.dma("sp", MK[:], A["masks"].rearrange("a p q -> p a q"), writes=[dMK])
        sqr = cx.rot([128, 512], BF16, 2)
        f32r = cx.rot([128, 512], F32, 3)
        km = cx.sb([128, 1], F32); dkm = Dep()
        negm = cx.sb([1, S], BF16); dnegm = Dep()
        kmax2(cx, KT, dKT, 96, S, sqr, f32r, km, dkm)
        negm_rows(cx, QT, dQT, 96, S, km, dkm, sqr, f32r, negm, dnegm)
        k.dma("sp", QT[96:97, :], negm[0:1, :], reads=[dnegm], writes=[dQT])
        ptr = cx.rot([128, 512], BF16, 5)
        recr = cx.rot([128, 512], F32, 2)
        outr = cx.rot([128, 512], BF16, 2)
        blocks = [(i, kb) for i in range(S // 512) for kb in range(4 * i + 4)]
        sb_i = 0
        pend = []

        def qk(i, kb, n):
            bank, dep = cx.banks[n % 4]
            out = bank[:, 0:512]
            k.op("pe", lambda e: e.matmul(out, KT[0:97, kb * 128:(kb + 1) * 128], QT[0:97, i * 512:(i + 1) * 512], start=True, stop=True),
                 reads=[dKT, dQT], writes=[dep])
            return out, dep

        def av(i, kb, sT, dsT):
            pt, dpt = ptr.next()
            k.op("act", lambda e: e.activation(pt[:], sT, AF.Exp), reads=[dsT], writes=[dpt])
            if kb >= 4 * i:
                a = kb - 4 * i
                k.op("pool", lambda e: e.tensor_tensor(pt[:], pt[:], MK[:, a, :], ALU.mult), reads=[dMK], writes=[dpt])
            ob, dob = cx.banks[4 + (i % 2)]
            db, ddb = cx.banks[6 + (i % 2)]
            first, last = (kb == 0), (kb == 4 * i + 3)
            k.op("pe", lambda e: e.matmul(ob[:, 0:512], V[:, kb, :], pt[:], start=first, stop=last), reads=[dV, dpt], writes=[dob], inc=False)
            k.op("pe", lambda e: e.matmul(db[:, 0:512], cx.ones[:], pt[:], start=first, stop=last), reads=[cx.dones, dpt], writes=[ddb])
            if last:
                rec, drec = recr.next()
                o, do = outr.next()
                k.op("dve", lambda e: e.reciprocal(rec[:], db[:, 0:512]), reads=[ddb], writes=[drec])
                k.op("dve", lambda e: e.tensor_tensor(o[:], ob[:, 0:512], rec[:], ALU.mult), reads=[dob, drec], writes=[do])
                k.dma("sp", A["oT"][:, i * 512:(i + 1) * 512], o[:], reads=[do])

        LOOK = 3
        for n, (i, kb) in enumerate(blocks):
            sT, dsT = qk(i, kb, n)
            pend.append((i, kb, sT, dsT))
            if len(pend) > LOOK:
                av(*pend.pop(0))
        while pend:
            av(*pend.pop(0))
        cx.pop()
    k.cut()


def emit_dil_attn(cx, T, A):
    k = cx.k
    HALO = 2048
    Lk = T + HALO
    NBK = Lk // 128
    with ExitStack() as ps:
        cx.push(ps)
        MK = cx.sb([128, 256], BF16); dMK = Dep()
        k.dma("sp", MK[:], A["dmask"], writes=[dMK])
        BK = cx.sb([2, Lk], BF16); dBK = Dep()
        k.dma("sp", BK[:], A["kbias2"], writes=[dBK])
        QB = cx.sb([2, T], BF16); dQB = Dep()
        KTr = cx.rot([128, Lk], BF16, 2)
        QTr = cx.rot([128, T], BF16, 2)
        Vr = cx.rot([128, 3, NBK, 128], BF16, 2)
        ACCr = cx.rot([128, 2, T], F32, 1)
        sqr = cx.rot([128, 512], BF16, 2)
        f32r = cx.rot([128, 512], F32, 3)
        kmr = cx.rot([128, 1], F32, 2)
        ptr = cx.rot([128, 256], BF16, 4)
        recr = cx.rot([128, T], F32, 1)
        outr = cx.rot([128, T], BF16, 2)
        def head(h):
            KT, dKT = KTr.next(); QT, dQT = QTr.next(); V, dV = Vr.next(); ACC, dACC = ACCr.next()
            k.dma("sp", KT[:], A["kaT"][h], writes=[dKT])
            k.dma("sp", QT[:], A["qaT"][h], writes=[dQT])
            for p in range(3):
                k.dma("sp", V[:, p, :, :], A["vreg"][p, h], writes=[dV])
            km, dkm = kmr.next()
            kmax2(cx, KT, dKT, 128, Lk, sqr, f32r, km, dkm)
            k.dma("sp", QB[1:2, :], A["onesrow"][0:1, 0:T], writes=[dQB])
            negm_rows(cx, QT, dQT, 128, T, km, dkm, sqr, f32r, QB, dQB)
            units = []
            for p, dil in enumerate((1, 4, 16)):
                nb = Lk // (128 * dil)
                nbh = HALO // (128 * dil)
                for r in range(dil):
                    for n in range(nbh, nb):
                        units.append((p, dil, nb, nbh, r, n))
            pend = []
            cnt = [0]

            def qk(u):
                p, dil, nb, nbh, r, n = u
                bank, dep = cx.banks[cnt[0] % 4]
                cnt[0] += 1
                q0 = (n - nbh) * 128 * dil + r
                qs = slice(q0, q0 + 127 * dil + 1, dil)
                for half, nn in enumerate((n - 1, n)):
                    k0 = nn * 128 * dil + r
                    ks = slice(k0, k0 + 127 * dil + 1, dil)
                    out = bank[:, half * 128:(half + 1) * 128]
                    k.op("pe", lambda e, out=out, ks=ks: e.matmul(out, KT[:, ks], QT[:, qs], start=True, stop=False), reads=[dKT, dQT], writes=[dep], inc=False)
                    k.op("pe", lambda e, out=out, ks=ks: e.matmul(out, BK[0:2, ks], QB[0:2, qs], start=False, stop=True), reads=[dBK, dQB], writes=[dep], inc=(half == 1))
                return (u, bank, dep, qs)

            def av(u, bank, dep, qs):
                p, dil, nb, nbh, r, n = u
                pt, dpt = ptr.next()
                k.op("act", lambda e: e.activation(pt[:], bank[:, 0:256], AF.Exp), reads=[dep], writes=[dpt])
                k.op("pool", lambda e: e.tensor_tensor(pt[:], pt[:], MK[:], ALU.mult), reads=[dMK], writes=[dpt])
                ob, dob = cx.banks[4 + (cnt[0] % 4)]
                for half, nn in enumerate((n - 1, n)):
                    k.op("pe", lambda e, half=half, nn=nn: e.matmul(ob[:, 0:128], V[:, p, r * nb + nn, :], pt[:, half * 128:(half + 1) * 128], start=(half == 0), stop=(half == 1)),
                         reads=[dV, dpt], writes=[dob], inc=False)
                for half in range(2):
                    k.op("pe", lambda e, half=half: e.matmul(ob[:, 128:256], cx.ones[:], pt[:, half * 128:(half + 1) * 128], start=(half == 0), stop=(half == 1)),
                         reads=[cx.dones, dpt], writes=[dob], inc=(half == 1))
                src3 = ob[:, 0:256].rearrange("p (a q) -> p a q", a=2)
                if p == 0:
                    k.op("dve", lambda e: e.tensor_copy(ACC[:, :, qs], src3), reads=[dob], writes=[dACC])
                else:
                    k.op("dve", lambda e: e.tensor_tensor(ACC[:, :, qs], src3, ACC[:, :, qs], ALU.add), reads=[dob], writes=[dACC])

            for u in units:
                pend.append(qk(u))
                if len(pend) > 2:
                    av(*pend.pop(0))
            while pend:
                av(*pend.pop(0))
            rec, drec = recr.next()
            o, do = outr.next()
            k.op("dve", lambda e, rec=rec, ACC=ACC: e.reciprocal(rec[:], ACC[:, 1, :]), reads=[dACC], writes=[drec])
            k.op("dve", lambda e, o=o, rec=rec, ACC=ACC: e.tensor_tensor(o[:], ACC[:, 0, :], rec[:], ALU.mult), reads=[dACC, drec], writes=[do])
            k.dma("sp", A["oaT"][h], o[:], reads=[do])

        for h in range(6):
            head(h)
        cx.pop()
    k.cut()


def outproj_tail(cx, T, t0, KC, W, dW, M, dM, hin, dhin, hout, dhout, xr_r, o_r, houtb=None, ob_r=None):
    k = cx.k
    for dc in range(16):
        ds_ = slice(dc * 128, (dc + 1) * 128)
        py, dpy = cx.chain(128, NT, [(W[:, c, ds_], M[:, c, :]) for c in range(KC)], dW + [dM])
        xr, dxr = xr_r.next()
        o, do = o_r.next()
        k.dma("sp", xr[:], hin[ds_, t0:t0 + NT], reads=[dhin], writes=[dxr])
        k.op("dve", lambda e, o=o, py=py, xr=xr: e.tensor_tensor(o[:], py, xr[:], ALU.add), reads=[dpy, dxr], writes=[do])
        k.dma("act", hout[ds_, t0:t0 + NT], o[:], reads=[do], writes=[dhout])
        if houtb is not None:
            ob16, dob16 = ob_r.next()
            k.op("act", lambda e, ob16=ob16, o=o: e.activation(ob16[:], o[:], AF.Copy), reads=[do], writes=[dob16])
            k.dma("act", houtb[ds_, t0:t0 + NT], ob16[:], reads=[dob16], writes=[dhout])


def emit_outproj_even(cx, T, mT, w_out, hin, dhin, hout, dhout, houtb=None):
    k = cx.k
    KC = 10
    with ExitStack() as ps:
        cx.push(ps)
        W, dW = load_w(cx, w_out, KC, D)
        Mr = cx.rot([128, KC, NT], BF16, 2)
        xr_r = cx.rot([128, NT], F32, 4)
        o_r = cx.rot([128, NT], F32, 4)
        ob_r = cx.rot([128, NT], BF16, 4)
        mv = mT.rearrange("(c p) t -> p c t", p=128)
        for tt in range(T // NT):
            t0 = tt * NT
            M, dM = Mr.next()
            k.dma("sp", M[:], mv[:, :, t0:t0 + NT], writes=[dM])
            outproj_tail(cx, T, t0, KC, W, dW, M, dM, hin, dhin, hout, dhout, xr_r, o_r, houtb, ob_r)
        cx.pop()
    k.cut()


def emit_inproj_odd(cx, T, src, dsrc, A, srcb=None):
    k = cx.k
    with ExitStack() as ps:
        cx.push(ps)
        gcol, dg = load_small(cx, A["g1col"], [128, 16])
        Win, dWin = load_w(cx, A["w_in"], 16, 2048, gcol, dg)
        srcv = src.rearrange("(kc p) t -> p kc t", p=128)
        Xr = cx.rot([128, 16, NT], BF16, 2)
        SQr = cx.rot([128, 16, NT], BF16, 1)
        L = Long(cx, ["rstd", "tmp"])
        f32r = cx.rot([128, NT], F32, 4)
        obr = cx.rot([128, NT], BF16, 6)
        otr = cx.rot([128, 512], BF16, 3)
        colr = cx.rot([128, 1], F32, 8)
        SA = 128 ** -0.5
        for tt in range(T // NT):
            t0 = tt * NT
            X, dX = Xr.next()
            SQ, dSQ = SQr.next()
            rstd, drs = L["rstd"]; tmp, dtmp = L["tmp"]
            load_x_stats(cx, srcv, t0, NT, X, dX, SQ, dSQ, 16, rstd[:], drs, tmp[:], dtmp, D, dsrc, srcb)
            for c in range(12):
                cs = slice(c * 128, (c + 1) * 128)
                p1, dp1 = cx.chain(128, NT, [(Win[:, kc, cs], X[:, kc, :]) for kc in range(16)], dWin + [dX])
                if c < 4:
                    t1, dt1 = f32r.next()
                    k.op("dve", lambda e, t1=t1, p1=p1: e.tensor_tensor(t1[:], p1, rstd[:], ALU.mult), reads=[dp1, drs], writes=[dt1])
                    k.dma("act", A["uT"][cs, t0:t0 + NT], t1[:], reads=[dt1])
                else:
                    ob, dob = obr.next()
                    sc = SA if c < 8 else 1.0
                    k.op("dve", lambda e, ob=ob, p1=p1, sc=sc: e.scalar_tensor_tensor(ob[:], p1, sc, rstd[:], ALU.mult, ALU.mult), reads=[dp1, drs], writes=[dob])
                    dstT = A["qdT"] if c < 8 else A["kdT"]
                    k.dma("act", dstT[c % 4, :, t0:t0 + NT], ob[:], reads=[dob])
            rcols = rstd_cols(cx, rstd, drs, colr, NT // 128)
            for s in range(NT // 128):
                ts_ = slice(s * 128, (s + 1) * 128)
                rc, drc = rcols[s]
                pv, dpv = cx.chain(128, 512, [(X[:, kc, ts_], Win[:, kc, 1536:2048]) for kc in range(16)], dWin + [dX])
                ot, dot = otr.next()
                k.op("act", lambda e, ot=ot, pv=pv, rc=rc: e.activation(ot[:], pv, AF.Copy, scale=rc[:, 0:1]), reads=[dpv, drc], writes=[dot])
                k.dma("act", A["vd"][t0 + s * 128:t0 + (s + 1) * 128, :], ot[:], reads=[dot])
        cx.pop()
    k.cut()


def emit_outproj_odd(cx, T, A, hin, dhin, hout, dhout, houtb=None):
    k = cx.k
    KC = 8
    HL = 16
    with ExitStack() as ps:
        cx.push(ps)
        W, dW = load_w(cx, A["w_out"], KC, D)
        PW, dPW = load_w(cx, A["pool_w"], 1, 512)
        psc, dpsc = load_small(cx, A["pscol"], [128, 4])
        Mr = cx.rot([128, KC, NT], BF16, 2)
        Ur = cx.rot([128, 4, NT + HL], F32, 2)
        S1r = cx.rot([128, 4, NT + HL], F32, 1)
        S2r = cx.rot([128, 4, NT + HL], F32, 1)
        ICr = cx.rot([128, 4, NT], F32, 2)
        PBr = cx.rot([128, 4, NT], BF16, 2)
        xr_r = cx.rot([128, NT], F32, 4)
        o_r = cx.rot([128, NT], F32, 4)
        ob_r = cx.rot([128, NT], BF16, 4)
        uv = A["uTh"].rearrange("(g p) t -> p g t", p=128)
        ov = A["odT"].rearrange("(c p) t -> p c t", p=128)
        for tt in range(T // NT):
            t0 = tt * NT
            M, dM = Mr.next()
            U, dU = Ur.next(); S1, dS1 = S1r.next(); S2, dS2 = S2r.next(); IC, dIC = ICr.next(); PB, dPB = PBr.next()
            k.dma("sp", M[:, 4:8, :], ov[:, :, t0:t0 + NT], writes=[dM])
            k.dma("sp", U[:], uv[:, :, t0:t0 + NT + HL], writes=[dU])
            k.dma("sp", IC[:], A["invcnt"][:, :, t0:t0 + NT], writes=[dIC])
            W_ = NT + HL
            src_t, dsrc_t = U, dU
            cur, dcur = None, None
            bufs = [(S1, dS1), (S2, dS2)]
            sh = 1
            for lvl in range(4):
                dstt, ddst = bufs[lvl % 2]
                g0 = lvl
                a, da = (U, dU) if lvl == 0 else bufs[(lvl - 1) % 2]
                k.op("dve", lambda e, dstt=dstt, a=a, g0=g0, sh=sh: e.tensor_tensor(dstt[:, g0:4, sh:W_], a[:, g0:4, sh:W_], a[:, g0:4, 0:W_ - sh], ALU.add),
                     reads=[da], writes=[ddst])
                t1 = dstt
                k.op("pool", lambda e, t1=t1, lvl=lvl, IC=IC: e.tensor_tensor(t1[:, lvl, HL:W_], t1[:, lvl, HL:W_], IC[:, lvl, :], ALU.mult), reads=[dIC], writes=[ddst])
                k.op("pool", lambda e, t1=t1, lvl=lvl, PB=PB, U=U: e.tensor_tensor(PB[:, lvl, :], t1[:, lvl, HL:W_], U[:, lvl, HL:W_], ALU.subtract), reads=[ddst, dU], writes=[dPB])
                sh *= 2
            for g in range(4):
                pm, dpm = cx.chain(128, NT, [(PW[:, 0, g * 128:(g + 1) * 128], PB[:, g, :])], dPW + [dPB])
                k.op("act", lambda e, M=M, g=g, pm=pm: e.activation(M[:, g, :], pm, AF.Copy, scale=psc[:, g:g + 1]), reads=[dpm, dpsc], writes=[dM])
            outproj_tail(cx, T, t0, KC, W, dW, M, dM, hin, dhin, hout, dhout, xr_r, o_r, houtb, ob_r)
        cx.pop()
    k.cut()


def emit_final_norm(cx, T, hin, dhin, gcol_ap, out):
    k = cx.k
    with ExitStack() as ps:
        cx.push(ps)
        gcol, dg = load_small(cx, gcol_ap, [128, 16])
        hv = hin.rearrange("(kc p) t -> p kc t", p=128)
        ov = out.rearrange("(kc p) t -> p kc t", p=128)
        Xr = cx.rot([128, 16, NT], F32, 2)
        SQr = cx.rot([128, 16, NT], BF16, 1)
        Or = cx.rot([128, 16, NT], F32, 2)
        L = Long(cx, ["rstd", "tmp"])
        for tt in range(T // NT):
            t0 = tt * NT
            X, dX = Xr.next(); SQ, dSQ = SQr.next(); O, dO = Or.next()
            rstd, drs = L["rstd"]; tmp, dtmp = L["tmp"]
            k.dma("sp", X[:], hv[:, :, t0:t0 + NT], reads=[dhin], writes=[dX])
            k.op("pool", lambda e, SQ=SQ, X=X: e.tensor_tensor(SQ[:], X[:], X[:], ALU.mult), reads=[dX], writes=[dSQ])
            ss, dss = cx.chain(128, NT, [(cx.ones[:], SQ[:, kc, :]) for kc in range(16)], [dSQ, cx.dones])
            rstd_from_ss(cx, ss, dss, rstd[:], drs, D, tmp[:], dtmp)
            for kc in range(16):
                k.op("dve", lambda e, O=O, X=X, kc=kc: e.scalar_tensor_tensor(O[:, kc, :], X[:, kc, :], gcol[:, kc:kc + 1], rstd[:], ALU.mult, ALU.mult), reads=[dX, dg, drs], writes=[dO])
            k.dma("act", ov[:, :, t0:t0 + NT], O[:], reads=[dO])
        cx.pop()
    k.cut()


def emit_sb_attn(cx, S, A):
    k = cx.k
    NB = S // 128
    with ExitStack() as ps:
        cx.push(ps)
        QT = cx.sb([128, S], BF16); dQT = Dep()
        KT = cx.sb([128, S], BF16); dKT = Dep()
        V = cx.sb([128, NB, 128], BF16); dV = Dep()
        MK = cx.sb([128, 4, 512], BF16); dMK = Dep()
        TRI = cx.sb([128, 2, 128], BF16); dTRI = Dep()
        k.dma("sp", QT[:], A["qT"], writes=[dQT])
        k.dma("sp", KT[:], A["kT"], writes=[dKT])
        k.dma("sp", V[:], A["v"].rearrange("(nb p) d -> p nb d", p=128), writes=[dV])
        k.dma("sp", MK[:], A["masks"].rearrange("a p q -> p a q"), writes=[dMK])
        k.dma("sp", TRI[:], A["tri"].rearrange("a p q -> p a q"), writes=[dTRI])
        exr = cx.rot([128, 512], F32, 2)
        spr = cx.rot([128, 512], BF16, 5)
        zsr = cx.rot([128, 512], F32, 4)
        ebr = cx.rot([128, 512], F32, 2)
        ar = cx.rot([128, 512], BF16, 5)
        outr = cx.rot([128, 512], BF16, 2)
        blocks = [(i, kb) for i in range(S // 512) for kb in range(4 * i + 3, -1, -1)]
        N = len(blocks)
        st = [dict() for _ in range(N)]

        def sZ(n):
            i, kb = blocks[n]
            bank, dep = cx.banks[n % 3]
            z = bank[:, 0:512]
            k.op("pe", lambda e: e.matmul(z, KT[:, kb * 128:(kb + 1) * 128], QT[:, i * 512:(i + 1) * 512], start=True, stop=True), reads=[dKT, dQT], writes=[dep])
            st[n]["z"] = (z, dep)

        def sSP(n):
            i, kb = blocks[n]
            z, dz = st[n]["z"]
            ex, dex = exr.next(); sp, dsp = spr.next(); zs, dzs = zsr.next()
            k.op("dve", lambda e: e.tensor_copy(zs[:], z), reads=[dz], writes=[dzs])
            k.op("act", lambda e: e.activation(ex[:], zs[:], AF.Exp), reads=[dzs], writes=[dex])
            k.op("act", lambda e: e.activation(sp[:], ex[:], AF.Ln, bias=1.0), reads=[dex], writes=[dsp])
            if kb >= 4 * i:
                a = kb - 4 * i
                k.op("pool", lambda e: e.tensor_tensor(sp[:], sp[:], MK[:, a, :], ALU.mult), reads=[dMK], writes=[dsp])
            st[n]["sp"] = (sp, dsp); st[n]["zs"] = (zs, dzs)

        def sTI(n):
            i, kb = blocks[n]
            sp, dsp = st[n]["sp"]
            cb, dcb = cx.banks[3 + (i % 2)]
            k.op("pe", lambda e: e.matmul(cb[:, 0:512], TRI[:, 0, :], sp[:], start=(kb == 4 * i + 3), stop=False, skip_group_check=True), reads=[dTRI, dsp], writes=[dcb])

        def sE(n):
            i, kb = blocks[n]
            zs, dzs = st[n]["zs"]
            cb, dcb = cx.banks[3 + (i % 2)]
            eb, deb = ebr.next(); a_, da_ = ar.next()
            k.op("dve", lambda e: e.scalar_tensor_tensor(eb[:], cb[:, 0:512], -1.0, zs[:], ALU.mult, ALU.add), reads=[dcb, dzs], writes=[deb])
            k.op("act", lambda e: e.activation(a_[:], eb[:], AF.Exp), reads=[deb], writes=[da_])
            if kb >= 4 * i:
                a = kb - 4 * i
                k.op("pool", lambda e: e.tensor_tensor(a_[:], a_[:], MK[:, a, :], ALU.mult), reads=[dMK], writes=[da_])
            st[n]["a"] = (a_, da_)

        def sTR(n):
            i, kb = blocks[n]
            sp, dsp = st[n]["sp"]
            cb, dcb = cx.banks[3 + (i % 2)]
            k.op("pe", lambda e: e.matmul(cb[:, 0:512], TRI[:, 1, :], sp[:], start=False, stop=(kb == 0), skip_group_check=True), reads=[dTRI, dsp], writes=[dcb])

        def sAV(n):
            i, kb = blocks[n]
            a_, da_ = st[n]["a"]
            ob, dob = cx.banks[5 + (i % 2)]
            k.op("pe", lambda e: e.matmul(ob[:, 0:512], V[:, kb, :], a_[:], start=(kb == 4 * i + 3), stop=(kb == 0)), reads=[dV, da_], writes=[dob])
            if kb == 0:
                o, do = outr.next()
                k.op("dve", lambda e: e.tensor_copy(o[:], ob[:, 0:512]), reads=[dob], writes=[do])
                k.dma("sp", A["oT"][:, i * 512:(i + 1) * 512], o[:], reads=[do])
            st[n].clear()

        for t in range(N + 4):
            if t < N:
                sZ(t)
            if 0 <= t - 1 < N:
                sSP(t - 1)
            if 0 <= t - 3 < N:
                sTR(t - 3)
            if 0 <= t - 2 < N:
                sTI(t - 2)
                sE(t - 2)
            if 0 <= t - 4 < N:
                sAV(t - 4)
        cx.pop()
    k.cut()


import numpy as np
import ml_dtypes
BF = ml_dtypes.bfloat16

def mla_masks():
    kk = np.arange(128)[:, None]; q = np.arange(512)[None, :]
    return np.stack([((a * 128 + kk) <= q) for a in range(4)]).astype(BF)

def sb_masks():
    kk = np.arange(128)[:, None]; q = np.arange(512)[None, :]
    return np.stack([((a * 128 + kk) < q) for a in range(4)]).astype(BF)

def dil_mask():
    kk = np.arange(128)[:, None]; q = np.arange(128)[None, :]
    return np.concatenate([(kk >= q), (kk <= q)], axis=1).astype(BF)

def vreg_layout(v_h, Lk):
    out = []
    for dil in (1, 4, 16):
        nb = Lk // (128 * dil)
        t = v_h.reshape(6, nb, 128, dil, 128)
        t = t.transpose(0, 2, 3, 1, 4).reshape(6, 128, dil * nb, 128)
        out.append(t)
    return np.ascontiguousarray(np.stack(out))

def tri_mats():
    j = np.arange(128)[:, None]; s = np.arange(128)[None, :]
    return np.stack([(j >= s), (j < s)]).astype(BF)


def _colT(v, n):
    return np.ascontiguousarray(np.asarray(v, np.float32).reshape(n, 128).T)


def _inv_cols():
    invA = (10000.0 ** (-np.arange(0, 128, 2, dtype=np.float32) / 128)).astype(np.float32)
    invA = np.concatenate([invA, invA]).reshape(128, 1)
    inv16 = (10000.0 ** (-np.arange(0, 32, 2, dtype=np.float32) / 32)).astype(np.float32)
    invB = np.zeros((128, 1), np.float32)
    invB[64:80, 0] = inv16
    invB[80:96, 0] = inv16
    return invA, invB


def _ffn_ins(cx, tag):
    return (cx.din("g_" + tag, [128, 16], F32), cx.din("wg_" + tag, [D, DFF], F32), cx.din("wu_" + tag, [D, DFF], F32), cx.din("wd_" + tag, [DFF, D], F32))


def _ffn_vals(tag, g, wg, wu, wd):
    return {"g_" + tag: _colT(g, 16), "wg_" + tag: np.ascontiguousarray(wg), "wu_" + tag: np.ascontiguousarray(wu), "wd_" + tag: np.ascontiguousarray(wd)}


def build_L1(T):
    nc = bass.Bass("TRN2", target_bir_lowering=False)
    with ExitStack() as st:
        cx = Cx(nc, st)
        xT = cx.din("xT", [D, T], F32)
        f = _ffn_ins(cx, "a")
        h1o = cx.dout("h1T", [D, T], F32)
        h1T = cx.dscr("h1s", [D, T], F32)
        dh = Dep()
        h1b = cx.dscr("h1b", [D, T], BF16)
        emit_ffn(cx, T, xT, f[0], f[1], f[2], f[3], h1T, ddst=dh, dst2=h1o)
        A = {"g1col": cx.din("g1col", [128, 16], F32), "qncol": cx.din("qncol", [128, 3], F32), "kvncol": cx.din("kvncol", [128, 2], F32),
             "invA": cx.din("invA", [128, 1], F32), "invB": cx.din("invB", [128, 1], F32), "w_in": cx.din("w_in", [D, 2976], F32),
             "w_q_up": cx.din("w_q_up", [384, 384], F32), "w_kv_up": cx.din("w_kv_up", [256, 768], F32), "posrep": cx.din("posrep", [128, T], I32),
             "qaT": cx.dout("qaT", [6, 128, T], BF16), "kaT": cx.dout("kaT", [6, 128, T], BF16), "va": cx.dout("va", [T, 768], BF16),
             "qbT": cx.dout("qbT", [4, 96, T], BF16), "kbT": cx.dout("kbT", [4, 96, T], BF16), "vb": cx.dout("vb", [T, 512], BF16)}
        emit_inproj_even_A(cx, T, h1T, dh, A)
        emit_inproj_even_B(cx, T, h1T, dh, A)
        cx.k.finish()
    return nc


def build_L2(S, T):
    nc = bass.Bass("TRN2", target_bir_lowering=False)
    Lk = T + 2048
    with ExitStack() as st:
        cx = Cx(nc, st)
        A = {"qT": cx.din("qT", [96, S], BF16), "kT": cx.din("kT", [96, S], BF16), "v": cx.din("v", [S, 128], BF16),
             "onesrow": cx.din("onesrow", [1, S], BF16), "masks": cx.din("masks", [4, 128, 512], BF16), "oT": cx.dout("oT", [128, S], BF16)}
        emit_mla_attn(cx, S, A)
        B = {"qaT": cx.din("qaT", [6, 128, T], BF16), "kaT": cx.din("kaT", [6, 128, Lk], BF16), "vreg": cx.din("vreg", [3, 6, 128, Lk // 128, 128], BF16),
             "dmask": cx.din("dmask", [128, 256], BF16), "kbias2": cx.din("kbias2", [2, Lk], BF16), "onesrow": A["onesrow"], "oaT": cx.dout("oaT", [6, 128, T], BF16)}
        emit_dil_attn(cx, T, B)
        cx.k.finish()
    return nc


def build_L3(T):
    nc = bass.Bass("TRN2", target_bir_lowering=False)
    with ExitStack() as st:
        cx = Cx(nc, st)
        mT = cx.din("mT", [1280, T], BF16)
        w_out = cx.din("w_out", [1280, D], F32)
        h1T = cx.din("h1T", [D, T], F32)
        h2T = cx.dscr("h2T", [D, T], F32); d2 = Dep()
        h3T = cx.dscr("h3T", [D, T], F32); d3 = Dep()
        h4o = cx.dout("h4T", [D, T], F32)
        h4T = cx.dscr("h4s", [D, T], F32); d4 = Dep()
        h2b = cx.dscr("h2b", [D, T], BF16); h3b = cx.dscr("h3b", [D, T], BF16); h4b = cx.dscr("h4b", [D, T], BF16)
        emit_outproj_even(cx, T, mT, w_out, h1T, Dep(), h2T, d2)
        f = _ffn_ins(cx, "b")
        emit_ffn(cx, T, h2T, f[0], f[1], f[2], f[3], h3T, dsrc=d2, ddst=d3)
        f = _ffn_ins(cx, "c")
        emit_ffn(cx, T, h3T, f[0], f[1], f[2], f[3], h4T, dsrc=d3, ddst=d4, dst2=h4o)
        A = {"g1col": cx.din("g1col", [128, 16], F32), "w_in": cx.din("w_in", [D, 2048], F32),
             "uT": cx.dout("uT", [512, T], F32), "qdT": cx.dout("qdT", [4, 128, T], BF16), "kdT": cx.dout("kdT", [4, 128, T], BF16), "vd": cx.dout("vd", [T, 512], BF16)}
        emit_inproj_odd(cx, T, h4T, d4, A)
        cx.k.finish()
    return nc


def build_L4(S):
    nc = bass.Bass("TRN2", target_bir_lowering=False)
    with ExitStack() as st:
        cx = Cx(nc, st)
        A = {"qT": cx.din("qT", [128, S], BF16), "kT": cx.din("kT", [128, S], BF16), "v": cx.din("v", [S, 128], BF16),
             "masks": cx.din("masks", [4, 128, 512], BF16), "tri": cx.din("tri", [2, 128, 128], BF16), "oT": cx.dout("oT", [128, S], BF16)}
        emit_sb_attn(cx, S, A)
        cx.k.finish()
    return nc


def build_L5(T):
    nc = bass.Bass("TRN2", target_bir_lowering=False)
    with ExitStack() as st:
        cx = Cx(nc, st)
        A = {"w_out": cx.din("w_out", [1024, D], F32), "pool_w": cx.din("pool_w", [128, 512], F32), "pscol": cx.din("pscol", [128, 4], F32),
             "uTh": cx.din("uTh", [512, T + 16], F32), "odT": cx.din("odT", [512, T], BF16), "invcnt": cx.din("invcnt", [128, 4, T], F32)}
        h4T = cx.din("h4T", [D, T], F32)
        h5T = cx.dscr("h5T", [D, T], F32); d5 = Dep()
        h6T = cx.dscr("h6T", [D, T], F32); d6 = Dep()
        outT = cx.dout("outT", [D, T], F32)
        h5b = cx.dscr("h5b", [D, T], BF16)
        emit_outproj_odd(cx, T, A, h4T, Dep(), h5T, d5)
        f = _ffn_ins(cx, "d")
        emit_ffn(cx, T, h5T, f[0], f[1], f[2], f[3], h6T, dsrc=d5, ddst=d6)
        emit_final_norm(cx, T, h6T, d6, cx.din("gfin", [128, 16], F32), outT)
        cx.k.finish()
    return nc


def _run(nc, ims):
    res = run_bass_kernel_spmd(nc, ims, core_ids=list(range(8)))
    return res.results


TH = 4096


def kernel(x, positions, norm_g, ffn_w_gate, ffn_w_up, ffn_w_down, even_w_in, even_q_norm, even_w_q_up,
           even_kv_norm, even_w_kv_up, even_w_out, odd_w_in, odd_pool_w, odd_pool_scale, odd_w_out, final_norm):
    x = np.asarray(x)
    Bn, S, _ = x.shape
    T = S // 4
    Th = min(TH, T)
    NS = S // Th
    shards = [(b, jj) for b in range(Bn) for jj in range(NS)]
    rounds = [shards[i:i + 8] for i in range(0, len(shards), 8)]
    Lk = T + 2048
    positions = np.asarray(positions)
    norm_g = np.asarray(norm_g); wg = np.asarray(ffn_w_gate); wu = np.asarray(ffn_w_up); wd = np.asarray(ffn_w_down)
    invA, invB = _inv_cols()
    cores = [(c // 4, c % 4) for c in range(8)]

    def tok(jj):
        return slice(jj * Th, (jj + 1) * Th)

    nc1 = build_L1(Th)
    H1T = np.empty((Bn, D, S), np.float32)
    QAT = np.empty((Bn, 6, 128, S), BF); KAT = np.empty((Bn, 6, 128, S), BF); VA = np.empty((Bn, S, 768), BF)
    QBT = np.empty((Bn, 4, 96, S), BF); KBT = np.empty((Bn, 4, 96, S), BF); VB = np.empty((Bn, S, 512), BF)
    common1 = {}
    common1.update(_ffn_vals("a", norm_g[0, 0], wg[0, 0], wu[0, 0], wd[0, 0]))
    common1.update({"g1col": _colT(norm_g[0, 1], 16), "qncol": _colT(np.asarray(even_q_norm)[0], 3), "kvncol": _colT(np.asarray(even_kv_norm)[0], 2),
                    "invA": invA, "invB": invB, "w_in": np.ascontiguousarray(np.asarray(even_w_in)[0]), "w_q_up": np.ascontiguousarray(np.asarray(even_w_q_up)[0]),
                    "w_kv_up": np.ascontiguousarray(np.asarray(even_w_kv_up)[0])})
    for rd in rounds:
        ims = []
        for b, jj in rd:
            im = dict(common1)
            im["xT"] = np.ascontiguousarray(x[b, tok(jj)].T)
            im["posrep"] = np.ascontiguousarray(np.broadcast_to(positions[b, tok(jj)][None, :], (128, Th))).astype(np.int32)
            ims.append(im)
        r = _run(nc1, ims)
        for (b, jj), rr in zip(rd, r):
            H1T[b][:, tok(jj)] = np.asarray(rr["h1T"])
            QAT[b][:, :, tok(jj)] = np.asarray(rr["qaT"]); KAT[b][:, :, tok(jj)] = np.asarray(rr["kaT"]); VA[b][tok(jj)] = np.asarray(rr["va"])
            QBT[b][:, :, tok(jj)] = np.asarray(rr["qbT"]); KBT[b][:, :, tok(jj)] = np.asarray(rr["kbT"]); VB[b][tok(jj)] = np.asarray(rr["vb"])
        del r
    ones_row = np.ones((1, S), BF)
    mm = mla_masks(); dm = dil_mask()
    ims = []
    for c, (b, j) in enumerate(cores):
        h = j
        q0 = j * T
        lo = q0 - 2048
        a = max(lo, 0)
        kaT = np.zeros((6, 128, Lk), BF)
        kaT[:, :, a - lo:] = KAT[b][:, :, a:q0 + T]
        vh = np.zeros((6, Lk, 128), BF)
        vh[:, a - lo:] = VA[b][a:q0 + T].reshape(-1, 6, 128).transpose(1, 0, 2)
        kb2 = np.zeros((2, Lk), np.float32); kb2[0] = 1.0; kb2[1, :a - lo] = -30000.0
        ims.append({"qT": np.ascontiguousarray(QBT[b][h]), "kT": np.ascontiguousarray(KBT[b][h]),
                    "v": np.ascontiguousarray(VB[b][:, h * 128:(h + 1) * 128]), "onesrow": ones_row, "masks": mm,
                    "qaT": np.ascontiguousarray(QAT[b][:, :, q0:q0 + T]), "kaT": kaT, "vreg": vreg_layout(vh, Lk), "dmask": dm, "kbias2": kb2.astype(BF)})
    r2 = _run(build_L2(S, T), ims)
    MT = np.empty((Bn, 1280, S), BF)
    for c, (b, j) in enumerate(cores):
        MT[b][0:768, j * T:(j + 1) * T] = np.asarray(r2[c]["oaT"]).reshape(768, T)
        MT[b][768 + j * 128:768 + (j + 1) * 128, :] = np.asarray(r2[c]["oT"])
    del r2, ims, QAT, KAT, VA, QBT, KBT, VB
    nc3 = build_L3(Th)
    H4T = np.empty((Bn, D, S), np.float32); UT = np.empty((Bn, 512, S), np.float32)
    QDT = np.empty((Bn, 4, 128, S), BF); KDT = np.empty((Bn, 4, 128, S), BF); VD = np.empty((Bn, S, 512), BF)
    common3 = {"w_out": np.ascontiguousarray(np.asarray(even_w_out)[0])}
    common3.update(_ffn_vals("b", norm_g[0, 2], wg[0, 1], wu[0, 1], wd[0, 1]))
    common3.update(_ffn_vals("c", norm_g[1, 0], wg[1, 0], wu[1, 0], wd[1, 0]))
    common3.update({"g1col": _colT(norm_g[1, 1], 16), "w_in": np.ascontiguousarray(np.asarray(odd_w_in)[0])})
    for rd in rounds:
        ims = []
        for b, jj in rd:
            im = dict(common3)
            im["mT"] = np.ascontiguousarray(MT[b][:, tok(jj)])
            im["h1T"] = np.ascontiguousarray(H1T[b][:, tok(jj)])
            ims.append(im)
        r = _run(nc3, ims)
        for (b, jj), rr in zip(rd, r):
            H4T[b][:, tok(jj)] = np.asarray(rr["h4T"]); UT[b][:, tok(jj)] = np.asarray(rr["uT"])
            QDT[b][:, :, tok(jj)] = np.asarray(rr["qdT"]); KDT[b][:, :, tok(jj)] = np.asarray(rr["kdT"]); VD[b][tok(jj)] = np.asarray(rr["vd"])
        del r
    del H1T, MT
    sm = sb_masks(); tm = tri_mats()
    ims = []
    for c, (b, j) in enumerate(cores):
        h = j
        ims.append({"qT": np.ascontiguousarray(QDT[b][h]), "kT": np.ascontiguousarray(KDT[b][h]),
                    "v": np.ascontiguousarray(VD[b][:, h * 128:(h + 1) * 128]), "masks": sm, "tri": tm})
    r4 = _run(build_L4(S), ims)
    ODT = np.empty((Bn, 512, S), BF)
    for c, (b, j) in enumerate(cores):
        ODT[b][j * 128:(j + 1) * 128, :] = np.asarray(r4[c]["oT"])
    del r4, ims
    nc5 = build_L5(Th)
    pw = np.ascontiguousarray(np.asarray(odd_pool_w)[0].transpose(1, 0, 2).reshape(128, 512))
    psc = _colT(np.asarray(odd_pool_scale)[0], 4)
    common5 = {"w_out": np.ascontiguousarray(np.asarray(odd_w_out)[0]), "pool_w": pw, "pscol": psc, "gfin": _colT(final_norm, 16)}
    common5.update(_ffn_vals("d", norm_g[1, 2], wg[1, 1], wu[1, 1], wd[1, 1]))
    out = np.empty((Bn, S, D), np.float32)
    for rd in rounds:
        ims = []
        for b, jj in rd:
            im = dict(common5)
            uTh = np.zeros((512, Th + 16), np.float32)
            uTh[:, 16:] = UT[b][:, tok(jj)]
            if jj > 0:
                uTh[:, :16] = UT[b][:, jj * Th - 16:jj * Th]
            tg = np.arange(jj * Th, (jj + 1) * Th)
            ic = np.stack([1.0 / np.minimum(tg + 1, w) for w in (2, 4, 8, 16)]).astype(np.float32)
            im.update({"uTh": uTh, "odT": np.ascontiguousarray(ODT[b][:, tok(jj)]),
                       "invcnt": np.ascontiguousarray(np.broadcast_to(ic[None], (128, 4, Th))), "h4T": np.ascontiguousarray(H4T[b][:, tok(jj)])})
            ims.append(im)
        r = _run(nc5, ims)
        for (b, jj), rr in zip(rd, r):
            out[b, tok(jj)] = np.asarray(rr["outT"]).T
        del r
    return out
```
